# Optimizing a Trainium2 kernel written in Bass

```python
import math
import jax, jax.numpy as jnp
from jax import lax
import numpy as np

D_MODEL = 1024
BATCH = 8
SEQ = 8192
DEPTH = 2

ALPHA = (2 * DEPTH) ** 0.25
BETA = (8 * DEPTH) ** -0.25
LN_EPS = 1e-5
N_ADA = 9

D_FF = 2816

RW_DIM = D_MODEL // 2
RW_HEAD = 64
RW_HEADS = RW_DIM // RW_HEAD
R_DECAY = 64
R_AAA = 64
R_GATE = 128
RW_GN_EPS = 64e-5
A_COLS = 3 * RW_DIM + 2 * R_DECAY + 2 * R_AAA + R_GATE

HY_DIM = D_MODEL // 2
HY_SHORT = 3
POS_BANDS = 16
POS_DIM = 1 + 2 * POS_BANDS
F_HID = 64
HY_TARGET = 1e-2
HY_FAST = 0.3
HY_SLOW = 1.5
B_COLS = 3 * HY_DIM

SSD_DIM = D_MODEL // 2
SSD_HEAD = 64
SSD_HEADS = SSD_DIM // SSD_HEAD
SSD_GROUPS = 2
SSD_HPG = SSD_HEADS // SSD_GROUPS
SSD_STATE = 128
SSD_CONV = 5
SSD_CHUNK = 128
SSD_XBC = SSD_DIM + 2 * SSD_GROUPS * SSD_STATE
C_COLS = SSD_DIM + SSD_XBC + 2 * SSD_HEADS

N_BRANCH = 3
G_COLS = N_BRANCH * D_MODEL
IN_COLS = A_COLS + B_COLS + C_COLS + G_COLS

kernel_name = 'hybrid_rwkv7_hyena_ssd_encoder'


def layer_norm(x, w, b):
    xf = x.astype(jnp.float32)
    mu = jnp.mean(xf, -1, keepdims=True)
    var = jnp.mean(jnp.square(xf - mu), -1, keepdims=True)
    return ((xf - mu) * lax.rsqrt(var + LN_EPS) * w + b).astype(x.dtype)


def modulate(x, shift, scale):
    return x * (1.0 + scale) + shift


def swiglu(h, wi, wo):
    gu = h @ wi
    return (jax.nn.silu(gu[..., :D_FF]) * gu[..., D_FF:]) @ wo


def dwconv_centred(u, w, b):
    K, C = w.shape
    p = K // 2
    out = lax.conv_general_dilated(u, w.astype(u.dtype)[:, None, :], window_strides=(1,), padding=[(p, p)],
                                   dimension_numbers=('NWC', 'WIO', 'NWC'), feature_group_count=C)
    return out + b


def token_shift(p, mu):
    prev = jnp.pad(p[:, :-1], ((0, 0), (1, 0), (0, 0)))
    nxt = jnp.pad(p[:, 1:], ((0, 0), (0, 1), (0, 0)))
    return p + mu[0] * (prev - p) + mu[1] * (nxt - p)


def rwkv7_step(S, inp):
    r_t, w_t, k_t, v_t, a_t, b_t = inp
    sa = jnp.einsum('dbhij,dbhj->dbhi', S, a_t)
    S = S * w_t[..., None, :] + sa[..., None] * b_t[..., None, :] + v_t[..., None] * k_t[..., None, :]
    return S, jnp.einsum('dbhij,dbhj->dbhi', S, r_t)


def rwkv7_mixer(pa, mu, w0, w2, a0, a2, g2, k_k, k_a, r_k, lnx_w, lnx_b):
    f32 = jnp.float32
    bsz, L, _ = pa.shape
    pa = token_shift(pa, mu)
    r = pa[..., :RW_DIM].astype(f32)
    k = pa[..., RW_DIM:2 * RW_DIM].astype(f32)
    v = pa[..., 2 * RW_DIM:3 * RW_DIM].astype(f32)
    o = 3 * RW_DIM
    lw = pa[..., o:o + 2 * R_DECAY].reshape(bsz, L, 2, R_DECAY)
    o = o + 2 * R_DECAY
    la = pa[..., o:o + 2 * R_AAA].reshape(bsz, L, 2, R_AAA)
    lg = pa[..., o + 2 * R_AAA:]
    w_raw = (w0 + jnp.einsum('bldr,drc->bldc', jnp.tanh(lw), w2)).astype(f32)
    decay = jnp.exp(-jnp.exp(-jax.nn.softplus(-w_raw) - 0.5))
    a = jax.nn.sigmoid((a0 + jnp.einsum('bldr,drc->bldc', la, a2)).astype(f32))
    g = jax.nn.sigmoid(lg) @ g2
    heads = lambda u: u.reshape(u.shape[:-1] + (RW_HEADS, RW_HEAD))
    kk = heads(k * k_k)
    kk = kk / jnp.maximum(jnp.linalg.norm(kk, axis=-1, keepdims=True), 1e-12)
    k_dir = k[:, :, None] * (1.0 + (a - 1.0) * k_a)
    b_dir = kk[:, :, None] * heads(a)
    both = lambda u: jnp.broadcast_to(u[:, :, None], (bsz, L, 2, RW_HEADS, RW_HEAD))

    def dir_major(u):
        u = jnp.stack([u[:, :, 0], jnp.flip(u[:, :, 1], axis=1)], axis=0)
        return jnp.transpose(u, (2, 0, 1, 3, 4))

    xs = (dir_major(both(heads(r))), dir_major(heads(decay)), dir_major(heads(k_dir)),
          dir_major(both(heads(v))), dir_major(both(-kk)), dir_major(b_dir))
    S0 = jnp.zeros((2, bsz, RW_HEADS, RW_HEAD, RW_HEAD), f32)
    _, ys = lax.scan(rwkv7_step, S0, xs)
    y = jnp.transpose(ys[:, 0] + jnp.flip(ys[:, 1], axis=0), (1, 0, 2, 3))
    mu_y = jnp.mean(y, -1, keepdims=True)
    var_y = jnp.mean(jnp.square(y - mu_y), -1, keepdims=True)
    y = ((y - mu_y) * lax.rsqrt(var_y + RW_GN_EPS)).reshape(bsz, L, RW_DIM) * lnx_w + lnx_b
    k_sum = heads(k_dir[:, :, 0] + k_dir[:, :, 1])
    bonus = (jnp.sum(heads(r) * k_sum * r_k, -1, keepdims=True) * heads(v)).reshape(bsz, L, RW_DIM)
    return ((y + bonus) * g).astype(pa.dtype)


def hyena_positions(L):
    t = jnp.linspace(0.0, 1.0, L, dtype=jnp.float32)
    w = 2.0 * math.pi * jnp.arange(L, dtype=jnp.float32) / L
    f = jnp.linspace(1e-4, POS_BANDS - 1, POS_BANDS, dtype=jnp.float32)
    feats = jnp.concatenate([t[:, None], jnp.cos(w[:, None] * f), jnp.sin(w[:, None] * f)], -1)
    return feats, jnp.abs(2.0 * t - 1.0)


def hyena_filter(feats, dist, f_w1, f_b1, f_w2, f_b2, f_w3, freq):
    f32 = jnp.float32
    hdn = jnp.sin(freq[0] * (feats @ f_w1 + f_b1))
    hdn = jnp.sin(freq[1] * (hdn @ f_w2 + f_b2))
    h = (hdn @ f_w3).astype(f32)
    deltas = jnp.abs(jnp.linspace(math.log(HY_TARGET) / HY_SLOW, math.log(HY_TARGET) / HY_FAST, HY_DIM, dtype=f32))
    h = h * jnp.exp(-dist[:, None] * deltas)
    return h * lax.rsqrt(jnp.sum(jnp.square(h), 0, keepdims=True) + 1e-12)


def fftconv_centred(u, h):
    L = u.shape[1]
    n = 2 * L
    U = jnp.fft.rfft(u, n=n, axis=1)
    H = jnp.fft.rfft(h, n=n, axis=0)
    y = jnp.fft.irfft(U * H[None], n=n, axis=1)
    return y[:, L // 2:L // 2 + L]


def hyena_mixer(pb, feats, dist, conv_w, conv_b, f_w1, f_b1, f_w2, f_b2, f_w3, freq, h_bias):
    u = dwconv_centred(pb, conv_w, conv_b)
    x0 = u[..., :HY_DIM]
    x1 = u[..., HY_DIM:2 * HY_DIM]
    v = u[..., 2 * HY_DIM:]
    h = hyena_filter(feats, dist, f_w1, f_b1, f_w2, f_b2, f_w3, freq)
    z = (x1 * v).astype(jnp.float32)
    y = fftconv_centred(z, h) + z * h_bias
    return (x0 * y).astype(pb.dtype)


def segsum_exp(a):
    T = a.shape[-1]
    cs = jnp.cumsum(a, -1)
    mask = jnp.tril(jnp.ones((T, T), dtype=bool))
    diff = jnp.where(mask, cs[..., :, None] - cs[..., None, :], 0.0)
    return jnp.where(mask, jnp.exp(diff), 0.0)


def ssd_chunked(x, dt, A, Bm, Cm):
    bsz, L = x.shape[:2]
    nc = L // SSD_CHUNK
    xd = (x * dt[..., None]).reshape(bsz, nc, SSD_CHUNK, SSD_GROUPS, SSD_HPG, SSD_HEAD)
    a = jnp.transpose((dt * A).reshape(bsz, nc, SSD_CHUNK, SSD_GROUPS, SSD_HPG), (0, 3, 4, 1, 2))
    a_cs = jnp.cumsum(a, -1)
    Bc = Bm.reshape(bsz, nc, SSD_CHUNK, SSD_GROUPS, SSD_STATE)
    Cc = Cm.reshape(bsz, nc, SSD_CHUNK, SSD_GROUPS, SSD_STATE)
    cb = jnp.einsum('bclgn,bcsgn->bcgls', Cc, Bc)
    y_diag = jnp.einsum('bcgls,bgecls,bcsgep->bclgep', cb, segsum_exp(a), xd)
    decay_states = jnp.exp(a_cs[..., -1:] - a_cs)
    states = jnp.einsum('bclgn,bgecl,bclgep->bcgepn', Bc, decay_states, xd)
    chunk_a = jnp.pad(a_cs[..., -1], ((0, 0), (0, 0), (0, 0), (1, 0)))
    states = jnp.concatenate([jnp.zeros_like(states[:, :1]), states], axis=1)
    states_in = jnp.einsum('bgezc,bcgepn->bzgepn', segsum_exp(chunk_a), states)[:, :-1]
    y_off = jnp.einsum('bclgn,bcgepn,bgecl->bclgep', Cc, states_in, jnp.exp(a_cs))
    return (y_diag + y_off).reshape(bsz, L, SSD_GROUPS, SSD_HPG, SSD_HEAD)


def ssd_mixer(pc, conv_w, conv_b, dt_bias, A_log, d_skip, norm_w):
    f32 = jnp.float32
    bsz, L, _ = pc.shape
    z = pc[..., :SSD_DIM].astype(f32)
    xbc = jax.nn.silu(dwconv_centred(pc[..., SSD_DIM:SSD_DIM + SSD_XBC], conv_w, conv_b)).astype(f32)
    dt_raw = pc[..., SSD_DIM + SSD_XBC:].astype(f32).reshape(bsz, L, 2, SSD_GROUPS, SSD_HPG)
    xs = xbc[..., :SSD_DIM].reshape(bsz, L, SSD_GROUPS, SSD_HPG, SSD_HEAD)
    Bm = xbc[..., SSD_DIM:SSD_DIM + SSD_GROUPS * SSD_STATE].reshape(bsz, L, SSD_GROUPS, SSD_STATE)
    Cm = xbc[..., SSD_DIM + SSD_GROUPS * SSD_STATE:].reshape(bsz, L, SSD_GROUPS, SSD_STATE)
    dt = jax.nn.softplus(dt_raw + dt_bias.astype(f32).reshape(2, SSD_GROUPS, SSD_HPG))
    A = -jnp.exp(A_log.astype(f32)).reshape(2, SSD_GROUPS, SSD_HPG)
    flip = lambda u: jnp.flip(u, axis=1)
    y = (ssd_chunked(xs, dt[:, :, 0], A[0], Bm, Cm)
         + flip(ssd_chunked(flip(xs), flip(dt[:, :, 1]), A[1], flip(Bm), flip(Cm)))
         + d_skip.reshape(SSD_GROUPS, SSD_HPG)[..., None] * xs)
    y = y.reshape(bsz, L, SSD_DIM) * jax.nn.silu(z)
    yg = y.reshape(bsz, L, SSD_GROUPS, SSD_DIM // SSD_GROUPS)
    yg = yg * lax.rsqrt(jnp.mean(jnp.square(yg), -1, keepdims=True) + 1e-5)
    return (yg.reshape(bsz, L, SSD_DIM) * norm_w).astype(pc.dtype)


def hybrid_mixer(h, feats, dist, w_in, w_out, pa_w, pb_w, pc_w,
                 rw_mu, rw_w0, rw_w2, rw_a0, rw_a2, rw_g2, rw_kk, rw_ka, rw_rk, rw_lnx_w, rw_lnx_b,
                 hy_conv_w, hy_conv_b, hy_f_w1, hy_f_b1, hy_f_w2, hy_f_b2, hy_f_w3, hy_freq, hy_bias,
                 ssd_conv_w, ssd_conv_b, ssd_dt_bias, ssd_A_log, ssd_D, ssd_norm_w):
    bsz, L, _ = h.shape
    proj = h @ w_in
    o1 = A_COLS
    o2 = o1 + B_COLS
    o3 = o2 + C_COLS
    ya = rwkv7_mixer(proj[..., :o1], rw_mu, rw_w0, rw_w2, rw_a0, rw_a2, rw_g2,
                     rw_kk, rw_ka, rw_rk, rw_lnx_w, rw_lnx_b) @ pa_w
    yb = hyena_mixer(proj[..., o1:o2], feats, dist, hy_conv_w, hy_conv_b, hy_f_w1, hy_f_b1,
                     hy_f_w2, hy_f_b2, hy_f_w3, hy_freq, hy_bias) @ pb_w
    yc = ssd_mixer(proj[..., o2:o3], ssd_conv_w, ssd_conv_b, ssd_dt_bias, ssd_A_log, ssd_D, ssd_norm_w) @ pc_w
    gates = jax.nn.sigmoid(proj[..., o3:]).reshape(bsz, L, N_BRANCH, D_MODEL)
    merged = gates[:, :, 0] * ya + gates[:, :, 1] * yb + gates[:, :, 2] * yc
    return merged @ w_out


def setup_inputs(seed: int = 0) -> dict:
    key = jax.random.key(seed)
    ks = iter(jax.random.split(key, 48))
    nrm = lambda shape, s: s * jax.random.normal(next(ks), shape, jnp.float32)
    uni = lambda shape, lo, hi: jax.random.uniform(next(ks), shape, jnp.float32, lo, hi)
    dt0 = jnp.exp(uni((DEPTH, 2, SSD_HEADS), math.log(1e-3), math.log(1e-1)))
    return {
        'x': nrm((BATCH, SEQ, D_MODEL), 1.0),
        'c': nrm((BATCH, D_MODEL), 1.0),
        'ada_w': nrm((DEPTH, D_MODEL, N_ADA * D_MODEL), 0.5 * D_MODEL ** -0.5),
        'ada_b': nrm((DEPTH, N_ADA * D_MODEL), 0.02),
        'ln_w': 1.0 + nrm((DEPTH, 3, D_MODEL), 0.02),
        'ln_b': nrm((DEPTH, 3, D_MODEL), 0.02),
        'ffn_wi': nrm((DEPTH, 2, D_MODEL, 2 * D_FF), D_MODEL ** -0.5),
        'ffn_wo': nrm((DEPTH, 2, D_FF, D_MODEL), BETA * D_FF ** -0.5),
        'mix_w_in': nrm((DEPTH, D_MODEL, IN_COLS), D_MODEL ** -0.5),
        'rw_mu': uni((DEPTH, 2, A_COLS), 0.0, 0.5),
        'rw_w0': uni((DEPTH, 2, RW_DIM), -6.0, -1.0),
        'rw_w2': nrm((DEPTH, 2, R_DECAY, RW_DIM), 0.1),
        'rw_a0': nrm((DEPTH, 2, RW_DIM), 0.5),
        'rw_a2': nrm((DEPTH, 2, R_AAA, RW_DIM), 0.5 * R_AAA ** -0.5),
        'rw_g2': nrm((DEPTH, R_GATE, RW_DIM), R_GATE ** -0.5),
        'rw_kk': 0.85 + nrm((DEPTH, RW_DIM), 0.02),
        'rw_ka': 1.0 + nrm((DEPTH, RW_DIM), 0.02),
        'rw_rk': nrm((DEPTH, RW_HEADS, RW_HEAD), 0.1),
        'rw_lnx_w': 1.0 + nrm((DEPTH, RW_DIM), 0.02),
        'rw_lnx_b': nrm((DEPTH, RW_DIM), 0.02),
        'hy_conv_w': nrm((DEPTH, HY_SHORT, B_COLS), HY_SHORT ** -0.5),
        'hy_conv_b': nrm((DEPTH, B_COLS), 0.02),
        'hy_f_w1': nrm((DEPTH, POS_DIM, F_HID), POS_DIM ** -0.5),
        'hy_f_b1': nrm((DEPTH, F_HID), 0.1),
        'hy_f_w2': nrm((DEPTH, F_HID, F_HID), F_HID ** -0.5),
        'hy_f_b2': nrm((DEPTH, F_HID), 0.1),
        'hy_f_w3': nrm((DEPTH, F_HID, HY_DIM), F_HID ** -0.5),
        'hy_freq': 1.0 + nrm((DEPTH, 2, F_HID), 0.02),
        'hy_bias': nrm((DEPTH, HY_DIM), 0.5),
        'ssd_conv_w': nrm((DEPTH, SSD_CONV, SSD_XBC), SSD_CONV ** -0.5),
        'ssd_conv_b': nrm((DEPTH, SSD_XBC), 0.02),
        'ssd_dt_bias': dt0 + jnp.log(-jnp.expm1(-dt0)),
        'ssd_A_log': jnp.log(uni((DEPTH, 2, SSD_HEADS), 1.0, 16.0)),
        'ssd_D': 1.0 + nrm((DEPTH, SSD_HEADS), 0.02),
        'ssd_norm_w': 1.0 + nrm((DEPTH, SSD_DIM), 0.02),
        'proj_a': nrm((DEPTH, RW_DIM, D_MODEL), RW_DIM ** -0.5),
        'proj_b': nrm((DEPTH, HY_DIM, D_MODEL), HY_DIM ** -0.5),
        'proj_c': nrm((DEPTH, SSD_DIM, D_MODEL), SSD_DIM ** -0.5),
        'mix_w_out': nrm((DEPTH, D_MODEL, D_MODEL), BETA * D_MODEL ** -0.5),
    }


def reference(x, c, ada_w, ada_b, ln_w, ln_b, ffn_wi, ffn_wo, mix_w_in,
              rw_mu, rw_w0, rw_w2, rw_a0, rw_a2, rw_g2, rw_kk, rw_ka, rw_rk, rw_lnx_w, rw_lnx_b,
              hy_conv_w, hy_conv_b, hy_f_w1, hy_f_b1, hy_f_w2, hy_f_b2, hy_f_w3, hy_freq, hy_bias,
              ssd_conv_w, ssd_conv_b, ssd_dt_bias, ssd_A_log, ssd_D, ssd_norm_w,
              proj_a, proj_b, proj_c, mix_w_out):
    bsz, L, _ = x.shape
    cond = jax.nn.silu(c)
    feats, dist = hyena_positions(L)
    for l in range(DEPTH):
        mod = (cond @ ada_w[l] + ada_b[l]).reshape(bsz, 1, N_ADA, D_MODEL)
        h = modulate(x, mod[:, :, 0], mod[:, :, 1])
        x = layer_norm(ALPHA * x + 0.5 * mod[:, :, 2] * swiglu(h, ffn_wi[l, 0], ffn_wo[l, 0]), ln_w[l, 0], ln_b[l, 0])
        h = modulate(x, mod[:, :, 3], mod[:, :, 4])
        y = hybrid_mixer(h, feats, dist, mix_w_in[l], mix_w_out[l], proj_a[l], proj_b[l], proj_c[l],
                         rw_mu[l], rw_w0[l], rw_w2[l], rw_a0[l], rw_a2[l], rw_g2[l], rw_kk[l], rw_ka[l],
                         rw_rk[l], rw_lnx_w[l], rw_lnx_b[l],
                         hy_conv_w[l], hy_conv_b[l], hy_f_w1[l], hy_f_b1[l], hy_f_w2[l], hy_f_b2[l],
                         hy_f_w3[l], hy_freq[l], hy_bias[l],
                         ssd_conv_w[l], ssd_conv_b[l], ssd_dt_bias[l], ssd_A_log[l], ssd_D[l], ssd_norm_w[l])
        x = layer_norm(ALPHA * x + mod[:, :, 5] * y, ln_w[l, 1], ln_b[l, 1])
        h = modulate(x, mod[:, :, 6], mod[:, :, 7])
        x = layer_norm(ALPHA * x + 0.5 * mod[:, :, 8] * swiglu(h, ffn_wi[l, 1], ffn_wo[l, 1]), ln_w[l, 2], ln_b[l, 2])
    return x
```

```python
import contextlib
import math
import numpy as np
import ml_dtypes
import concourse.bass as bass
import concourse.mybir as mybir
from concourse.bass_utils import run_bass_kernel_spmd

F32 = mybir.dt.float32
BF16 = mybir.dt.bfloat16
AF = mybir.ActivationFunctionType
ALU = mybir.AluOpType
AX = mybir.AxisListType

D = 1024
DEPTH = 2
ALPHA = (2 * DEPTH) ** 0.25
LN_EPS = 1e-5
D_FF = 2816
NFC = D_FF // 128

ENGS = ["pe", "act", "dve", "pool", "sp"]
NDSEM = 12


class Op:
    __slots__ = ("eng", "fn", "idx", "dma", "waits", "has_dep", "ms", "dsem", "dval", "prewait")

    def __init__(self, eng, fn, dma):
        self.eng = eng
        self.fn = fn
        self.dma = dma
        self.waits = []
        self.has_dep = False
        self.ms = None
        self.dsem = None
        self.dval = None
        self.prewait = None


class Sched:
    def __init__(self, nc, same_engine_sync=True):
        self.nc = nc
        self.ops = {e: [] for e in ENGS}
        self.last_w = {}
        self.readers = {}
        self.known = {e: {f: -1 for f in ENGS} for e in ENGS}
        self.known_dma = {e: set() for e in ENGS}
        self.ndma = {e: 0 for e in ENGS}
        self.dma_hist = {e: [] for e in ENGS}
        self.same = same_engine_sync
        self.all_dma = []
        self.pending_barrier = {e: None for e in ENGS}
        self.excl_last = {}
        self.pe_rg = {}

    def add(self, eng, fn, reads=(), writes=(), dma=False, excl=(), rowid=None, rgkeys=()):
        op = Op(eng, fn, dma)
        op.idx = len(self.ops[eng])
        deps = []
        for k in excl:
            d_ = self.excl_last.setdefault(k, {})
            for e2, o in d_.items():
                if e2 != eng:
                    deps.append((o, "excl"))
            d_[eng] = op
        if eng == "pe" and rowid is not None:
            for k in rgkeys:
                prev = self.pe_rg.get(k)
                if prev is not None and prev[0] != rowid:
                    o = prev[1]
                    if self.known["pe"]["pe"] < o.idx:
                        self.known["pe"]["pe"] = o.idx
                        o.has_dep = True
                        op.waits.append(o)
                self.pe_rg[k] = (rowid, op)
        for k in reads:
            w = self.last_w.get(k)
            if w is not None:
                deps.append((w, "raw"))
        for k in writes:
            w = self.last_w.get(k)
            if w is not None:
                deps.append((w, "waw"))
            r = self.readers.get(k)
            if r:
                for o in r[0].values():
                    deps.append((o, "war"))
                for o in r[1]:
                    deps.append((o, "war"))
        pb = self.pending_barrier[eng]
        if pb is not None:
            for o in pb:
                deps.append((o, "bar"))
            self.pending_barrier[eng] = None
        for d, kind in deps:
            self._need(op, d, kind)
        for k in reads:
            r = self.readers.setdefault(k, ({}, []))
            if dma:
                r[1].append(op)
            else:
                r[0][eng] = op
        for k in writes:
            self.last_w[k] = op
            self.readers[k] = ({}, [])
        if dma:
            n = self.ndma[eng]
            self.ndma[eng] = n + 1
            nd = 4 if eng == "pool" else NDSEM
            op.dsem = n % nd
            op.dval = 16 * (n // nd + 1)
            if n >= nd:
                prev = self.dma_hist[eng][n - nd]
                if prev not in self.known_dma[eng]:
                    op.prewait = prev
                    self.known_dma[eng].add(prev)
            self.dma_hist[eng].append(op)
            self.all_dma.append(op)
        self.ops[eng].append(op)
        return op

    def _need(self, op, d, kind):
        e = op.eng
        if d is op:
            return
        if d.dma:
            if d in self.known_dma[e]:
                return
            self.known_dma[e].add(d)
            op.waits.append(d)
            return
        if d.eng == e:
            if e == "pe" or kind in ("war", "waw") or e not in self.same:
                return
        if self.known[e][d.eng] >= d.idx:
            return
        self.known[e][d.eng] = d.idx
        d.has_dep = True
        op.waits.append(d)

    def barrier(self):
        lst = []
        for e in ENGS:
            for o in reversed(self.ops[e]):
                if not o.dma and o.fn is not None:
                    lst.append(o)
                    break
        lst.extend(self.all_dma)
        self.all_dma = []
        for e in ENGS:
            prev = self.pending_barrier[e]
            self.pending_barrier[e] = (list(prev) if prev else []) + lst
        self.last_w = {}
        self.readers = {}

    def emit(self):
        nc = self.nc
        self.barrier()
        self.add("sp", None)
        for e in ENGS:
            c = 0
            for o in self.ops[e]:
                if not o.dma and o.has_dep:
                    c += 1
                    o.ms = c
            print(f"[sched] {e}: {len(self.ops[e])} ops, {c} milestones, {self.ndma[e]} dmas", flush=True)
        with contextlib.ExitStack() as st:
            csem = {e: st.enter_context(nc.semaphore(f"c_{e}")) for e in ENGS}
            dsem = {e: [st.enter_context(nc.semaphore(f"d_{e}_{i}")) for i in range(NDSEM)]
                    for e in ENGS if self.ndma[e] > 0}
            block = st.enter_context(nc.Block())

            def run(e, eng):
                for o in self.ops[e]:
                    if o.prewait is not None:
                        p = o.prewait
                        eng.wait_ge(dsem[p.eng][p.dsem], p.dval)
                    for d in o.waits:
                        if d.dma:
                            eng.wait_ge(dsem[d.eng][d.dsem], d.dval)
                        else:
                            eng.wait_ge(csem[d.eng], d.ms)
                    if o.fn is None:
                        continue
                    ins = o.fn(eng)
                    if o.dma:
                        ins.then_inc(dsem[e][o.dsem], 16)
                    elif o.ms is not None:
                        ins.then_inc(csem[e], 1)

            @block.tensor
            def _(eng):
                run("pe", eng)

            @block.scalar
            def _(eng):
                run("act", eng)

            @block.vector
            def _(eng):
                run("dve", eng)

            @block.gpsimd
            def _(eng):
                run("pool", eng)

            @block.sync
            def _(eng):
                run("sp", eng)


class Ctx:
    def __init__(self, L):
        self.L = L
        self.nc = bass.Bass("TRN2", target_bir_lowering=False)
        import os
        self.S = Sched(self.nc, same_engine_sync=os.environ.get("SAME", "act,dve,pool").split(","))
        self.d = {}
        self.uid = 0

    def din(self, name, shape, dt=F32):
        self.d[name] = self.nc.dram_tensor(name, list(shape), dt, kind="ExternalInput").ap()
        return self.d[name]

    def dout(self, name, shape, dt=F32):
        self.d[name] = self.nc.dram_tensor(name, list(shape), dt, kind="ExternalOutput").ap()
        return self.d[name]

    def dscr(self, name, shape, dt=F32, debug=False):
        kind = "ExternalOutput" if debug else "Internal"
        self.d[name] = self.nc.dram_tensor(name, list(shape), dt, kind=kind).ap()
        return self.d[name]

    def sb(self, st, name, shape, dt):
        self.uid += 1
        return st.enter_context(self.nc.sbuf_tensor(f"{name}_{self.uid}", list(shape), dt))

    def ps(self, st, name, shape, dt=F32):
        self.uid += 1
        return st.enter_context(self.nc.psum_tensor(f"{name}_{self.uid}", list(shape), dt))

    @staticmethod
    def pk(*aps):
        keys = []
        for ap in aps:
            if ap is None or not hasattr(ap, "space") or str(ap.space) != "PSUM":
                continue
            pat = ap.ap
            row = pat[0][0]
            col0 = ap.offset % row
            ext = 0
            for stp, cnt in pat[1:]:
                ext += abs(stp) * (cnt - 1)
            es = 2 if ap.dtype == BF16 else 4
            b0 = (col0 * es) // 2048
            b1 = ((col0 + ext) * es + es - 1) // 2048
            for bk in range(b0, b1 + 1):
                keys.append(("PS", ap.tensor.name, bk))
        return keys

    @staticmethod
    def rowid(ap):
        pat = ap.ap
        row = pat[0][0]
        return (ap.offset // row, pat[0][1])

    @classmethod
    def pkq(cls, ap):
        pat = ap.ap
        row = pat[0][0]
        p0 = ap.offset // row
        p1 = p0 + pat[0][1] - 1
        return [(k, q) for k in cls.pk(ap) for q in range(p0 // 32, p1 // 32 + 1)]

    def mm(self, out, lhsT, rhs, start, stop, r, w, **kw):
        return self.S.add("pe", lambda e: e.matmul(out, lhsT, rhs, start=start, stop=stop, **kw), r, w,
                          excl=self.pk(out), rowid=self.rowid(lhsT), rgkeys=self.pkq(out))

    def tr(self, out, in_, ident, r, w):
        return self.S.add("pe", lambda e: e.transpose(out, in_, ident), r, w, excl=self.pk(out), rowid=self.rowid(in_), rgkeys=self.pkq(out))

    def act(self, out, in_, func, r, w, bias=None, scale=None, eng="act"):
        kw = {}
        if bias is not None:
            kw["bias"] = bias
        if scale is not None:
            kw["scale"] = scale
        return self.S.add(eng, lambda e: e.activation(out, in_, func, **kw), r, w, excl=self.pk(out, in_))

    def raw(self, eng, fn, r, w, aps=()):
        return self.S.add(eng, fn, r, w, excl=self.pk(*aps))

    def tt(self, eng, out, in0, in1, op, r, w):
        return self.S.add(eng, lambda e: e.tensor_tensor(out, in0, in1, op), r, w, excl=self.pk(out, in0, in1))

    def ts(self, eng, out, in0, s1, op0, r, w, s2=None, op1=None):
        if op1 is None:
            return self.S.add(eng, lambda e: e.tensor_scalar(out, in0, s1, None, op0), r, w, excl=self.pk(out, in0))
        return self.S.add(eng, lambda e: e.tensor_scalar(out, in0, s1, s2, op0, op1), r, w, excl=self.pk(out, in0))

    def stt(self, out, in0, scalar, in1, op0, op1, r, w):
        return self.S.add("dve", lambda e: e.scalar_tensor_tensor(out, in0, scalar, in1, op0, op1), r, w,
                          excl=self.pk(out, in0, in1))

    def cp(self, eng, out, in_, r, w):
        return self.S.add(eng, lambda e: e.tensor_copy(out, in_), r, w, excl=self.pk(out, in_))

    def memset(self, eng, ap, val, w):
        return self.S.add(eng, lambda e: e.memset(ap, val), (), w, excl=self.pk(ap))

    def recip(self, out, in_, r, w):
        return self.S.add("dve", lambda e: e.reciprocal(out, in_), r, w, excl=self.pk(out, in_))

    def dma(self, out, in_, r, w, q="sp", **kw):
        return self.S.add(q, lambda e: e.dma_start(out, in_, **kw), r, w, dma=True)


def stage_consts(C, st):
    nc = C.nc
    C.ident = C.sb(st, "ident", [128, 128], F32)
    C.identb = C.sb(st, "identb", [128, 128], BF16)
    C.ones = C.sb(st, "ones", [128, 128], F32)
    C.dma(C.ident[:], C.d["c_ident"][:, :], ["d:c_ident"], ["ident"])
    C.cp("dve", C.identb[:], C.ident[:], ["ident"], ["identb"])
    C.memset("dve", C.ones[:], 1.0, ["ones"])


def stage_transpose(C, src, dst, to_feature_major):
    L = C.L
    with contextlib.ExitStack() as st:
        ib = [C.sb(st, f"ti{i}", [128, 1024], F32) for i in range(2)]
        ob = [C.sb(st, f"to{i}", [128, 1024], F32) for i in range(2)]
        pp = [C.ps(st, f"tp{i}", [128, 512], F32) for i in range(4)]
        if to_feature_major:
            dstv = dst.rearrange("(c p) l -> p c l", p=128)
            for t in range(L // 128):
                b = t % 2
                C.dma(ib[b][:], src[t * 128:(t + 1) * 128, :], [("d", src.name, t)], [("ti", b)])
                for c in range(8):
                    p = pp[(c // 4 + 2 * b) % 4]
                    C.tr(p[:, (c % 4) * 128:(c % 4 + 1) * 128], ib[b][:, c * 128:(c + 1) * 128], C.ident[:],
                         [("ti", b), "ident"], [("tp", (c // 4 + 2 * b) % 4)])
                for h in range(2):
                    k = (h + 2 * b) % 4
                    C.cp("dve" if h == 0 else "act", ob[b][:, h * 512:(h + 1) * 512], pp[k][:], [("tp", k)], [("to", b)]) \
                        if h == 0 else C.act(ob[b][:, h * 512:(h + 1) * 512], pp[k][:], AF.Copy, [("tp", k)], [("to", b)])
                C.dma(dstv[:, :, t * 128:(t + 1) * 128], ob[b][:].rearrange("p (c l) -> p c l", c=8),
                      [("to", b)], [("d", dst.name, t)], q="pool")
        else:
            srcv = src.rearrange("(c p) l -> p c l", p=128)
            for t in range(L // 128):
                b = t % 2
                C.dma(ib[b][:].rearrange("p (c l) -> p c l", c=8), srcv[:, :, t * 128:(t + 1) * 128],
                      [("d", src.name, t)], [("ti", b)])
                for c in range(8):
                    k = (c // 4 + 2 * b) % 4
                    C.tr(pp[k][:, (c % 4) * 128:(c % 4 + 1) * 128], ib[b][:, c * 128:(c + 1) * 128], C.ident[:],
                         [("ti", b), "ident"], [("tp", k)])
                for h in range(2):
                    k = (h + 2 * b) % 4
                    if h == 0:
                        C.cp("dve", ob[b][:, h * 512:(h + 1) * 512], pp[k][:], [("tp", k)], [("to", b)])
                    else:
                        C.act(ob[b][:, h * 512:(h + 1) * 512], pp[k][:], AF.Copy, [("tp", k)], [("to", b)])
                C.dma(dst[t * 128:(t + 1) * 128, :], ob[b][:], [("to", b)], [("d", dst.name, t)], q="pool")
    C.S.barrier()


def stage_mod(C, l, st_persist):
    if not hasattr(C, "modT"):
        C.modT = [C.sb(st_persist, f"modT{i}", [128, 72], F32) for i in range(DEPTH)]
        C.mod1p = [C.sb(st_persist, f"mod1p{i}", [128, 72], F32) for i in range(DEPTH)]
        C.modhg = [C.sb(st_persist, f"modhg{i}", [128, 72], F32) for i in range(DEPTH)]
        C.lnw = C.sb(st_persist, "lnw", [128, DEPTH * 3 * 8], F32)
        C.lnb = C.sb(st_persist, "lnb", [128, DEPTH * 3 * 8], F32)
        C.dma(C.lnw[:], C.d["ln_w"][:, :], [], ["lnw"])
        C.dma(C.lnb[:], C.d["ln_b"][:, :], [], ["lnb"])
    with contextlib.ExitStack() as st:
        cond = C.sb(st, "cond", [128, 8], F32)
        adab = C.sb(st, "adab", [128, 72], F32)
        wb = [C.sb(st, f"adaw{i}", [128, 8, 1024], F32) for i in range(2)]
        pm = C.ps(st, "pmod", [128, 512], F32)
        C.dma(cond[:], C.d["c"][:, :], [], ["cond"])
        C.dma(adab[:], C.d["ada_b"][l], [], ["adab"])
        C.act(cond[:], cond[:], AF.Silu, ["cond"], ["cond"])
        wv = C.d["ada_w"][l].rearrange("(c p) n -> p c n", p=128)
        for blk in range(9):
            b = blk % 2
            C.dma(wb[b][:], wv[:, :, blk * 1024:(blk + 1) * 1024], [], [("adaw", b)], q="sp" if b == 0 else "pool")
            for j in range(8):
                col = blk * 8 + j
                for k in range(8):
                    C.mm(pm[:, col:col + 1], wb[b][:, k, j * 128:(j + 1) * 128], cond[:, k:k + 1], k == 0, k == 7,
                         [("adaw", b), "cond"], ["pmod"])
        C.tt("dve", C.modT[l][:], pm[:, 0:72], adab[:], ALU.add, ["pmod", "adab"], [("modT", l)])
        C.ts("dve", C.mod1p[l][:], C.modT[l][:], 1.0, ALU.add, [("modT", l)], [("mod1p", l)])
        C.ts("dve", C.modhg[l][:], C.modT[l][:], 0.5, ALU.mult, [("modT", l)], [("modhg", l)])
    C.S.barrier()


def load_w_bf16(C, st, dst_fn, src, nk, ncols, key, blk=1408):
    stg = [C.sb(st, f"stg{i}", [128, blk], F32) for i in range(3)]
    i = 0
    for k in range(nk):
        for c0 in range(0, ncols, blk):
            c1 = min(ncols, c0 + blk)
            b = i % 3
            C.dma(stg[b][:, 0:c1 - c0], src[k * 128:(k + 1) * 128, c0:c1], [], [("stg", b)],
                  q="sp" if i % 2 == 0 else "act")
            C.cp("pool", dst_fn(k, c0, c1), stg[b][:, 0:c1 - c0], [("stg", b)], [key])
            i += 1


def stage_ffn(C, l, which, src, dst, mi, lni):
    L = C.L
    TT = 256
    NT = L // TT
    with contextlib.ExitStack() as st:
        wi = C.sb(st, "wi", [128, 8, 2 * D_FF], BF16)
        wo = C.sb(st, "wo", [128, NFC, D], BF16)
        with contextlib.ExitStack() as st2:
            load_w_bf16(C, st2, lambda k, a, b: wi[:, k, a:b], C.d["ffn_wi"][l, which], 8, 2 * D_FF, "wi")
            load_w_bf16(C, st2, lambda k, a, b: wo[:, k, a:b], C.d["ffn_wo"][l, which], NFC, D, "wo", blk=1024)
            C.S.barrier()
        xt = [C.sb(st, f"xt{i}", [128, 8, TT], F32) for i in range(3)]
        hTb = [C.sb(st, f"hT{i}", [128, 8, TT], BF16) for i in range(2)]
        sg = [C.sb(st, f"sg{i}", [128, TT], F32) for i in range(4)]
        actT = C.sb(st, "actT", [128, NFC, TT], BF16)
        zsq = C.sb(st, "zsq", [128, 8, TT], F32)
        mean = C.sb(st, "mean", [128, TT], F32)
        msq = C.sb(st, "msq", [128, TT], F32)
        rstd = C.sb(st, "rstd", [128, TT], F32)
        pgu = [C.ps(st, f"pgu{i}", [128, 512], F32) for i in range(4)]
        py = [C.ps(st, f"py{i}", [128, 512], F32) for i in range(4)]
        srcv = src.rearrange("(c p) l -> p c l", p=128)
        dstv = dst.rearrange("(c p) l -> p c l", p=128)
        mhg = C.modhg[l]

        def load(t):
            C.dma(xt[t % 3][:], srcv[:, :, t * TT:(t + 1) * TT], [("d", src.name, t)], [("xt", t % 3)])

        def prep(t):
            x, kx = xt[t % 3], ("xt", t % 3)
            modulate_tile(C, hTb[t % 2], x, kx, l, mi, TT, hkey=("hT", t % 2))
            C.ts("pool", x[:], x[:], float(ALPHA), ALU.mult, [kx], [kx])

        def tail_pieces(t):
            x, kx = xt[t % 3], ("xt", t % 3)
            ps = layer_norm_pieces(C, x, kx, TT, zsq, mean, msq, rstd, py[0], ("py", 0), py[1], ("py", 1), l, lni)
            ps.append(lambda: C.dma(dstv[:, :, t * TT:(t + 1) * TT], x[:], [kx], [("d", dst.name, t)], q="pool"))
            return ps

        load(0)
        prep(0)
        pend = []
        for t in range(NT):
            x, kx = xt[t % 3], ("xt", t % 3)
            hT, kh = hTb[t % 2], ("hT", t % 2)
            if t + 1 < NT:
                load(t + 1)
            for f in range(NFC):
                fb = (t * NFC + f) % 4
                for k in range(8):
                    C.mm(pgu[fb][:, 0:TT], wi[:, k, f * 128:(f + 1) * 128], hT[:, k, :], k == 0, k == 7,
                         ["wi", kh], [("pgu", fb)])
                for k in range(8):
                    C.mm(pgu[fb][:, TT:2 * TT], wi[:, k, D_FF + f * 128:D_FF + (f + 1) * 128], hT[:, k, :], k == 0, k == 7,
                         ["wi", kh], [("pgu", fb)])
                C.act(sg[fb][:], pgu[fb][:, 0:TT], AF.Silu, [("pgu", fb)], [("sg", fb)])
                C.tt("dve", actT[:, f, :], sg[fb][:], pgu[fb][:, TT:2 * TT], ALU.mult, [("sg", fb), ("pgu", fb)], ["actT"])
                if f >= 1 and pend:
                    pend.pop(0)()
                if f == 16 and t + 1 < NT:
                    prep(t + 1)
            while pend:
                pend.pop(0)()
            for d in range(8):
                pyd = py[d // 2][:, (d % 2) * TT:(d % 2 + 1) * TT]
                for f in range(NFC):
                    C.mm(pyd, wo[:, f, d * 128:(d + 1) * 128], actT[:, f, :], f == 0, f == NFC - 1,
                         ["wo", "actT"], [("py", d // 2)])
                C.stt(x[:, d, :], pyd, mhg[:, 8 * (mi + 2) + d:8 * (mi + 2) + d + 1], x[:, d, :], ALU.mult, ALU.add,
                      [("py", d // 2), kx, ("modhg", l)], [kx])
            pend = tail_pieces(t)
        while pend:
            pend.pop(0)()
    C.S.barrier()


def layer_norm_pieces(C, x, kx, TT, zsq, mean, msq, rstd, ps1, k1, ps2, k2, l, lni):
    o = (l * 3 + lni) * 8
    P = []
    P.append(lambda: C.act(zsq[:], x[:], AF.Square, [kx], ["zsq"]))

    def stats():
        for d in range(8):
            C.mm(ps1[:, 0:TT], C.ones[:], x[:, d, :], d == 0, d == 7, ["ones", kx], [k1])
        for d in range(8):
            C.mm(ps2[:, 0:TT], C.ones[:], zsq[:, d, :], d == 0, d == 7, ["ones", "zsq"], [k2])
        C.ts("dve", mean[:], ps1[:, 0:TT], 1.0 / D, ALU.mult, [k1], ["mean"])
        C.tt("dve", msq[:], mean[:], mean[:], ALU.mult, ["mean"], ["msq"])
        C.stt(msq[:], ps2[:, 0:TT], 1.0 / D, msq[:], ALU.mult, ALU.subtract, [k2, "msq"], ["msq"])
        C.ts("dve", msq[:], msq[:], float(LN_EPS), ALU.add, ["msq"], ["msq"])
    P.append(stats)

    def rs():
        C.act(msq[:], msq[:], AF.Sqrt, ["msq"], ["msq"])
        C.recip(rstd[:], msq[:], ["msq"], ["rstd"])
    P.append(rs)
    for d in range(8):
        def nrm(d=d):
            C.tt("dve", x[:, d, :], x[:, d, :], mean[:], ALU.subtract, [kx, "mean"], [kx])
            C.tt("pool", x[:, d, :], x[:, d, :], rstd[:], ALU.mult, [kx, "rstd"], [kx])
            C.act(x[:, d, :], x[:, d, :], AF.Identity, [kx, "lnw", "lnb"], [kx],
                  bias=C.lnb[:, o + d:o + d + 1], scale=C.lnw[:, o + d:o + d + 1])
        P.append(nrm)
    return P


def layer_norm_fm(C, x, kx, TT, zsq, mean, msq, rstd, ps1, k1, ps2, k2, l, lni):
    C.act(zsq[:], x[:], AF.Square, [kx], ["zsq"])
    for d in range(8):
        C.mm(ps1[:, 0:TT], C.ones[:], x[:, d, :], d == 0, d == 7, ["ones", kx], [k1])
    for d in range(8):
        C.mm(ps2[:, 0:TT], C.ones[:], zsq[:, d, :], d == 0, d == 7, ["ones", "zsq"], [k2])
    C.ts("dve", mean[:], ps1[:, 0:TT], 1.0 / D, ALU.mult, [k1], ["mean"])
    C.tt("dve", msq[:], mean[:], mean[:], ALU.mult, ["mean"], ["msq"])
    C.stt(msq[:], ps2[:, 0:TT], 1.0 / D, msq[:], ALU.mult, ALU.subtract, [k2, "msq"], ["msq"])
    C.ts("dve", msq[:], msq[:], float(LN_EPS), ALU.add, ["msq"], ["msq"])
    C.act(msq[:], msq[:], AF.Sqrt, ["msq"], ["msq"])
    C.recip(rstd[:], msq[:], ["msq"], ["rstd"])
    o = (l * 3 + lni) * 8
    for d in range(8):
        C.tt("dve", x[:, d, :], x[:, d, :], mean[:], ALU.subtract, [kx, "mean"], [kx])
        C.tt("pool", x[:, d, :], x[:, d, :], rstd[:], ALU.mult, [kx, "rstd"], [kx])
        C.act(x[:, d, :], x[:, d, :], AF.Identity, [kx, "lnw", "lnb"], [kx],
              bias=C.lnb[:, o + d:o + d + 1], scale=C.lnw[:, o + d:o + d + 1])


RW_DIM = 512
A_COLS = 1920
B_COLS = 1536
XBC = 1024
O_B = A_COLS
O_Z = A_COLS + B_COLS
O_X = O_Z + 512
O_DT = O_X + XBC
O_G = O_DT + 16
IN_COLS = O_G + 3 * D
NEG = -30000.0


def modulate_tile(C, hT, x, kx, l, mi, TT, hkey="hT"):
    for c in range(8):
        C.act(hT[:, c, :], x[:, c, :], AF.Identity, [kx, ("modT", l), ("mod1p", l)], [hkey],
              bias=C.modT[l][:, 8 * mi + c:8 * mi + c + 1],
              scale=C.mod1p[l][:, 8 * (mi + 1) + c:8 * (mi + 1) + c + 1])


def stage_inproj(C, l, src):
    L = C.L
    TT = 256
    NT = L // TT
    pa, pb, px, zt, dtt = C.d["projA"], C.d["projB"], C.d["projX"], C.d["z_tok"], C.d["dt_tok"]
    with contextlib.ExitStack() as st:
        w = C.sb(st, "win", [128, 8, O_G], BF16)
        with contextlib.ExitStack() as st2:
            load_w_bf16(C, st2, lambda k, a, b: w[:, k, a:b], C.d["mix_w_in"][l][:, 0:O_G], 8, O_G, "win", blk=1252)
            C.S.barrier()
        xt = [C.sb(st, f"xt{i}", [128, 8, TT], F32) for i in range(2)]
        hTb = [C.sb(st, f"hT{i}", [128, 8, TT], BF16) for i in range(2)]
        pj = [C.sb(st, f"pj{i}", [128, 35, TT], BF16) for i in range(2)]
        zs = [C.sb(st, f"zs{i}", [128, 2, 512], BF16) for i in range(2)]
        ds = [C.sb(st, f"ds{i}", [128, 2, 16], F32) for i in range(2)]
        pp = [C.ps(st, f"pp{i}", [128, 512], F32) for i in range(4)]
        pz = [C.ps(st, f"pz{i}", [128, 512], F32) for i in range(2)]
        pd = C.ps(st, "pd", [128, 512], F32)
        srcv = src.rearrange("(c p) l -> p c l", p=128)
        cols = [j * 128 for j in range(15)] + [O_B + j * 128 for j in range(12)] + [O_X + j * 128 for j in range(8)]
        C.dma(xt[0][:], srcv[:, :, 0:TT], [("d", src.name, 0)], [("xt", 0)])
        modulate_tile(C, hTb[0], xt[0], ("xt", 0), l, 3, TT, hkey=("hT", 0))
        for t in range(NT):
            b = t % 2
            hT = hTb[b]
            if t + 1 < NT:
                C.dma(xt[1 - b][:], srcv[:, :, (t + 1) * TT:(t + 2) * TT], [("d", src.name, t + 1)], [("xt", 1 - b)])
            for jj in range(18):
                if jj == 10 and t + 1 < NT:
                    modulate_tile(C, hTb[1 - b], xt[1 - b], ("xt", 1 - b), l, 3, TT, hkey=("hT", 1 - b))
                bank = jj % 4
                for half in range(2):
                    j = jj * 2 + half
                    if j >= 35:
                        continue
                    for k in range(8):
                        C.mm(pp[bank][:, half * TT:(half + 1) * TT], w[:, k, cols[j]:cols[j] + 128], hT[:, k, :],
                             k == 0, k == 7, ["win", ("hT", b)], [("pp", bank)])
                n = 2 if jj * 2 + 1 < 35 else 1
                o = pj[b][:, jj * 2:jj * 2 + n, :]
                i_ = pp[bank][:, 0:n * TT].rearrange("p (c l) -> p c l", c=n)
                if jj % 2 == 0:
                    C.act(o, i_, AF.Copy, [("pp", bank)], [("pj", b)])
                else:
                    C.cp("dve", o, i_, [("pp", bank)], [("pj", b)])
            for blk in range(2):
                for k in range(8):
                    C.mm(pz[blk][:], hT[:, k, blk * 128:(blk + 1) * 128], w[:, k, O_Z:O_Z + 512], k == 0, k == 7,
                         ["win", ("hT", b)], [("pz", blk)])
                C.cp("dve", zs[b][:, blk, :], pz[blk][:], [("pz", blk)], [("zs", b)])
                for k in range(8):
                    C.mm(pd[:, blk * 16:(blk + 1) * 16], hT[:, k, blk * 128:(blk + 1) * 128], w[:, k, O_DT:O_DT + 16],
                         k == 0, k == 7, ["win", ("hT", b)], ["pd"])
            C.act(ds[b][:], pd[:, 0:32].rearrange("p (c h) -> p c h", c=2), AF.Copy, ["pd"], [("ds", b)])
            sl = slice(t * TT, (t + 1) * TT)
            C.dma(pa.rearrange("(c p) l -> p c l", p=128)[:, :, sl], pj[b][:, 0:15, :], [("pj", b)], [("d", "projA", t)], q="sp")
            C.dma(pb.rearrange("(c p) l -> p c l", p=128)[:, :, sl], pj[b][:, 15:27, :], [("pj", b)], [("d", "projB", t)], q="sp")
            C.dma(px.rearrange("(c p) l -> p c l", p=128)[:, :, sl], pj[b][:, 27:35, :], [("pj", b)], [("d", "projX", t)], q="sp")
            C.dma(zt.rearrange("(c p) n -> p c n", p=128)[:, 2 * t:2 * t + 2, :], zs[b][:], [("zs", b)], [("d", "z_tok", t)], q="pool")
            C.dma(dtt.rearrange("(c p) n -> p c n", p=128)[:, 2 * t:2 * t + 2, :], ds[b][:], [("ds", b)], [("d", "dt_tok", t)], q="pool")
    C.S.barrier()

def stage_ssd(C, l):
    L = C.L
    NQ = L // 128
    px, zt, dtt = C.d["projX"], C.d["z_tok"], C.d["dt_tok"]
    bcT, xbt, ycT = C.d["ssd_bcT"], C.d["ssd_xbtok"], C.d["ycT"]
    with contextlib.ExitStack() as st:
        cw = C.sb(st, "cw", [128, 40], F32)
        cb = C.sb(st, "cb", [128, 8], F32)
        dg = C.sb(st, "dg", [128, 8, 5, 128], BF16)
        C.dma(cw[:], C.d["ssd_cw"][l], [], ["cw"])
        C.dma(cb[:], C.d["ssd_cb"][l], [], ["cb"])
        for c in range(8):
            for k in range(5):
                C.ts("pool", dg[:, c, k, :], C.ident[:], cw[:, c * 5 + k:c * 5 + k + 1], ALU.mult, ["cw", "ident"], ["dg"])
        TA = 512
        xin = [C.sb(st, f"xin{i}", [128, 8, TA + 4], BF16) for i in range(2)]
        xc = [C.sb(st, f"xc{i}", [128, 8, TA], BF16) for i in range(2)]
        tokb = [C.sb(st, f"tokb{i}", [128, 4, 768], BF16) for i in range(2)]
        pc = [C.ps(st, f"pc{i}", [128, 512], F32) for i in range(4)]
        pT = [C.ps(st, f"pT{i}", [128, 1024], BF16) for i in range(2)]
        pxv = px.rearrange("(c p) l -> p c l", p=128)
        NTA = L // TA
        for t in range(NTA):
            b = t % 2
            kx = ("xin", b)
            lo = t * TA - 2
            hi = (t + 1) * TA + 2
            c0, c1 = 0, TA + 4
            if t == 0:
                C.memset("pool", xin[b][:, :, 0:2], 0.0, [kx])
                lo, c0 = 0, 2
            if t == NTA - 1:
                C.memset("pool", xin[b][:, :, TA + 2:TA + 4], 0.0, [kx])
                hi, c1 = L, TA + 2
            C.dma(xin[b][:, :, c0:c1], pxv[:, :, lo:hi], [("d", "projX", i) for i in range(max(0, 2 * t - 1), min(L // 256, 2 * t + 3))], [kx])
            for c in range(8):
                bank = c % 4
                for k in range(5):
                    C.mm(pc[bank][:], dg[:, c, k, :], xin[b][:, c, k:k + TA], k == 0, k == 4, ["dg", kx], [("pc", bank)])
                C.act(xc[b][:, c, :], pc[bank][:], AF.Silu, [("pc", bank), "cb"], [("xc", b)], bias=cb[:, c:c + 1])
            C.dma(bcT.rearrange("(c p) l -> p c l", p=128)[:, :, t * TA:(t + 1) * TA], xc[b][:, 4:8, :], [("xc", b)],
                  [("d", "bcT", t)], q="pool")
            for blk in range(4):
                pb_ = (t * 4 + blk) % 2
                for c in range(6):
                    C.tr(pT[pb_][:, c * 128:(c + 1) * 128], xc[b][:, c, blk * 128:(blk + 1) * 128], C.identb[:],
                         [("xc", b), "identb"], [("pT", pb_)])
                if blk % 2 == 0:
                    C.cp("dve", tokb[b][:, blk, :], pT[pb_][:, 0:768], [("pT", pb_)], [("tokb", b)])
                else:
                    C.act(tokb[b][:, blk, :], pT[pb_][:, 0:768], AF.Copy, [("pT", pb_)], [("tokb", b)])
            C.dma(xbt.rearrange("(c p) n -> p c n", p=128)[:, 4 * t:4 * t + 4, :], tokb[b][:], [("tokb", b)],
                  [("d", "xbtok", t)], q="pool")
    C.S.barrier()
    with contextlib.ExitStack() as st:
        masks = C.sb(st, "masks", [128, 6, 128], F32)
        C.dma(masks[:], C.d["c_masks"][:, :, :], [], ["masks"])
        dtb = C.sb(st, "dtb", [128, 16], F32)
        alog = C.sb(st, "alog", [128, 16], F32)
        Dbc = C.sb(st, "Dbc", [128, 512], F32)
        nw = C.sb(st, "nw", [128, 512], F32)
        C.dma(dtb[:], C.d["ssd_dtb"][l], [], ["dtb"])
        C.dma(alog[:], C.d["ssd_alog"][l], [], ["alog"])
        C.dma(Dbc[:], C.d["ssd_Dbc"][l], [], ["Dbc"])
        C.dma(nw[:], C.d["ssd_nw"][l], [], ["nw"])
        dtv = C.sb(st, "dtv", [128, NQ, 16], F32)
        av = C.sb(st, "av", [128, NQ, 16], F32)
        Ev = C.sb(st, "Ev", [128, 5, NQ * 16], F32)
        Wf = C.sb(st, "Wf", [128, NQ, 8], F32)
        Wb = C.sb(st, "Wb", [128, NQ, 8], F32)
        stF = C.sb(st, "stF", [128, NQ, 512], BF16)
        stB = C.sb(st, "stB", [128, NQ, 512], BF16)
        state = C.sb(st, "state", [128, 512], F32)
        pD = [C.ps(st, f"pD{i}", [128, 512], F32) for i in range(2)]
        pe5 = pD
        C.dma(dtv[:], dtt.rearrange("(q p) h -> p q h", p=128), [("d", "dt_tok", i) for i in range(L // 256)], ["dtv"])
        C.tt("dve", dtv[:], dtv[:], dtb[:].unsqueeze(1).broadcast_to([128, NQ, 16]), ALU.add, ["dtv", "dtb"], ["dtv"])
        C.act(dtv[:], dtv[:], AF.Exp, ["dtv"], ["dtv"])
        C.act(dtv[:], dtv[:], AF.Ln, ["dtv"], ["dtv"], bias=1.0)
        C.act(alog[:], alog[:], AF.Exp, ["alog"], ["alog"])
        C.ts("dve", alog[:], alog[:], -1.0, ALU.mult, ["alog"], ["alog"])
        C.tt("dve", av[:], dtv[:], alog[:].unsqueeze(1).broadcast_to([128, NQ, 16]), ALU.mult, ["dtv", "alog"], ["av"])
        lhs5 = [masks[:, 1, :], masks[:, 3, :], masks[:, 2, :], masks[:, 0, :], C.ones[:]]
        for g8 in range(NQ // 8):
            rhs = av[:, g8 * 8:(g8 + 1) * 8, :]
            for kind in range(5):
                bank = kind // 4
                C.mm(pe5[bank][:, (kind % 4) * 128:(kind % 4 + 1) * 128], lhs5[kind], rhs, True, True,
                     ["masks", "ones", "av"], [("pD", bank)])
            C.act(Ev[:, 0:4, g8 * 128:(g8 + 1) * 128], pe5[0][:].rearrange("p (k n) -> p k n", k=4), AF.Exp,
                  [("pD", 0)], ["Ev"])
            C.act(Ev[:, 4, g8 * 128:(g8 + 1) * 128], pe5[1][:, 0:128], AF.Exp, [("pD", 1)], ["Ev"])
        Ev4 = lambda kind: Ev[:, kind, :].rearrange("p (q h) -> p q h", h=16)
        C.tt("dve", Wf[:], dtv[:, :, 0:8], Ev4(3)[:, :, 0:8], ALU.mult, ["dtv", "Ev"], ["Wf"])
        C.tt("dve", Wb[:], dtv[:, :, 8:16], Ev4(2)[:, :, 8:16], ALU.mult, ["dtv", "Ev"], ["Wb"])
        tok = [C.sb(st, f"tok{i}", [128, 768], BF16) for i in range(3)]
        xdd = [C.sb(st, f"xdd{i}", [128, 512], BF16) for i in range(2)]
        stmp = C.sb(st, "stmp", [128, 512], F32)
        pst = [C.ps(st, f"pst{i}", [128, 512], F32) for i in range(2)]
        xbv = xbt.rearrange("(q p) n -> p q n", p=128)
        ld = 0
        ya = C.sb(st, "ya", [128, 512], F32)
        yb = C.sb(st, "yb", [128, 512], F32)
        state2 = [state, ya]
        stmp2 = [stmp, yb]
        orders = [list(range(NQ)), list(range(NQ - 1, -1, -1))]
        for dr in range(2):
            stX = stF if dr == 0 else stB
            C.memset("pool", stX[:, orders[dr][0], :], 0.0, [("stX", dr)])
            C.memset("pool", state2[dr][:], 0.0, [("state", dr)])
        for i in range(NQ - 1):
            for dr in range(2):
                stX, W = (stF, Wf) if dr == 0 else (stB, Wb)
                q = orders[dr][i]
                tb = ld % 3
                ld += 1
                C.dma(tok[tb][:], xbv[:, q, :], [], [("tok", tb)])
                C.tt("dve", xdd[dr][:].rearrange("p (h e) -> p h e", h=8), tok[tb][:, 0:512].rearrange("p (h e) -> p h e", h=8),
                     W[:, q, :].unsqueeze(2).broadcast_to([128, 8, 64]), ALU.mult, [("tok", tb), "Wf", "Wb"], [("xdd", dr)])
                for g in range(2):
                    C.mm(pst[dr][:, g * 256:(g + 1) * 256], tok[tb][:, 512 + g * 128:512 + (g + 1) * 128],
                         xdd[dr][:, g * 256:(g + 1) * 256], True, True, [("tok", tb), ("xdd", dr)], [("pst", dr)])
                C.tt("pool", stmp2[dr][:].rearrange("p (h e) -> p h e", h=8), state2[dr][:].rearrange("p (h e) -> p h e", h=8),
                     Ev4(4)[:, q, dr * 8:dr * 8 + 8].unsqueeze(2).broadcast_to([128, 8, 64]), ALU.mult, [("state", dr), "Ev"], [("stmp", dr)])
                C.tt("dve", state2[dr][:], stmp2[dr][:], pst[dr][:], ALU.add, [("stmp", dr), ("pst", dr)], [("state", dr)])
                C.act(stX[:, orders[dr][i + 1], :], state2[dr][:], AF.Copy, [("state", dr)], [("stX", dr)])
        C.S.barrier()
        bc = [C.sb(st, f"bc{i}", [128, 4, 128], BF16) for i in range(2)]
        zb = [C.sb(st, f"zb{i}", [128, 512], BF16) for i in range(2)]
        Ah = [C.sb(st, f"Ah{i}", [128, 128], F32) for i in range(4)]
        Lt = [C.sb(st, f"Lt{i}", [128, 4, 128], F32) for i in range(2)]
        Mt = [C.sb(st, f"Mt{i}", [128, 4, 128], BF16) for i in range(2)]
        xd = [C.sb(st, f"xd{i}", [128, 512], BF16) for i in range(2)]
        sz = C.sb(st, "sz", [128, 512], F32)
        ss = C.sb(st, "ss", [128, 2], F32)
        rs = C.sb(st, "rs", [128, 2], F32)
        y3 = C.sb(st, "y3", [128, 512], BF16)
        yT = [C.sb(st, f"yT{i}", [128, 4, 128], BF16) for i in range(2)]
        pcb = C.ps(st, "pcb", [128, 512], F32)
        pY = C.ps(st, "pY", [128, 512], F32)
        pO = pst
        pT = C.ps(st, "pT", [128, 1024], BF16)
        bcv = bcT.rearrange("(c p) l -> p c l", p=128)
        ztv = zt.rearrange("(q p) n -> p q n", p=128)
        ycv = ycT.rearrange("(c p) l -> p c l", p=128)
        nD = 0
        nA = 0
        for q in range(NQ):
            b = q % 2
            tb = ld % 3
            ld += 1
            C.dma(tok[tb][:], xbv[:, q, :], [("d", "xbtok", q // 4)], [("tok", tb)])
            C.dma(bc[b][:], bcv[:, :, q * 128:(q + 1) * 128], [("d", "bcT", q // 4)], [("bc", b)])
            C.dma(zb[b][:], ztv[:, q, :], [("d", "z_tok", q // 2)], [("zb", b)], q="act")
            for g in range(2):
                C.mm(pcb[:, g * 128:(g + 1) * 128], bc[b][:, g, :], bc[b][:, 2 + g, :], g == 0, True,
                     [("bc", b)], ["pcb"], skip_group_check=True)
            first_y = True
            for dr in range(2):
                xb_ = dr
                C.tt("pool", xd[xb_][:].rearrange("p (h e) -> p h e", h=8), tok[tb][:, 0:512].rearrange("p (h e) -> p h e", h=8),
                     dtv[:, q, dr * 8:dr * 8 + 8].unsqueeze(2).broadcast_to([128, 8, 64]), ALU.mult, [("tok", tb), "dtv"], [("xd", xb_)])
                mA = masks[:, 0, :] if dr == 0 else masks[:, 2, :]
                mR = masks[:, 1, :] if dr == 0 else masks[:, 3, :]
                mN = masks[:, 4, :] if dr == 0 else masks[:, 5, :]
                for g in range(2):
                    db = nD % 2
                    nD += 1
                    for e in range(4):
                        h = g * 4 + e
                        ab = nA % 4
                        nA += 1
                        C.ts("dve" if e % 2 == 0 else "pool", Ah[ab][:], mA, av[:, q, dr * 8 + h:dr * 8 + h + 1], ALU.mult,
                             ["masks", "av"], [("Ah", ab)])
                        C.mm(pD[db][:, e * 128:(e + 1) * 128], Ah[ab][:], mR, e == 0, False, [("Ah", ab), "masks"], [("pD", db)],
                             skip_group_check=True)
                        C.mm(pD[db][:, e * 128:(e + 1) * 128], C.ident[:], mN, False, True, ["ident", "masks"], [("pD", db)],
                             skip_group_check=True)
                    C.act(Lt[db][:], pD[db][:].rearrange("p (e l) -> p e l", e=4), AF.Exp, [("pD", db)], [("Lt", db)])
                    C.tt("dve", Mt[db][:], Lt[db][:], pcb[:, g * 128:(g + 1) * 128].unsqueeze(1).broadcast_to([128, 4, 128]), ALU.mult,
                         [("Lt", db), "pcb"], [("Mt", db)])
                    for e in range(4):
                        h = g * 4 + e
                        C.mm(pY[:, h * 64:(h + 1) * 64], Mt[db][:, e, :], xd[xb_][:, h * 64:(h + 1) * 64], first_y, True,
                             [("Mt", db), ("xd", xb_)], ["pY"], skip_group_check=True)
                        first_y = False
                stX = stF if dr == 0 else stB
                for g in range(2):
                    C.mm(pO[dr][:, g * 256:(g + 1) * 256], bc[b][:, 2 + g, :], stX[:, q, g * 256:(g + 1) * 256], g == 0, True,
                         [("bc", b), ("stX", dr)], [("pst", dr)], skip_group_check=True)
            v3 = lambda ap: ap.rearrange("p (h e) -> p h e", h=8)
            C.tt("pool", ya[:], tok[tb][:, 0:512], Dbc[:], ALU.mult, [("tok", tb), "Dbc"], ["ya"])
            C.tt("dve", v3(yb[:]), v3(pO[0][:]), Ev4(0)[:, q, 0:8].unsqueeze(2).broadcast_to([128, 8, 64]), ALU.mult,
                 [("pst", 0), "Ev"], ["yb"])
            C.tt("pool", ya[:], ya[:], yb[:], ALU.add, ["ya", "yb"], ["ya"])
            C.tt("dve", v3(yb[:]), v3(pO[1][:]), Ev4(1)[:, q, 8:16].unsqueeze(2).broadcast_to([128, 8, 64]), ALU.mult,
                 [("pst", 1), "Ev"], ["yb"])
            C.tt("pool", ya[:], ya[:], yb[:], ALU.add, ["ya", "yb"], ["ya"])
            C.tt("dve", ya[:], ya[:], pY[:], ALU.add, ["ya", "pY"], ["ya"])
            C.act(sz[:], zb[b][:], AF.Silu, [("zb", b)], ["sz"])
            C.tt("pool", ya[:], ya[:], sz[:], ALU.mult, ["ya", "sz"], ["ya"])
            for g in range(2):
                C.S.add("act", (lambda g=g: (lambda e: e.activation(sz[:, g * 256:(g + 1) * 256], ya[:, g * 256:(g + 1) * 256],
                                                                  AF.Square, accum_out=ss[:, g:g + 1])))(), ["ya"], ["sz", "ss"])
            C.ts("dve", ss[:], ss[:], 1.0 / 256, ALU.mult, ["ss"], ["ss"], s2=1e-5, op1=ALU.add)
            C.act(ss[:], ss[:], AF.Sqrt, ["ss"], ["ss"])
            C.recip(rs[:], ss[:], ["ss"], ["rs"])
            for g in range(2):
                C.stt(y3[:, g * 256:(g + 1) * 256], ya[:, g * 256:(g + 1) * 256], rs[:, g:g + 1], nw[:, g * 256:(g + 1) * 256],
                      ALU.mult, ALU.mult, ["ya", "rs", "nw"], ["y3"])
            for c in range(4):
                C.tr(pT[:, c * 128:(c + 1) * 128], y3[:, c * 128:(c + 1) * 128], C.identb[:], ["y3", "identb"], ["pT"])
            C.act(yT[b][:], pT[:, 0:512].rearrange("p (c l) -> p c l", c=4), AF.Copy, ["pT"], [("yT", b)])
            C.dma(ycv[:, :, q * 128:(q + 1) * 128], yT[b][:], [("yT", b)], [("d", "ycT", q)], q="pool")
    C.S.barrier()

PI = math.pi


def dw_conv_diag(C, st, name, wdram, nch, K):
    cw = C.sb(st, name + "cw", [128, nch * K], F32)
    dg = C.sb(st, name + "dg", [128, nch, K, 128], BF16)
    C.dma(cw[:], wdram, [], [name + "cw"])
    for c in range(nch):
        for k in range(K):
            C.ts("pool", dg[:, c, k, :], C.ident[:], cw[:, c * K + k:c * K + k + 1], ALU.mult, [name + "cw", "ident"], [name + "dg"])
    return dg


def load_halo(C, buf, kx, view, t, TA, halo, L, dkeys):
    NTA = L // TA
    lo, hi = t * TA - halo, (t + 1) * TA + halo
    c0, c1 = 0, TA + 2 * halo
    if t == 0:
        C.memset("pool", buf[:, :, 0:halo], 0.0, [kx])
        lo, c0 = 0, halo
    if t == NTA - 1:
        C.memset("pool", buf[:, :, TA + halo:TA + 2 * halo], 0.0, [kx])
        hi, c1 = L, TA + halo
    C.dma(buf[:, :, c0:c1], view[:, :, lo:hi], dkeys, [kx])


def hy_filter(C, l):
    L = C.L
    TT = 512
    NT = L // TT
    with contextlib.ExitStack() as st:
        w1 = C.sb(st, "w1", [33, 64], F32)
        w2 = C.sb(st, "w2", [64, 64], F32)
        w3 = C.sb(st, "w3", [64, 512], F32)
        bb = C.sb(st, "bb", [64, 2], F32)
        fq = C.sb(st, "fq", [64, 2], F32)
        nd = C.sb(st, "nd", [128, 4], F32)
        hbi = C.sb(st, "hbi", [128, 4], F32)
        C.dma(w1[:], C.d["hy_f_w1"][l], [], ["w1"])
        C.dma(w2[:], C.d["hy_f_w2"][l], [], ["w2"])
        C.dma(w3[:], C.d["hy_f_w3"][l], [], ["w3"])
        C.dma(bb[:], C.d["hy_fb"][l], [], ["bb"])
        C.dma(fq[:], C.d["hy_freq"][l], [], ["fq"])
        C.dma(nd[:], C.d["c_ndelta"][:, :], [], ["nd"])
        C.dma(hbi[:], C.d["hy_bias"][l], [], ["hbi"])
        C.tt("dve", bb[:], bb[:], fq[:], ALU.mult, ["bb", "fq"], ["bb"])
        hbuf = C.sb(st, "hbuf", [128, 4, L], F32)
        ft = [C.sb(st, f"ft{i}", [33, TT], F32) for i in range(2)]
        dt_ = [C.sb(st, f"dtl{i}", [128, TT], F32) for i in range(2)]
        u = C.sb(st, "u", [64, TT], F32)
        m = C.sb(st, "m", [64, TT], F32)
        h1 = C.sb(st, "h1", [64, TT], F32)
        wd = [C.sb(st, f"wd{i}", [128, TT], F32) for i in range(2)]
        junk = C.sb(st, "junk", [128, TT], F32)
        sp = C.sb(st, "sp", [128, 4, NT], F32)
        ssum = C.sb(st, "ssum", [128, 4], F32)
        rs = C.sb(st, "rs", [128, 4], F32)
        p12 = C.ps(st, "p12", [128, 512], F32)
        p3 = [C.ps(st, f"p3{i}", [128, 512], F32) for i in range(2)]

        def sin_layer(src_ps, j):
            C.act(u[:], src_ps, AF.Identity, ["p12", "fq", "bb"], ["u"], bias=bb[:, j:j + 1], scale=fq[:, j:j + 1])
            C.ts("dve", m[:], u[:], PI, ALU.is_gt, ["u"], ["m"], s2=-2 * PI, op1=ALU.mult)
            C.tt("dve", u[:], u[:], m[:], ALU.add, ["u", "m"], ["u"])
            C.ts("dve", m[:], u[:], -PI, ALU.is_lt, ["u"], ["m"], s2=2 * PI, op1=ALU.mult)
            C.tt("dve", u[:], u[:], m[:], ALU.add, ["u", "m"], ["u"])
            C.ts("dve", u[:], u[:], 3.14159, ALU.min, ["u"], ["u"], s2=-3.14159, op1=ALU.max)
            C.act(h1[:], u[:], AF.Sin, ["u"], ["h1"])

        for t in range(NT):
            b = t % 2
            sl = slice(t * TT, (t + 1) * TT)
            C.dma(ft[b][:], C.d["c_featsT"][:, sl], [], [("ft", b)])
            C.dma(dt_[b][:], C.d["c_distb"][:, sl], [], [("dtl", b)])
            C.mm(p12[0:64, :], w1[:], ft[b][:], True, True, ["w1", ("ft", b)], ["p12"])
            sin_layer(p12[0:64, :], 0)
            C.mm(p12[0:64, :], w2[:], h1[:], True, True, ["w2", "h1"], ["p12"])
            sin_layer(p12[0:64, :], 1)
            for c in range(4):
                pb_ = c % 2
                C.mm(p3[pb_][:], w3[:, c * 128:(c + 1) * 128], h1[:], True, True, ["w3", "h1"], [("p3", pb_)])
                C.act(wd[pb_][:], dt_[b][:], AF.Exp, [("dtl", b), "nd"], [("wd", pb_)], scale=nd[:, c:c + 1])
                C.tt("dve", hbuf[:, c, sl], p3[pb_][:], wd[pb_][:], ALU.mult, [("p3", pb_), ("wd", pb_)], ["hbuf"])
                C.S.add("act", (lambda c=c, t=t, sl=sl: (lambda e: e.activation(junk[:], hbuf[:, c, sl], AF.Square,
                                                                           accum_out=sp[:, c, t:t + 1])))(), ["hbuf"], ["junk", "sp"])
        C.S.add("dve", lambda e: e.tensor_reduce(ssum[:], sp[:], AX.X, ALU.add), ["sp"], ["ssum"])
        C.ts("dve", ssum[:], ssum[:], 1e-12, ALU.add, ["ssum"], ["ssum"])
        C.act(ssum[:], ssum[:], AF.Sqrt, ["ssum"], ["ssum"])
        C.recip(rs[:], ssum[:], ["ssum"], ["rs"])
        hb = [C.sb(st, f"hb{i}", [128, 2048], BF16) for i in range(2)]
        hv = C.d["hy_hT"].rearrange("(c p) l -> p c l", p=128)
        i = 0
        for c in range(4):
            C.ts("dve", hbuf[:, c, L // 2:L // 2 + 1], hbuf[:, c, L // 2:L // 2 + 1], rs[:, c:c + 1], ALU.mult, ["hbuf", "rs", "hbi"], ["hbuf"],
                 s2=hbi[:, c:c + 1], op1=ALU.add)
            C.recip(junk[:, 0:1], rs[:, c:c + 1], ["rs"], ["junk"])
            C.ts("dve", hbuf[:, c, L // 2:L // 2 + 1], hbuf[:, c, L // 2:L // 2 + 1], junk[:, 0:1], ALU.mult, ["hbuf", "junk"], ["hbuf"])
            for s in range(L // 2048):
                b = i % 2
                i += 1
                C.act(hb[b][:], hbuf[:, c, s * 2048:(s + 1) * 2048], AF.Identity, ["hbuf", "rs"], [("hb", b)], scale=rs[:, c:c + 1])
                C.dma(hv[:, c, s * 2048:(s + 1) * 2048], hb[b][:], [("hb", b)], [("d", "hy_hT", i)], q="pool")
    C.S.barrier()


def hy_front(C, l):
    L = C.L
    TA = 512
    NTA = L // TA
    with contextlib.ExitStack() as st:
        dg = dw_conv_diag(C, st, "hy", C.d["hy_cw"][l], 12, 3)
        cb = C.sb(st, "hycb", [128, 12], F32)
        C.dma(cb[:], C.d["hy_cb"][l], [], ["hycb"])
        xin = [C.sb(st, f"hxin{i}", [128, 12, TA + 2], BF16) for i in range(2)]
        x1 = C.sb(st, "hx1", [128, 4, TA], F32)
        ob = [C.sb(st, f"hob{i}", [128, 8, TA], BF16) for i in range(2)]
        pc = [C.ps(st, f"hpc{i}", [128, 512], F32) for i in range(4)]
        pv = C.d["projB"].rearrange("(c p) l -> p c l", p=128)
        x0v = C.d["hy_x0T"].rearrange("(c p) l -> p c l", p=128)
        zv = C.d["hy_zT"].rearrange("(c p) l -> p c l", p=128)
        for t in range(NTA):
            b = t % 2
            kx = ("hxin", b)
            load_halo(C, xin[b], kx, pv, t, TA, 1, L, [("d", "projB", i) for i in range(max(0, 2 * t - 1), min(L // 256, 2 * t + 3))])
            for c in range(12):
                bank = c % 4
                for k in range(3):
                    C.mm(pc[bank][:], dg[:, c, k, :], xin[b][:, c, k:k + TA], k == 0, k == 2, ["hydg", kx], [("hpc", bank)])
                if c < 4:
                    C.act(ob[b][:, c, :], pc[bank][:], AF.Identity, [("hpc", bank), "hycb"], [("hob", b)], bias=cb[:, c:c + 1])
                elif c < 8:
                    C.act(x1[:, c - 4, :], pc[bank][:], AF.Identity, [("hpc", bank), "hycb"], ["hx1"], bias=cb[:, c:c + 1])
                else:
                    C.stt(ob[b][:, c - 4, :], pc[bank][:], cb[:, c:c + 1], x1[:, c - 8, :], ALU.add, ALU.mult,
                          [("hpc", bank), "hycb", "hx1"], [("hob", b)])
            sl = slice(t * TA, (t + 1) * TA)
            C.dma(x0v[:, :, sl], ob[b][:, 0:4, :], [("hob", b)], [("d", "hy_x0T", t)], q="pool")
            C.dma(zv[:, :, sl], ob[b][:, 4:8, :], [("hob", b)], [("d", "hy_zT", t)], q="pool")
    C.S.barrier()


def hy_fft(C, l, filt):
    L = C.L
    assert L == 8192
    NP = 256
    with contextlib.ExitStack() as st:
        F256 = C.sb(st, "F256", [128, 512], BF16)
        Tw = C.sb(st, "Tw", [128, 768], F32)
        L4 = C.sb(st, "L4", [128, 4, 64], BF16)
        G3 = C.sb(st, "G3", [128, 3, 128], BF16)
        TAB = C.sb(st, "TAB", [128, 2, 2, 128], F32)
        FIN = C.sb(st, "FIN", [128, 2, 2, 128], BF16)
        for nm, t_ in (("F256", F256), ("Tw", Tw), ("L4", L4), ("G3", G3), ("TAB", TAB), ("FIN", FIN)):
            C.dma(t_[:], C.d["c_" + nm], [], [nm])
        zA = C.sb(st, "zA", [128, 512, 64], BF16)
        src = C.d["hy_hT"] if filt else C.d["hy_zT"]
        for g in range(16):
            C.dma(zA[:, g * 32:(g + 1) * 32, :], src[g * 32:(g + 1) * 32, :].rearrange("c (p j) -> p c j", j=64),
                  [], [("zA", g)], q="sp" if g % 2 == 0 else "act")
        if not filt:
            x0A = C.sb(st, "x0A", [128, 512, 64], BF16)
            for g in range(16):
                C.dma(x0A[:, g * 32:(g + 1) * 32, :], C.d["hy_x0T"][g * 32:(g + 1) * 32, :].rearrange("c (p j) -> p c j", j=64),
                      [], [("x0A", g)], q="sp" if g % 2 == 0 else "act")
        NB = 3
        t12 = [C.sb(st, f"t12{i}", [128, 2, 512], BF16) for i in range(NB)]
        pB = [C.ps(st, f"pB{i}", [128, 512], F32) for i in range(2)]
        pZ = [C.ps(st, f"pZ{i}", [128, 512], F32) for i in range(2)]
        if filt:
            Hfb = [C.sb(st, f"Hfb{i}", [128, 768], BF16) for i in range(2)]
        else:
            Hf = [C.sb(st, f"Hf{i}", [128, 768], BF16) for i in range(4)]
            u12 = [C.sb(st, f"u12{i}", [128, 2, 512], BF16) for i in range(NB)]
            w12 = [C.sb(st, f"w12{i}", [128, 2, 4, 128], BF16) for i in range(NB)]
            pQ = [C.ps(st, f"pQ{i}", [128, 512], F32) for i in range(2)]
            pY = [C.ps(st, f"pY{i}", [128, 512], F32) for i in range(2)]
        for i in range(NP):
            b = i % 2
            nb = i % NB
            if not filt:
                hb3 = i % 4
                C.dma(Hf[hb3][:], C.d["hy_Hf"][i], [], [("Hf", hb3)], q="sp")
            for hh in range(2):
                C.mm(pB[b][64 * hh:64 * hh + 64, :], zA[:, 2 * i + hh, :], F256[:], True, True, [("zA", (2 * i) // 32), "F256"], [("pB", b)])
            C.tt("dve", t12[nb][:, 0, :], pB[b][:], Tw[:, 0:512], ALU.mult, [("pB", b), "Tw"], [("t12", nb)])
            C.tt("dve", t12[nb][:, 1, :], pB[b][:], Tw[:, 256:768], ALU.mult, [("pB", b), "Tw"], [("t12", nb)])
            for hh in range(2):
                ps_ = slice(64 * hh, 64 * hh + 64)
                t1a, t1b = t12[nb][ps_, 0, 0:256], t12[nb][ps_, 0, 256:512]
                t2a, t2b = t12[nb][ps_, 1, 0:256], t12[nb][ps_, 1, 256:512]
                cc, nc_, ss, ns = L4[ps_, 0, :], L4[ps_, 1, :], L4[ps_, 2, :], L4[ps_, 3, :]
                zr, zi = pZ[b][ps_, 0:256], pZ[b][ps_, 256:512]
                seq = [(zr, cc, t1a), (zr, nc_, t1b), (zr, ss, t2a), (zr, ss, t2b),
                       (zi, cc, t2a), (zi, cc, t2b), (zi, ns, t1a), (zi, ss, t1b)]
                for j, (o, lt, rh) in enumerate(seq):
                    C.mm(o, lt, rh, j == 0, j == 7, ["L4", ("t12", nb)], [("pZ", b)], skip_group_check=True)
            if filt:
                C.act(Hfb[b][:, 0:512], pZ[b][:], AF.Copy, [("pZ", b)], [("Hfb", b)])
                C.cp("dve", Hfb[b][:, 512:768], pZ[b][:, 0:256], [("pZ", b)], [("Hfb", b)])
                C.dma(C.d["hy_Hf"][i], Hfb[b][:], [("Hfb", b)], [("d", "hy_Hf", i)], q="act")
                continue
            C.tt("dve", u12[nb][:, 0, :], pZ[b][:], Hf[hb3][:, 0:512], ALU.mult, [("pZ", b), ("Hf", hb3)], [("u12", nb)])
            C.tt("dve", u12[nb][:, 1, :], pZ[b][:], Hf[hb3][:, 256:768], ALU.mult, [("pZ", b), ("Hf", hb3)], [("u12", nb)])
            for hh in range(2):
                ps_ = slice(64 * hh, 64 * hh + 64)
                for kh in range(2):
                    o = pQ[b][:, (hh * 2 + kh) * 128:(hh * 2 + kh + 1) * 128]
                    ks = slice(kh * 128, (kh + 1) * 128)
                    ks2 = slice(256 + kh * 128, 256 + (kh + 1) * 128)
                    seq = [(u12[nb][ps_, 0, ks], G3[ps_, 0, :]), (u12[nb][ps_, 0, ks2], G3[ps_, 1, :]),
                           (u12[nb][ps_, 1, ks], G3[ps_, 2, :]), (u12[nb][ps_, 1, ks2], G3[ps_, 2, :])]
                    for j, (lt, rh) in enumerate(seq):
                        C.mm(o, lt, rh, j == 0, j == 3, [("u12", nb), "G3"], [("pQ", b)], skip_group_check=True)
            q4 = pQ[b][:].rearrange("p (g n) -> p g n", g=4)
            for ab in range(2):
                for hh in range(2):
                    C.tt("dve", w12[nb][:, ab, 2 * hh:2 * hh + 2, :], q4[:, 2 * hh:2 * hh + 2, :], TAB[:, ab, :, :], ALU.mult,
                         [("pQ", b), "TAB"], [("w12", nb)])
            yb_ = (i // 4) % 2
            for hh in range(2):
                slot = (2 * i + hh) % 8
                o = pY[yb_][:, slot * 64:(slot + 1) * 64]
                n = 0
                for kh in range(2):
                    for ab in range(2):
                        for half in range(2):
                            C.mm(o, FIN[:, kh, ab, :], w12[nb][:, ab, hh * 2 + kh, half * 64:(half + 1) * 64], n == 0, n == 7,
                                 ["FIN", ("w12", nb)], [("pY", yb_)], skip_group_check=True)
                            n += 1
            if i % 4 == 3:
                c0 = 2 * i - 6
                C.tt("dve", x0A[:, c0:c0 + 8, :], x0A[:, c0:c0 + 8, :], pY[yb_][:].rearrange("p (c n) -> p c n", c=8), ALU.mult,
                     [("x0A", c0 // 32), ("pY", yb_)], [("x0A", c0 // 32)])
                if (c0 + 8) % 32 == 0:
                    g = c0 // 32
                    C.dma(C.d["ybT"][g * 32:(g + 1) * 32, :].rearrange("c (p j) -> p c j", j=64), x0A[:, g * 32:(g + 1) * 32, :],
                          [("x0A", g)], [("d", "ybT", g)], q="sp")
    C.S.barrier()


def stage_hyena(C, l):
    hy_filter(C, l)
    hy_front(C, l)
    hy_fft(C, l, True)
    hy_fft(C, l, False)

RW_C = 0.6065306597126334
RW_GN_EPS = 64e-5


def rw_front(C, l):
    L = C.L
    TA = 512
    NTA = L // TA
    with contextlib.ExitStack() as st:
        mu = C.sb(st, "mu", [128, 2, 15], F32)
        wm = C.sb(st, "wm", [128, 15], F32)
        C.dma(mu[:], C.d["rw_mu"][l], [], ["mu"])
        C.tt("dve", wm[:], mu[:, 0, :], mu[:, 1, :], ALU.add, ["mu"], ["wm"])
        C.ts("dve", wm[:], wm[:], -1.0, ALU.mult, ["wm"], ["wm"], s2=1.0, op1=ALU.add)
        dg = C.sb(st, "rdg", [128, 15, 3, 128], BF16)
        for c in range(15):
            C.ts("pool", dg[:, c, 0, :], C.ident[:], mu[:, 0, c:c + 1], ALU.mult, ["mu", "ident"], ["rdg"])
            C.ts("pool", dg[:, c, 1, :], C.ident[:], wm[:, c:c + 1], ALU.mult, ["wm", "ident"], ["rdg"])
            C.ts("pool", dg[:, c, 2, :], C.ident[:], mu[:, 1, c:c + 1], ALU.mult, ["mu", "ident"], ["rdg"])
        lw = C.sb(st, "lw", [128, 3, 512], BF16)
        with contextlib.ExitStack() as st2:
            stg = C.sb(st2, "lstg", [128, 3, 512], F32)
            C.dma(stg[:, 0, :], C.d["rw_w2"][l], [], ["lstg"])
            C.dma(stg[:, 1, :], C.d["rw_a2"][l], [], ["lstg"])
            C.dma(stg[:, 2, :], C.d["rw_g2"][l], [], ["lstg"])
            C.cp("dve", lw[:], stg[:], ["lstg"], ["lw"])
            C.S.barrier()
        bcs = C.sb(st, "bcs", [128, 8, 512], F32)
        C.dma(bcs[:, 0:7, :], C.d["rw_bc"][l], [], ["bcs"])
        C.dma(bcs[:, 7, :], C.d["rw_bc"][l][:, 6, :], [], ["bcs"])
        C.ts("dve", bcs[:, 6, :], bcs[:, 5, :], -1.0, ALU.mult, ["bcs"], ["bcs"], s2=1.0, op1=ALU.add)
        xin = [C.sb(st, f"rxin{i}", [128, 15, TA + 2], BF16) for i in range(2)]
        sh = C.sb(st, "rsh", [128, 15, TA], BF16)
        rkv = [C.sb(st, f"rkv{i}", [128, 1536], BF16) for i in range(2)]
        T = {n: C.sb(st, "r_" + n, [128, 512], F32) for n in
             ["tmp", "sg0", "sg1", "as0", "as1", "kk", "sq", "kkn", "t", "kd0", "kd1", "rkx"]}
        s8 = C.sb(st, "s8", [128, 8], F32)
        oc = [C.sb(st, f"oc{i}", [128, 5, 512], BF16) for i in range(2)]
        od = [C.sb(st, f"od{i}", [128, 2, 2, 512], BF16) for i in range(2)]
        osg = [C.sb(st, f"osg{i}", [128, 2, 512], F32) for i in range(2)]
        pc = [C.ps(st, f"rpc{i}", [128, 512], F32) for i in range(2)]
        pT = C.ps(st, "rpT", [128, 2048], BF16)
        pl = [C.ps(st, f"rpl{i}", [128, 512], F32) for i in range(4)]
        pv = C.d["projA"].rearrange("(c p) l -> p c l", p=128)
        v3 = lambda ap: ap.rearrange("p (h e) -> p h e", h=8)
        bc8 = lambda ap: ap.unsqueeze(2).broadcast_to([128, 8, 64])
        nblk = 0
        for t in range(NTA):
            b = t % 2
            kx = ("rxin", b)
            load_halo(C, xin[b], kx, pv, t, TA, 1, L, [("d", "projA", i) for i in range(max(0, 2 * t - 1), min(L // 256, 2 * t + 3))])
            for c in range(15):
                bank = c % 2
                for k in range(3):
                    C.mm(pc[bank][:], dg[:, c, k, :], xin[b][:, c, k:k + TA], k == 0, k == 2, ["rdg", kx], [("rpc", bank)])
                fn = AF.Tanh if c == 12 else (AF.Sigmoid if c == 14 else AF.Copy)
                if c % 2 == 0 or c >= 12:
                    C.act(sh[:, c, :], pc[bank][:], fn, [("rpc", bank)], ["rsh"])
                else:
                    C.cp("dve", sh[:, c, :], pc[bank][:], [("rpc", bank)], ["rsh"])
            import os
            RWF = int(os.environ.get("RWF", "9"))
            for blk in range(4):
                if RWF < 2:
                    break
                ob = nblk % 2
                nblk += 1
                bs = slice(blk * 128, (blk + 1) * 128)
                for c in range(12):
                    C.tr(pT[:, c * 128:(c + 1) * 128], sh[:, c, bs], C.identb[:], ["rsh", "identb"], ["rpT"])
                C.cp("dve", rkv[ob][:, 0:768], pT[:, 0:768], ["rpT"], [("rkv", ob)])
                C.act(rkv[ob][:, 768:1536], pT[:, 768:1536], AF.Copy, ["rpT"], [("rkv", ob)])
                r_ = rkv[ob][:, 0:512]
                k_ = rkv[ob][:, 512:1024]
                v_ = rkv[ob][:, 1024:1536]
                krkv = ("rkv", ob)
                if RWF < 3:
                    continue
                for d in range(2):
                    ps_ = slice(64 * d, 64 * d + 64)
                    C.mm(pl[d][:], sh[ps_, 12, bs], lw[ps_, 0, :], True, True, ["rsh", "lw"], [("rpl", d)])
                    C.mm(pl[2 + d][:], sh[ps_, 13, bs], lw[ps_, 1, :], True, True, ["rsh", "lw"], [("rpl", 2 + d)])
                for d in range(2):
                    C.tt("dve", T["tmp"][:], pl[d][:], bcs[:, d, :], ALU.add, [("rpl", d), "bcs"], ["r_tmp"])
                    C.act(osg[ob][:, d, :], T["tmp"][:], AF.Sigmoid, ["r_tmp"], [("osg", ob)])
                    C.tt("dve", T["tmp"][:], pl[2 + d][:], bcs[:, 2 + d, :], ALU.add, [("rpl", 2 + d), "bcs"], ["r_tmp"])
                    C.act(T[f"as{d}"][:], T["tmp"][:], AF.Sigmoid, ["r_tmp"], [f"r_as{d}"])
                C.mm(pl[0][:], sh[:, 14, bs], lw[:, 2, :], True, True, ["rsh", "lw"], [("rpl", 0)])
                C.act(oc[ob][:, 3, :], pl[0][:], AF.Copy, [("rpl", 0)], [("oc", ob)])
                C.cp("pool", oc[ob][:, 0, :], r_, [krkv], [("oc", ob)])
                C.cp("pool", oc[ob][:, 1, :], v_, [krkv], [("oc", ob)])
                if RWF < 4:
                    continue
                C.tt("dve", T["kk"][:], k_, bcs[:, 4, :], ALU.mult, [krkv, "bcs"], ["r_kk"])
                C.tt("pool", T["sq"][:], T["kk"][:], T["kk"][:], ALU.mult, ["r_kk"], ["r_sq"])
                C.S.add("dve", lambda e: e.tensor_reduce(s8[:], v3(T["sq"][:]), AX.X, ALU.add), ["r_sq"], ["s8"])
                C.act(s8[:], s8[:], AF.Sqrt, ["s8"], ["s8"])
                C.ts("dve", s8[:], s8[:], 1e-12, ALU.max, ["s8"], ["s8"])
                C.recip(s8[:], s8[:], ["s8"], ["s8"])
                C.tt("dve", v3(T["kkn"][:]), v3(T["kk"][:]), bc8(s8[:]), ALU.mult, ["r_kk", "s8"], ["r_kkn"])
                C.cp("pool", oc[ob][:, 2, :], T["kkn"][:], ["r_kkn"], [("oc", ob)])
                if RWF < 5:
                    continue
                for d in range(2):
                    a_ = T[f"as{d}"]
                    C.tt("pool", T["t"][:], a_[:], bcs[:, 5, :], ALU.mult, [f"r_as{d}", "bcs"], ["r_t"])
                    C.tt("pool", T["t"][:], T["t"][:], bcs[:, 6, :], ALU.add, ["r_t", "bcs"], ["r_t"])
                    C.tt("dve", T[f"kd{d}"][:], k_, T["t"][:], ALU.mult, [krkv, "r_t"], [f"r_kd{d}"])
                    C.cp("pool", od[ob][:, d, 0, :], T[f"kd{d}"][:], [f"r_kd{d}"], [("od", ob)])
                    C.tt("dve", od[ob][:, d, 1, :], T["kkn"][:], a_[:], ALU.mult, ["r_kkn", f"r_as{d}"], [("od", ob)])
                if RWF == 6:
                    row = t * 4 + blk
                    rs_ = slice(row * 128, (row + 1) * 128)
                    C.dma(C.d["rw_c"][rs_], oc[ob][:], [("oc", ob)], [("d", "rw_c", row)], q="sp")
                    for d in range(2):
                        C.dma(C.d["rw_d"][d, rs_], od[ob][:, d], [("od", ob)], [("d", "rw_d", d, row)], q="sp")
                        C.dma(C.d["rw_sg"][d, rs_], osg[ob][:, d, :], [("osg", ob)], [("d", "rw_sg", d, row)], q="sp")
                if RWF <= 6:
                    continue
                C.tt("pool", T["kd0"][:], T["kd0"][:], T["kd1"][:], ALU.add, ["r_kd0", "r_kd1"], ["r_kd0"])
                C.tt("dve", T["rkx"][:], r_, bcs[:, 7, :], ALU.mult, [krkv, "bcs"], ["r_rkx"])
                C.tt("pool", T["rkx"][:], T["rkx"][:], T["kd0"][:], ALU.mult, ["r_rkx", "r_kd0"], ["r_rkx"])
                C.S.add("dve", lambda e: e.tensor_reduce(s8[:], v3(T["rkx"][:]), AX.X, ALU.add), ["r_rkx"], ["s8"])
                C.tt("dve", v3(oc[ob][:, 4, :]), v3(v_), bc8(s8[:]), ALU.mult, [krkv, "s8"], [("oc", ob)])
                row = t * 4 + blk
                rs_ = slice(row * 128, (row + 1) * 128)
                if os.environ.get("RWD", "1") == "0":
                    continue
                C.dma(C.d["rw_c"][rs_], oc[ob][:], [("oc", ob)], [("d", "rw_c", row)], q="sp")
                for d in range(2):
                    C.dma(C.d["rw_d"][d, rs_], od[ob][:, d], [("od", ob)], [("d", "rw_d", d, row)], q="sp")
                    C.dma(C.d["rw_sg"][d, rs_], osg[ob][:, d, :], [("osg", ob)], [("d", "rw_sg", d, row)], q="sp")
    C.S.barrier()


def rw_scan(C, l):
    L = C.L
    NT = L // 128
    v3 = lambda ap: ap.rearrange("p (h e) -> p h e", h=8)
    bc8 = lambda ap: ap.unsqueeze(2).broadcast_to([128, 8, 64])
    with contextlib.ExitStack() as st:
        rwm = C.sb(st, "rwm", [128, 2, 5, 128], F32)
        C.dma(rwm[:, 0], C.d["c_rwm"][0], [], ["rwm"])
        C.dma(rwm[:, 1], C.d["c_rwm"][1], [], ["rwm"])
        lnx = C.sb(st, "lnx", [128, 2, 512], F32)
        C.dma(lnx[:], C.d["rw_lnx"][l], [], ["lnx"])
        id64 = C.sb(st, "id64", [64, 16, 64], F32)
        C.cp("dve", id64[:], C.ident[0:64, 0:64].unsqueeze(1).broadcast_to([64, 16, 64]), ["ident"], ["id64"])
        ci = [C.sb(st, f"ci{i}", [128, 5, 512], BF16) for i in range(2)]
        di = [C.sb(st, f"di{i}", [128, 2, 512], BF16) for i in range(2)]
        sgi = [C.sb(st, f"sgi{i}", [128, 512], F32) for i in range(2)]
        yfi = [C.sb(st, f"yfi{i}", [128, 512], F32) for i in range(2)]
        E = C.sb(st, "E", [128, 4, 512], F32)
        tk = C.sb(st, "tk", [128, 6, 512], BF16)
        fT = C.sb(st, "fT", [128, 4, 2, 4, 64], BF16)
        NMb = C.sb(st, "NMb", [128, 8, 128], BF16)
        NMk = C.sb(st, "NMk", [128, 8, 128], BF16)
        PP = [C.sb(st, f"PP{i}", [128, 8, 128], BF16) for i in range(2)]
        Xs = C.sb(st, "Xs", [128, 8, 128], BF16)
        gam = C.sb(st, "gam", [64, 8, 2], F32)
        dgam = C.sb(st, "dgam", [64, 16, 64], F32)
        Gs = C.sb(st, "Gs", [64, 16, 64], F32)
        Hs = C.sb(st, "Hs", [64, 16, 64], F32)
        Qs = C.sb(st, "Qs", [64, 16, 64], BF16)
        ST32 = C.sb(st, "ST32", [64, 512], F32)
        STb = C.sb(st, "STb", [64, 512], BF16)
        yo = [C.sb(st, f"yo{i}", [128, 512], F32) for i in range(2)]
        W = {n: C.sb(st, "w_" + n, [128, 512], F32) for n in ["y", "sq", "t"]}
        m8 = C.sb(st, "m8", [128, 8], F32)
        q8 = C.sb(st, "q8", [128, 8], F32)
        y3 = C.sb(st, "ry3", [128, 512], BF16)
        yT = [C.sb(st, f"ryT{i}", [128, 4, 128], BF16) for i in range(2)]
        pT = C.ps(st, "spT", [128, 2048], BF16)
        pA = C.ps(st, "spA", [128, 1024], F32)
        pB = C.ps(st, "spB", [128, 1024], F32)
        pC = C.ps(st, "spC", [128, 1024], F32)
        cv = C.d["rw_c"].rearrange("(q p) s n -> p q s n", p=128)
        yav = C.d["yaT"].rearrange("(c p) l -> p c l", p=128)
        it = 0
        for dr in range(2):
            order = list(range(NT)) if dr == 0 else list(range(NT - 1, -1, -1))
            dv = C.d["rw_d"][dr].rearrange("(q p) s n -> p q s n", p=128)
            sv = C.d["rw_sg"][dr].rearrange("(q p) n -> p q n", p=128)
            yfv = C.d["rw_yf"].rearrange("(q p) n -> p q n", p=128)
            C.memset("dve", ST32[:], 0.0, ["ST32"])
            C.memset("dve", STb[:], 0.0, ["STb"])
            M = rwm[:, dr]
            INC, STR, STRT = M[:, 0, :], M[:, 1, :], M[:, 2, :]
            mask1 = M[:, 3, :]
            maskT = M[:, 4, 0:64]
            ind = M[:, 4, 64:66]
            for q in order:
                b = it % 2
                it += 1
                C.dma(ci[b][:], cv[:, q], [], [("ci", b)])
                C.dma(di[b][:], dv[:, q], [], [("di", b)])
                C.dma(sgi[b][:], sv[:, q], [], [("sgi", b)], q="act")
                if dr == 1:
                    C.dma(yfi[b][:], yfv[:, q], [("d", "rw_yf", q)], [("yfi", b)], q="act")
                r_, v_, kkn_ = ci[b][:, 0, :], ci[b][:, 1, :], ci[b][:, 2, :]
                kd_, bd_ = di[b][:, 0, :], di[b][:, 1, :]
                kci, kdi, ksg = ("ci", b), ("di", b), ("sgi", b)
                kA = [("pA", 0), ("pA", 1)]
                kB = [("pB", 0), ("pB", 1)]
                kX = [("Xs", 0), ("Xs", 1)]
                kP = lambda i: [("PP", i, 0), ("PP", i, 1)]
                C.mm(pA[:, 0:512], INC, sgi[b][:], True, True, ["rwm", ksg], [("pA", 0)])
                C.mm(pA[:, 512:1024], STR, sgi[b][:], True, True, ["rwm", ksg], [("pA", 1)])
                C.mm(pB[:, 0:512], STRT, sgi[b][:], True, True, ["rwm", ksg], [("pB", 0)])
                for h in range(8):
                    C.mm(pB[0:64, 512 + 2 * h:512 + 2 * h + 2], sgi[b][:, h * 64:(h + 1) * 64], ind, h == 0, True, [ksg, "rwm"], [("pB", 1)],
                         skip_group_check=True)
                C.act(E[:, 0, :], pA[:, 512:1024], AF.Exp, [("pA", 1)], ["E"], scale=-RW_C)
                C.act(E[:, 1, :], pA[:, 0:512], AF.Exp, [("pA", 0)], ["E"], scale=-RW_C)
                C.act(E[:, 2, :], pA[:, 0:512], AF.Exp, [("pA", 0)], ["E"], scale=RW_C)
                C.act(E[:, 3, :], pB[:, 0:512], AF.Exp, [("pB", 0)], ["E"], scale=-RW_C)
                C.act(gam[:], pB[0:64, 512:528].rearrange("p (h c) -> p h c", c=2), AF.Exp, [("pB", 1)], ["gam"], scale=-RW_C)
                C.stt(tk[:, 0, :], kkn_, -1.0, E[:, 0, :], ALU.mult, ALU.mult, [kci, "E"], ["tk"])
                C.tt("pool", tk[:, 1, :], r_, E[:, 1, :], ALU.mult, [kci, "E"], ["tk"])
                C.tt("dve", tk[:, 2, :], bd_, E[:, 2, :], ALU.mult, [kdi, "E"], ["tk"])
                C.tt("pool", tk[:, 3, :], kd_, E[:, 2, :], ALU.mult, [kdi, "E"], ["tk"])
                C.tt("dve", tk[:, 4, :], bd_, E[:, 3, :], ALU.mult, [kdi, "E"], ["tk"])
                C.tt("pool", tk[:, 5, :], kd_, E[:, 3, :], ALU.mult, [kdi, "E"], ["tk"])
                for x in range(4):
                    for hp in range(4):
                        o = pT[:, (hp * 4 + x) * 128:(hp * 4 + x + 1) * 128]
                        C.tr(o, tk[:, x, hp * 128:(hp + 1) * 128], C.identb[:], ["tk", "identb"], ["spT"])
                src4 = pT[:].rearrange("p (hp x c t) -> p hp x c t", hp=4, x=4, c=2)
                for c in range(2):
                    if c == 0:
                        C.cp("dve", fT[:, :, c, :, :], src4[:, :, :, c, :], ["spT"], ["fT"])
                    else:
                        C.act(fT[:, :, c, :, :], src4[:, :, :, c, :], AF.Copy, ["spT"], ["fT"])
                for par, c, h in [(par, c, h) for par in range(2) for c in range(2) for h in range(par, 8, 2)]:
                    hp, base = h // 2, 64 * (h % 2)
                    ps_ = slice(base, base + 64)
                    oc_ = slice(64 * c, 64 * c + 64)
                    C.mm(pA[oc_, h * 128:(h + 1) * 128], fT[ps_, hp, c, 2, :], fT[ps_, hp, c, 0:2, :], h % 4 == 0, True, ["fT"], [("pA", h // 4)],
                         skip_group_check=True)
                    C.mm(pB[oc_, h * 128:(h + 1) * 128], fT[ps_, hp, c, 3, :], fT[ps_, hp, c, 0:2, :], h % 4 == 0, True, ["fT"], [("pB", h // 4)],
                         skip_group_check=True)
                    C.mm(pC[oc_, h * 64:(h + 1) * 64], fT[ps_, hp, c, 0, :], fT[ps_, hp, c, 2, :], h == 0, True, ["fT"], ["pC"],
                         skip_group_check=True)
                C.tt("dve", NMb[:], pA[:].rearrange("p (h n) -> p h n", h=8), mask1.unsqueeze(1).broadcast_to([128, 8, 128]), ALU.mult,
                     kA + ["rwm"], ["NMb"])
                C.tt("dve", NMk[:], pB[:].rearrange("p (h n) -> p h n", h=8), mask1.unsqueeze(1).broadcast_to([128, 8, 128]), ALU.mult,
                     kB + ["rwm"], ["NMk"])
                C.cp("pool", PP[0][:, :, 0:64], NMb[:, :, 0:64], ["NMb"], kP(0))
                C.tt("dve", PP[0][:, :, 64:128], pC[:, 0:512].rearrange("p (h n) -> p h n", h=8), maskT.unsqueeze(1).broadcast_to([128, 8, 64]),
                     ALU.mult, ["pC", "rwm"], kP(0))
                for h in range(8):
                    C.mm(pA[:, h * 128:h * 128 + 64], C.identb[:], tk[:, 0, h * 64:(h + 1) * 64], h % 4 == 0, True, ["identb", "tk", "NMb"], [("pA", h // 4)],
                         skip_group_check=True)
                for c in range(2):
                    oc_ = slice(64 * c, 64 * c + 64)
                    for h in range(8):
                        C.mm(pA[oc_, h * 128 + 64:(h + 1) * 128], NMk[oc_, h, 0:64], ci[b][oc_, 1, h * 64:(h + 1) * 64], False, True,
                             ["NMk", kci], [("pA", h // 4)], skip_group_check=True)
                pA3 = pA[:].rearrange("p (h n) -> p h n", h=8)
                pB3 = pB[:].rearrange("p (h n) -> p h n", h=8)
                for g2 in range(2):
                    hs = slice(4 * g2, 4 * g2 + 4)
                    C.act(Xs[:, hs, :], pA3[:, hs, :], AF.Copy, [("pA", g2)], [("Xs", g2)])
                for rd in range(6):
                    pc_, pn_ = PP[rd % 2], PP[(rd + 1) % 2]
                    for g2 in range(2):
                        hs = slice(4 * g2, 4 * g2 + 4)
                        kc, kn = ("PP", rd % 2, g2), ("PP", (rd + 1) % 2, g2)
                        for c in range(2):
                            oc_ = slice(64 * c, 64 * c + 64)
                            for h in range(4 * g2, 4 * g2 + 4):
                                C.mm(pA[oc_, h * 128:(h + 1) * 128], pc_[oc_, h, 0:64], Xs[oc_, h, :], False, True, [kc, ("Xs", g2)], [("pA", g2)],
                                     skip_group_check=True)
                        if rd < 5:
                            for c in range(2):
                                oc_ = slice(64 * c, 64 * c + 64)
                                for h in range(4 * g2, 4 * g2 + 4):
                                    C.mm(pB[oc_, h * 128:h * 128 + 64], pc_[oc_, h, 64:128], pc_[oc_, h, 0:64], h % 4 == 0, True, [kc], [("pB", g2)],
                                         skip_group_check=True)
                                    C.mm(pB[oc_, h * 128 + 64:(h + 1) * 128], pc_[oc_, h, 0:64], pc_[oc_, h, 64:128], False, True, [kc], [("pB", g2)],
                                         skip_group_check=True)
                            C.cp("dve", pn_[:, hs, :], pB3[:, hs, :], [("pB", g2)], [kn])
                        C.act(Xs[:, hs, :], pA3[:, hs, :], AF.Copy, [("pA", g2)], [("Xs", g2)])
                for c in range(2):
                    oc_ = slice(64 * c, 64 * c + 64)
                    for h in range(8):
                        g = c * 8 + h
                        fs = (g % 8 == 0)
                        kx_ = ("Xs", h // 4)
                        C.mm(pC[0:64, g * 64:(g + 1) * 64], Xs[oc_, h, 0:64], tk[oc_, 4, h * 64:(h + 1) * 64], fs, True, [kx_, "tk"], ["pC", "pC2"],
                             skip_group_check=True)
                        C.mm(pB[0:64, g * 64:(g + 1) * 64], tk[oc_, 4, h * 64:(h + 1) * 64], Xs[oc_, h, 64:128], fs, False, [kx_, "tk"], [("pB", c)],
                             skip_group_check=True)
                        C.mm(pB[0:64, g * 64:(g + 1) * 64], tk[oc_, 5, h * 64:(h + 1) * 64], ci[b][oc_, 1, h * 64:(h + 1) * 64], False, True,
                             ["tk", kci], [("pB", c)], skip_group_check=True)
                        C.mm(pA[0:64, g * 64:(g + 1) * 64], Xs[oc_, h, 0:64], NMb[oc_, h, 64:128], fs, False, [kx_, "NMb"], [("pA", c)],
                             skip_group_check=True)
                for par, c, h in [(par, c, h) for par in range(2) for c in range(2) for h in range(par, 8, 2)]:
                    hp, base = h // 2, 64 * (h % 2)
                    ps_ = slice(base, base + 64)
                    g = c * 8 + h
                    C.mm(pA[0:64, g * 64:(g + 1) * 64], C.identb[ps_, base:base + 64], fT[ps_, hp, c, 1, :], False, True, ["identb", "fT"], [("pA", c)],
                         skip_group_check=True)
                for c in range(2):
                    C.tt("pool", dgam[:, c * 8:(c + 1) * 8, :], id64[:, 0:8, :], gam[:, :, c].unsqueeze(2).broadcast_to([64, 8, 64]), ALU.mult,
                         ["id64", "gam"], ["dgam"])
                C.tt("dve", Gs[:], dgam[:], pC[0:64, :].rearrange("p (g n) -> p g n", g=16), ALU.add, ["dgam", "pC", "pC2"], ["Gs"])
                C.act(Hs[:], pB[0:64, :].rearrange("p (g n) -> p g n", g=16), AF.Copy, kB, ["Hs"])
                C.act(Qs[:], pA[0:64, :].rearrange("p (g n) -> p g n", g=16), AF.Copy, kA, ["Qs"])
                for c in ([0, 1] if dr == 0 else [1, 0]):
                    oc_ = slice(64 * c, 64 * c + 64)
                    for h in range(8):
                        o = pC[oc_, h * 64:(h + 1) * 64]
                        C.mm(o, NMb[oc_, h, 64:128], Xs[oc_, h, 64:128], h == 0, False, ["NMb", ("Xs", h // 4), "Gs"], ["pC"], skip_group_check=True)
                        C.mm(o, NMk[oc_, h, 64:128], ci[b][oc_, 1, h * 64:(h + 1) * 64], False, False, ["NMk", kci], ["pC"], skip_group_check=True)
                        C.mm(o, Qs[:, c * 8 + h, :], STb[:, h * 64:(h + 1) * 64], False, True, ["Qs", "STb"], ["pC"], skip_group_check=True)
                    for h in range(8):
                        C.mm(pC[0:64, 512 + h * 64:512 + (h + 1) * 64], Gs[:, c * 8 + h, :], ST32[:, h * 64:(h + 1) * 64], h == 0, True,
                             ["Gs", "ST32"], ["pC2"], skip_group_check=True)
                    C.tt("dve", ST32[:], pC[0:64, 512:1024], Hs[:, c * 8:(c + 1) * 8, :], ALU.add, ["pC2", "Hs"], ["ST32"])
                    C.act(STb[:], ST32[:], AF.Copy, ["ST32"], ["STb"])
                if dr == 0:
                    ob = it % 2
                    C.act(yo[ob][:], pC[:, 0:512], AF.Copy, ["pC"], [("yo", ob)])
                    C.dma(yfv[:, q], yo[ob][:], [("yo", ob)], [("d", "rw_yf", q)], q="pool")
                    continue
                y = W["y"]
                C.tt("dve", y[:], pC[:, 0:512], yfi[b][:], ALU.add, ["pC", ("yfi", b)], ["w_y"])
                C.S.add("dve", lambda e: e.tensor_reduce(m8[:], v3(y[:]), AX.X, ALU.add), ["w_y"], ["m8"])
                C.ts("dve", m8[:], m8[:], 1.0 / 64, ALU.mult, ["m8"], ["m8"])
                C.tt("pool", v3(y[:]), v3(y[:]), bc8(m8[:]), ALU.subtract, ["w_y", "m8"], ["w_y"])
                C.tt("pool", W["sq"][:], y[:], y[:], ALU.mult, ["w_y"], ["w_sq"])
                C.S.add("dve", lambda e: e.tensor_reduce(q8[:], v3(W["sq"][:]), AX.X, ALU.add), ["w_sq"], ["q8"])
                C.ts("dve", q8[:], q8[:], 1.0 / 64, ALU.mult, ["q8"], ["q8"], s2=RW_GN_EPS, op1=ALU.add)
                C.act(q8[:], q8[:], AF.Sqrt, ["q8"], ["q8"])
                C.recip(q8[:], q8[:], ["q8"], ["q8"])
                C.tt("dve", v3(y[:]), v3(y[:]), bc8(q8[:]), ALU.mult, ["w_y", "q8"], ["w_y"])
                C.tt("pool", y[:], y[:], lnx[:, 0, :], ALU.mult, ["w_y", "lnx"], ["w_y"])
                C.tt("pool", y[:], y[:], lnx[:, 1, :], ALU.add, ["w_y", "lnx"], ["w_y"])
                C.tt("dve", y[:], y[:], ci[b][:, 4, :], ALU.add, ["w_y", kci], ["w_y"])
                C.tt("dve", y3[:], y[:], ci[b][:, 3, :], ALU.mult, ["w_y", kci], ["ry3"])
                for c4 in range(4):
                    C.tr(pT[:, c4 * 128:(c4 + 1) * 128], y3[:, c4 * 128:(c4 + 1) * 128], C.identb[:], ["ry3", "identb"], ["spT"])
                C.act(yT[b][:], pT[:, 0:512].rearrange("p (c l) -> p c l", c=4), AF.Copy, ["spT"], [("ryT", b)])
                C.dma(yav[:, :, q * 128:(q + 1) * 128], yT[b][:], [("ryT", b)], [("d", "yaT", q)], q="pool")
            C.S.barrier()
    C.S.barrier()


def stage_rwkv(C, l):
    rw_front(C, l)
    if not getattr(C, "no_rwscan", False):
        rw_scan(C, l)

def stage_out(C, l, src, dst):
    L = C.L
    TT = 256
    NT = L // TT
    with contextlib.ExitStack() as st:
        wg = C.sb(st, "wg", [128, 8, 3 * D], BF16)
        wp = C.sb(st, "wp", [128, 3, 4, D], BF16)
        wo = C.sb(st, "wout", [128, 8, D], BF16)
        with contextlib.ExitStack() as st2:
            load_w_bf16(C, st2, lambda k, a, b: wg[:, k, a:b], C.d["mix_w_in"][l][:, O_G:IN_COLS], 8, 3 * D, "wg", blk=1024)
            for br, nm in enumerate(["proj_a", "proj_b", "proj_c"]):
                load_w_bf16(C, st2, (lambda br: (lambda k, a, b: wp[:, br, k, a:b]))(br), C.d[nm][l], 4, D, "wp", blk=1024)
            load_w_bf16(C, st2, lambda k, a, b: wo[:, k, a:b], C.d["mix_w_out"][l], 8, D, "wout", blk=1024)
            C.S.barrier()
        xt = [C.sb(st, f"xt{i}", [128, 8, TT], F32) for i in range(3)]
        hTb = [C.sb(st, f"hT{i}", [128, 8, TT], BF16) for i in range(2)]
        yin = [C.sb(st, f"yin{i}", [128, 3, 4, TT], BF16) for i in range(2)]
        sg = [C.sb(st, f"sg{i}", [128, TT], F32) for i in range(4)]
        tm = [C.sb(st, f"tm{i}", [128, TT], F32) for i in range(6)]
        mT = C.sb(st, "mT", [128, 8, TT], BF16)
        zsq = C.sb(st, "zsq", [128, 8, TT], F32)
        mean = C.sb(st, "mean", [128, TT], F32)
        msq = C.sb(st, "msq", [128, TT], F32)
        rstd = C.sb(st, "rstd", [128, TT], F32)
        pgp = [C.ps(st, f"pgp{i}", [128, 512], F32) for i in range(4)]
        py = [C.ps(st, f"py{i}", [128, 512], F32) for i in range(4)]
        srcv = src.rearrange("(c p) l -> p c l", p=128)
        dstv = dst.rearrange("(c p) l -> p c l", p=128)
        yv = [C.d[n].rearrange("(c p) l -> p c l", p=128) for n in ["yaT", "ybT", "ycT"]]
        mg = C.modT[l]

        def load(t):
            sl = slice(t * TT, (t + 1) * TT)
            C.dma(xt[t % 3][:], srcv[:, :, sl], [("d", src.name, t)], [("xt", t % 3)])
            for br in range(3):
                C.dma(yin[t % 2][:, br], yv[br][:, :, sl], [], [("yin", t % 2)], q="act" if br == 1 else "sp")

        def prep(t):
            x, kx = xt[t % 3], ("xt", t % 3)
            modulate_tile(C, hTb[t % 2], x, kx, l, 3, TT, hkey=("hT", t % 2))
            C.ts("pool", x[:], x[:], float(ALPHA), ALU.mult, [kx], [kx])

        def tail_pieces(t):
            x, kx = xt[t % 3], ("xt", t % 3)
            ps = layer_norm_pieces(C, x, kx, TT, zsq, mean, msq, rstd, py[0], ("py", 0), py[1], ("py", 1), l, 1)
            ps.append(lambda: C.dma(dstv[:, :, t * TT:(t + 1) * TT], x[:], [kx], [("d", dst.name, t)], q="pool"))
            return ps

        load(0)
        prep(0)
        pend = []
        n = 0
        for t in range(NT):
            b = t % 2
            x, kx = xt[t % 3], ("xt", t % 3)
            hT, kh = hTb[b], ("hT", b)
            if t + 1 < NT:
                load(t + 1)
            for d in range(8):
                for br in range(3):
                    gb = n % 4
                    n += 1
                    tb = (d % 2) * 3 + br
                    for k in range(8):
                        C.mm(pgp[gb][:, 0:TT], wg[:, k, br * D + d * 128:br * D + (d + 1) * 128], hT[:, k, :], k == 0, k == 7,
                             ["wg", kh], [("pgp", gb)])
                    for k in range(4):
                        C.mm(pgp[gb][:, TT:2 * TT], wp[:, br, k, d * 128:(d + 1) * 128], yin[b][:, br, k, :], k == 0, k == 3,
                             ["wp", ("yin", b)], [("pgp", gb)])
                    C.act(sg[gb][:], pgp[gb][:, 0:TT], AF.Sigmoid, [("pgp", gb)], [("sg", gb)])
                    C.tt("dve", tm[tb][:], sg[gb][:], pgp[gb][:, TT:2 * TT], ALU.mult, [("sg", gb), ("pgp", gb)], [("tm", tb)])
                    if (d, br) >= (0, 1) and pend:
                        pend.pop(0)()
                    if (d, br) == (5, 0) and t + 1 < NT:
                        prep(t + 1)
                t0_ = (d % 2) * 3
                C.tt("dve", tm[t0_][:], tm[t0_][:], tm[t0_ + 1][:], ALU.add, [("tm", t0_), ("tm", t0_ + 1)], [("tm", t0_)])
                C.tt("dve", mT[:, d, :], tm[t0_][:], tm[t0_ + 2][:], ALU.add, [("tm", t0_), ("tm", t0_ + 2)], ["mT"])
            while pend:
                pend.pop(0)()
            for d in range(8):
                pyd = py[d // 2][:, (d % 2) * TT:(d % 2 + 1) * TT]
                for k in range(8):
                    C.mm(pyd, wo[:, k, d * 128:(d + 1) * 128], mT[:, k, :], k == 0, k == 7, ["wout", "mT"], [("py", d // 2)])
                C.stt(x[:, d, :], pyd, mg[:, 40 + d:40 + d + 1], x[:, d, :], ALU.mult, ALU.add, [("py", d // 2), kx, ("modT", l)], [kx])
            pend = tail_pieces(t)
        while pend:
            pend.pop(0)()
    C.S.barrier()

STAGES = ["ffn1", "inproj", "ssd", "hyena", "rwkv", "out", "ffn2"]


def build(L, upto="ffn2", depth=DEPTH, debug=()):
    C = Ctx(L)
    C.no_rwscan = 'norwscan' in debug
    dbg = lambda n: n in debug
    C.din("x", [L, D])
    C.din("c", [128, 8])
    C.din("c_ident", [128, 128])
    C.din("c_masks", [128, 6, 128])
    C.din("ada_w", [DEPTH, D, 9 * D])
    C.din("ada_b", [DEPTH, 128, 72])
    C.din("ln_w", [128, DEPTH * 3 * 8])
    C.din("ln_b", [128, DEPTH * 3 * 8])
    C.din("ffn_wi", [DEPTH, 2, D, 2 * D_FF])
    C.din("ffn_wo", [DEPTH, 2, D_FF, D])
    C.din("mix_w_in", [DEPTH, D, IN_COLS])
    C.din("ssd_cw", [DEPTH, 128, 40])
    C.din("ssd_cb", [DEPTH, 128, 8])
    C.din("ssd_dtb", [DEPTH, 128, 16])
    C.din("ssd_alog", [DEPTH, 128, 16])
    C.din("ssd_Dbc", [DEPTH, 128, 512])
    C.din("ssd_nw", [DEPTH, 128, 512])
    C.din("c_featsT", [33, L])
    C.din("c_distb", [128, L])
    C.din("c_ndelta", [128, 4])
    C.din("c_F256", [128, 512], BF16)
    C.din("c_Tw", [128, 768])
    C.din("c_L4", [128, 4, 64], BF16)
    C.din("c_G3", [128, 3, 128], BF16)
    C.din("c_TAB", [128, 2, 2, 128])
    C.din("c_FIN", [128, 2, 2, 128], BF16)
    C.din("hy_f_w1", [DEPTH, 33, 64])
    C.din("hy_f_w2", [DEPTH, 64, 64])
    C.din("hy_f_w3", [DEPTH, 64, 512])
    C.din("hy_fb", [DEPTH, 64, 2])
    C.din("hy_freq", [DEPTH, 64, 2])
    C.din("hy_bias", [DEPTH, 128, 4])
    C.din("hy_cw", [DEPTH, 128, 36])
    C.din("hy_cb", [DEPTH, 128, 12])
    C.din("proj_a", [DEPTH, 512, D])
    C.din("proj_b", [DEPTH, 512, D])
    C.din("proj_c", [DEPTH, 512, D])
    C.din("mix_w_out", [DEPTH, D, D])
    C.din("c_rwm", [2, 128, 5, 128])
    C.din("rw_mu", [DEPTH, 128, 2, 15])
    C.din("rw_w2", [DEPTH, 128, 512])
    C.din("rw_a2", [DEPTH, 128, 512])
    C.din("rw_g2", [DEPTH, 128, 512])
    C.din("rw_bc", [DEPTH, 128, 7, 512])
    C.din("rw_lnx", [DEPTH, 128, 2, 512])
    C.dout("out", [L, D])
    xa = C.dscr("xTa", [D, L], debug=dbg("xTa"))
    xb = C.dscr("xTb", [D, L], debug=dbg("xTb"))
    C.dscr("projA", [A_COLS, L], BF16, debug=dbg("projA"))
    C.dscr("projB", [B_COLS, L], BF16, debug=dbg("projB"))
    C.dscr("projX", [XBC, L], BF16, debug=dbg("projX"))
    C.dscr("z_tok", [L, 512], BF16, debug=dbg("z_tok"))
    C.dscr("dt_tok", [L, 16], F32, debug=dbg("dt_tok"))
    C.dscr("ssd_bcT", [512, L], BF16, debug=dbg("ssd_bcT"))
    C.dscr("ssd_xbtok", [L, 768], BF16, debug=dbg("ssd_xbtok"))
    C.dscr("ycT", [512, L], BF16, debug=dbg("ycT"))
    C.dscr("hy_hT", [512, L], BF16, debug=dbg("hy_hT"))
    C.dscr("hy_x0T", [512, L], BF16, debug=dbg("hy_x0T"))
    C.dscr("hy_zT", [512, L], BF16, debug=dbg("hy_zT"))
    C.dscr("hy_Hf", [256, 128, 768], BF16, debug=dbg("hy_Hf"))
    C.dscr("ybT", [512, L], BF16, debug=dbg("ybT"))
    C.dscr("rw_c", [L, 5, 512], BF16, debug=dbg("rw_c"))
    C.dscr("rw_d", [2, L, 2, 512], BF16, debug=dbg("rw_d"))
    C.dscr("rw_sg", [2, L, 512], F32, debug=dbg("rw_sg"))
    C.dscr("rw_yf", [L, 512], F32, debug=dbg("rw_yf"))
    C.dscr("yaT", [512, L], BF16, debug=dbg("yaT"))
    last = STAGES.index(upto)
    with contextlib.ExitStack() as stp:
        stage_consts(C, stp)
        stage_transpose(C, C.d["x"], xa, True)
        cur, nxt = xa, xb
        for l in range(depth):
            stage_mod(C, l, stp)
            stage_ffn(C, l, 0, cur, nxt, 0, 0)
            cur, nxt = nxt, cur
            if last >= 1:
                stage_inproj(C, l, cur)
            if last >= 2 and "nossd" not in debug:
                stage_ssd(C, l)
            if last >= 3 and "nohy" not in debug:
                stage_hyena(C, l)
            if last >= 4:
                stage_rwkv(C, l)
            if last >= 5:
                stage_out(C, l, cur, nxt)
                cur, nxt = nxt, cur
            if last >= 6:
                stage_ffn(C, l, 1, cur, nxt, 6, 2)
                cur, nxt = nxt, cur
        stage_transpose(C, cur, C.d["out"], False)
        C.S.emit()
    return C


def fm(v, ncol):
    v = np.asarray(v, np.float32)
    sh = v.shape[:-1]
    return np.ascontiguousarray(np.swapaxes(v.reshape(sh + (ncol, 128)), -1, -2))


def bc128(v):
    v = np.asarray(v, np.float32)
    return np.ascontiguousarray(np.broadcast_to(v[:, None, :], (v.shape[0], 128, v.shape[1])))


def const_masks():
    k = np.arange(128)[:, None]
    s = np.arange(128)[None, :]
    m = np.zeros((128, 6, 128), np.float32)
    m[:, 0] = k > s
    m[:, 1] = k <= s
    m[:, 2] = k < s
    m[:, 3] = k >= s
    m[:, 4] = NEG * (s < k)
    m[:, 5] = NEG * (s > k)
    return m


def hyena_consts(L):
    bf = ml_dtypes.bfloat16
    m = {}
    t = np.linspace(0.0, 1.0, L, dtype=np.float32)
    w = (2.0 * np.float32(math.pi) * np.arange(L, dtype=np.float32) / np.float32(L)).astype(np.float32)
    f = np.linspace(1e-4, 15, 16, dtype=np.float32)
    wf = (w[:, None] * f).astype(np.float32)
    feats = np.concatenate([t[:, None], np.cos(wf), np.sin(wf)], -1).astype(np.float32)
    m["c_featsT"] = np.ascontiguousarray(feats.T)
    dist = np.abs(2.0 * t - 1.0).astype(np.float32)
    m["c_distb"] = np.ascontiguousarray(np.broadcast_to(dist[None, :], (128, L)))
    deltas = np.abs(np.linspace(math.log(1e-2) / 1.5, math.log(1e-2) / 0.3, 512, dtype=np.float32))
    m["c_ndelta"] = fm(-deltas, 4)
    N = 16384
    N2 = 64
    N1 = N // N2
    n1 = np.arange(128, dtype=np.float64)[:, None]
    k1 = np.arange(N1, dtype=np.float64)[None, :]
    m["c_F256"] = np.concatenate([np.cos(2 * np.pi * n1 * k1 / N1), -np.sin(2 * np.pi * n1 * k1 / N1)], 1).astype(bf)
    n2 = np.arange(N2, dtype=np.float64)[:, None]
    Tr = np.cos(2 * np.pi * n2 * k1 / N)
    Ti = -np.sin(2 * np.pi * n2 * k1 / N)
    T = np.concatenate([Tr, Ti, Tr], 1)
    m["c_Tw"] = np.concatenate([T, T], 0).astype(np.float32)
    k2 = np.arange(N2, dtype=np.float64)[None, :]
    c64 = np.cos(2 * np.pi * n2 * k2 / N2)
    s64 = np.sin(2 * np.pi * n2 * k2 / N2)
    L4 = np.stack([c64, -c64, s64, -s64], 1)
    m["c_L4"] = np.concatenate([L4, L4], 0).astype(bf)
    G = np.stack([np.concatenate([c64, s64], 1), np.concatenate([-c64, -s64], 1), np.concatenate([-s64, c64], 1)], 1)
    m["c_G3"] = np.concatenate([G, G], 0).astype(bf)
    kk = np.arange(N1, dtype=np.float64)[:, None]
    nn = np.arange(N2, dtype=np.float64)[None, :]
    tc = np.cos(2 * np.pi * kk * nn / N)
    ts = np.sin(2 * np.pi * kk * nn / N)
    TA = np.concatenate([tc, -ts], 1).reshape(2, 128, 128)
    TB = np.concatenate([ts, tc], 1).reshape(2, 128, 128)
    m["c_TAB"] = np.ascontiguousarray(np.transpose(np.stack([TA, TB], 0), (2, 0, 1, 3))).astype(np.float32)
    n1w = np.arange(N1 // 4, N1 // 4 + 128, dtype=np.float64)[None, :]
    cf = (np.cos(2 * np.pi * kk * n1w / N1) / N).reshape(2, 128, 128)
    sf = (-np.sin(2 * np.pi * kk * n1w / N1) / N).reshape(2, 128, 128)
    m["c_FIN"] = np.ascontiguousarray(np.transpose(np.stack([cf, sf], 1), (2, 0, 1, 3))).astype(bf)
    return m


def rwkv_masks():
    m = np.zeros((2, 128, 5, 128), np.float32)
    s = np.arange(64)[:, None]
    t = np.arange(64)[None, :]
    for d in range(2):
        inc = (s <= t) if d == 0 else (s >= t)
        strict = (s < t) if d == 0 else (s > t)
        inc = inc.astype(np.float32)
        strict = strict.astype(np.float32)
        for c in range(2):
            sl = slice(64 * c, 64 * c + 64)
            m[d, sl, 0, sl] = inc
            m[d, sl, 1, sl] = strict
            m[d, sl, 2, sl] = strict.T
            m[d, sl, 3, 0:64] = strict
            m[d, sl, 3, 64:128] = inc
            m[d, sl, 4, 0:64] = strict.T
            m[d, sl, 4, 64 + c] = 1.0
    return m


def make_shared(inp, L):
    m = hyena_consts(L)
    m["c_rwm"] = rwkv_masks()
    for nm in ["proj_a", "proj_b", "proj_c", "mix_w_out"]:
        m[nm] = np.ascontiguousarray(np.asarray(inp[nm], np.float32))
    mu = fm(inp["rw_mu"], 15)
    m["rw_mu"] = np.ascontiguousarray(np.transpose(mu, (0, 2, 1, 3)))
    m["rw_w2"] = np.ascontiguousarray(np.asarray(inp["rw_w2"], np.float32).reshape(DEPTH, 128, 512))
    m["rw_a2"] = np.ascontiguousarray(np.asarray(inp["rw_a2"], np.float32).reshape(DEPTH, 128, 512))
    m["rw_g2"] = np.ascontiguousarray(np.asarray(inp["rw_g2"], np.float32))
    w0 = np.asarray(inp["rw_w0"], np.float32)
    a0 = np.asarray(inp["rw_a0"], np.float32)
    vecs = np.stack([w0[:, 0], w0[:, 1], a0[:, 0], a0[:, 1], np.asarray(inp["rw_kk"], np.float32), np.asarray(inp["rw_ka"], np.float32),
                     np.asarray(inp["rw_rk"], np.float32).reshape(DEPTH, 512)], 1)
    m["rw_bc"] = np.ascontiguousarray(np.broadcast_to(vecs[:, None], (DEPTH, 128, 7, 512)))
    lnx = np.stack([np.asarray(inp["rw_lnx_w"], np.float32), np.asarray(inp["rw_lnx_b"], np.float32)], 1)
    m["rw_lnx"] = np.ascontiguousarray(np.broadcast_to(lnx[:, None], (DEPTH, 128, 2, 512)))
    f32 = lambda a: np.ascontiguousarray(np.asarray(a, dtype=np.float32))
    m["hy_f_w1"] = f32(inp["hy_f_w1"])
    m["hy_f_w2"] = f32(inp["hy_f_w2"])
    m["hy_f_w3"] = f32(inp["hy_f_w3"])
    m["hy_fb"] = f32(np.stack([np.asarray(inp["hy_f_b1"]), np.asarray(inp["hy_f_b2"])], -1))
    m["hy_freq"] = f32(np.swapaxes(np.asarray(inp["hy_freq"]), 1, 2))
    m["hy_bias"] = fm(inp["hy_bias"], 4)
    hcw = fm(inp["hy_conv_w"], 12)
    m["hy_cw"] = np.ascontiguousarray(np.transpose(hcw, (0, 2, 3, 1)).reshape(DEPTH, 128, 36))
    m["hy_cb"] = fm(inp["hy_conv_b"], 12)
    m["c_ident"] = np.eye(128, dtype=np.float32)
    m["c_masks"] = const_masks()
    m["ada_w"] = f32(inp["ada_w"])
    m["ada_b"] = fm(inp["ada_b"], 72)
    lw = fm(np.asarray(inp["ln_w"]).reshape(DEPTH * 3, D), 8)
    lb = fm(np.asarray(inp["ln_b"]).reshape(DEPTH * 3, D), 8)
    m["ln_w"] = np.ascontiguousarray(np.transpose(lw, (1, 0, 2)).reshape(128, -1))
    m["ln_b"] = np.ascontiguousarray(np.transpose(lb, (1, 0, 2)).reshape(128, -1))
    m["ffn_wi"] = f32(inp["ffn_wi"])
    m["ffn_wo"] = f32(inp["ffn_wo"])
    m["mix_w_in"] = f32(inp["mix_w_in"])
    cw = fm(inp["ssd_conv_w"], 8)
    m["ssd_cw"] = np.ascontiguousarray(np.transpose(cw, (0, 2, 3, 1)).reshape(DEPTH, 128, 40))
    m["ssd_cb"] = fm(inp["ssd_conv_b"], 8)
    m["ssd_dtb"] = bc128(np.asarray(inp["ssd_dt_bias"]).reshape(DEPTH, 16))
    m["ssd_alog"] = bc128(np.asarray(inp["ssd_A_log"]).reshape(DEPTH, 16))
    m["ssd_Dbc"] = bc128(np.repeat(np.asarray(inp["ssd_D"]), 64, axis=-1))
    m["ssd_nw"] = bc128(inp["ssd_norm_w"])
    return m


def make_inputs(inp, b, shared=None):
    m = dict(shared if shared is not None else make_shared(inp, inp['x'].shape[1]))
    m["x"] = np.ascontiguousarray(np.asarray(inp["x"][b], dtype=np.float32))
    m["c"] = fm(inp["c"][b], 8)
    return m


def kernel(**inputs):
    inp = {k: np.asarray(v) for k, v in inputs.items()}
    B, L, _ = inp["x"].shape
    C = build(L)
    shared = make_shared(inp, L)
    in_maps = [make_inputs(inp, b, shared) for b in range(B)]
    res = run_bass_kernel_spmd(C.nc, in_maps, core_ids=list(range(B)))
    return np.stack([r["out"] for r in res.results], axis=0).astype(np.float32)
```

```python
import contextlib
import math
import numpy as np
import ml_dtypes
import concourse.bass as bass
import concourse.mybir as mybir
from concourse.bass_utils import run_bass_kernel_spmd

F32 = mybir.dt.float32
BF16 = mybir.dt.bfloat16
AF = mybir.ActivationFunctionType
ALU = mybir.AluOpType
AX = mybir.AxisListType

D = 1024
DEPTH = 2
ALPHA = (2 * DEPTH) ** 0.25
LN_EPS = 1e-5
D_FF = 2816
NFC = D_FF // 128

ENGS = ["pe", "act", "dve", "pool", "sp"]
NDSEM = 12


class Op:
    __slots__ = ("eng", "fn", "idx", "dma", "waits", "has_dep", "ms", "dsem", "dval", "prewait")

    def __init__(self, eng, fn, dma):
        self.eng = eng
        self.fn = fn
        self.dma = dma
        self.waits = []
        self.has_dep = False
        self.ms = None
        self.dsem = None
        self.dval = None
        self.prewait = None


class Sched:
    def __init__(self, nc, same_engine_sync=True):
        self.nc = nc
        self.ops = {e: [] for e in ENGS}
        self.last_w = {}
        self.readers = {}
        self.known = {e: {f: -1 for f in ENGS} for e in ENGS}
        self.known_dma = {e: set() for e in ENGS}
        self.ndma = {e: 0 for e in ENGS}
        self.dma_hist = {e: [] for e in ENGS}
        self.same = same_engine_sync
        self.all_dma = []
        self.pending_barrier = {e: None for e in ENGS}
        self.excl_last = {}
        self.pe_rg = {}

    def add(self, eng, fn, reads=(), writes=(), dma=False, excl=(), rowid=None, rgkeys=()):
        op = Op(eng, fn, dma)
        op.idx = len(self.ops[eng])
        deps = []
        for k in excl:
            d_ = self.excl_last.setdefault(k, {})
            for e2, o in d_.items():
                if e2 != eng:
                    deps.append((o, "excl"))
            d_[eng] = op
        if eng == "pe" and rowid is not None:
            for k in rgkeys:
                prev = self.pe_rg.get(k)
                if prev is not None and prev[0] != rowid:
                    o = prev[1]
                    if self.known["pe"]["pe"] < o.idx:
                        self.known["pe"]["pe"] = o.idx
                        o.has_dep = True
                        op.waits.append(o)
                self.pe_rg[k] = (rowid, op)
        for k in reads:
            w = self.last_w.get(k)
            if w is not None:
                deps.append((w, "raw"))
        for k in writes:
            w = self.last_w.get(k)
            if w is not None:
                deps.append((w, "waw"))
            r = self.readers.get(k)
            if r:
                for o in r[0].values():
                    deps.append((o, "war"))
                for o in r[1]:
                    deps.append((o, "war"))
        pb = self.pending_barrier[eng]
        if pb is not None:
            for o in pb:
                deps.append((o, "bar"))
            self.pending_barrier[eng] = None
        for d, kind in deps:
            self._need(op, d, kind)
        for k in reads:
            r = self.readers.setdefault(k, ({}, []))
            if dma:
                r[1].append(op)
            else:
                r[0][eng] = op
        for k in writes:
            self.last_w[k] = op
            self.readers[k] = ({}, [])
        if dma:
            n = self.ndma[eng]
            self.ndma[eng] = n + 1
            nd = 4 if eng == "pool" else NDSEM
            op.dsem = n % nd
            op.dval = 16 * (n // nd + 1)
            if n >= nd:
                prev = self.dma_hist[eng][n - nd]
                if prev not in self.known_dma[eng]:
                    op.prewait = prev
                    self.known_dma[eng].add(prev)
            self.dma_hist[eng].append(op)
            self.all_dma.append(op)
        self.ops[eng].append(op)
        return op

    def _need(self, op, d, kind):
        e = op.eng
        if d is op:
            return
        if d.dma:
            if d in self.known_dma[e]:
                return
            self.known_dma[e].add(d)
            op.waits.append(d)
            return
        if d.eng == e:
            if e == "pe" or kind in ("war", "waw") or e not in self.same:
                return
        if self.known[e][d.eng] >= d.idx:
            return
        self.known[e][d.eng] = d.idx
        d.has_dep = True
        op.waits.append(d)

    def barrier(self):
        lst = []
        for e in ENGS:
            for o in reversed(self.ops[e]):
                if not o.dma and o.fn is not None:
                    lst.append(o)
                    break
        lst.extend(self.all_dma)
        self.all_dma = []
        for e in ENGS:
            prev = self.pending_barrier[e]
            self.pending_barrier[e] = (list(prev) if prev else []) + lst
        self.last_w = {}
        self.readers = {}

    def emit(self):
        nc = self.nc
        self.barrier()
        self.add("sp", None)
        for e in ENGS:
            c = 0
            for o in self.ops[e]:
                if not o.dma and o.has_dep:
                    c += 1
                    o.ms = c
            print(f"[sched] {e}: {len(self.ops[e])} ops, {c} milestones, {self.ndma[e]} dmas", flush=True)
        with contextlib.ExitStack() as st:
            csem = {e: st.enter_context(nc.semaphore(f"c_{e}")) for e in ENGS}
            dsem = {e: [st.enter_context(nc.semaphore(f"d_{e}_{i}")) for i in range(NDSEM)]
                    for e in ENGS if self.ndma[e] > 0}
            block = st.enter_context(nc.Block())

            def run(e, eng):
                for o in self.ops[e]:
                    if o.prewait is not None:
                        p = o.prewait
                        eng.wait_ge(dsem[p.eng][p.dsem], p.dval)
                    for d in o.waits:
                        if d.dma:
                            eng.wait_ge(dsem[d.eng][d.dsem], d.dval)
                        else:
                            eng.wait_ge(csem[d.eng], d.ms)
                    if o.fn is None:
                        continue
                    ins = o.fn(eng)
                    if o.dma:
                        ins.then_inc(dsem[e][o.dsem], 16)
                    elif o.ms is not None:
                        ins.then_inc(csem[e], 1)

            @block.tensor
            def _(eng):
                run("pe", eng)

            @block.scalar
            def _(eng):
                run("act", eng)

            @block.vector
            def _(eng):
                run("dve", eng)

            @block.gpsimd
            def _(eng):
                run("pool", eng)

            @block.sync
            def _(eng):
                run("sp", eng)


class Ctx:
    def __init__(self, L):
        self.L = L
        self.nc = bass.Bass("TRN2", target_bir_lowering=False)
        import os
        self.S = Sched(self.nc, same_engine_sync=os.environ.get("SAME", "dve,pool").split(","))
        self.d = {}
        self.uid = 0

    def din(self, name, shape, dt=F32):
        self.d[name] = self.nc.dram_tensor(name, list(shape), dt, kind="ExternalInput").ap()
        return self.d[name]

    def dout(self, name, shape, dt=F32):
        self.d[name] = self.nc.dram_tensor(name, list(shape), dt, kind="ExternalOutput").ap()
        return self.d[name]

    def dscr(self, name, shape, dt=F32, debug=False):
        kind = "ExternalOutput" if debug else "Internal"
        self.d[name] = self.nc.dram_tensor(name, list(shape), dt, kind=kind).ap()
        return self.d[name]

    def sb(self, st, name, shape, dt):
        self.uid += 1
        return st.enter_context(self.nc.sbuf_tensor(f"{name}_{self.uid}", list(shape), dt))

    def ps(self, st, name, shape, dt=F32):
        self.uid += 1
        return st.enter_context(self.nc.psum_tensor(f"{name}_{self.uid}", list(shape), dt))

    @staticmethod
    def pk(*aps):
        keys = []
        for ap in aps:
            if ap is None or not hasattr(ap, "space") or str(ap.space) != "PSUM":
                continue
            pat = ap.ap
            row = pat[0][0]
            col0 = ap.offset % row
            ext = 0
            for stp, cnt in pat[1:]:
                ext += abs(stp) * (cnt - 1)
            es = 2 if ap.dtype == BF16 else 4
            b0 = (col0 * es) // 2048
            b1 = ((col0 + ext) * es + es - 1) // 2048
            for bk in range(b0, b1 + 1):
                keys.append(("PS", ap.tensor.name, bk))
        return keys

    @staticmethod
    def rowid(ap):
        pat = ap.ap
        row = pat[0][0]
        return (ap.offset // row, pat[0][1])

    @classmethod
    def pkq(cls, ap):
        pat = ap.ap
        row = pat[0][0]
        p0 = ap.offset // row
        p1 = p0 + pat[0][1] - 1
        return [(k, q) for k in cls.pk(ap) for q in range(p0 // 32, p1 // 32 + 1)]

    def mm(self, out, lhsT, rhs, start, stop, r, w, **kw):
        return self.S.add("pe", lambda e: e.matmul(out, lhsT, rhs, start=start, stop=stop, **kw), r, w,
                          excl=self.pk(out), rowid=self.rowid(lhsT), rgkeys=self.pkq(out))

    def tr(self, out, in_, ident, r, w):
        return self.S.add("pe", lambda e: e.transpose(out, in_, ident), r, w, excl=self.pk(out), rowid=self.rowid(in_), rgkeys=self.pkq(out))

    def act(self, out, in_, func, r, w, bias=None, scale=None, eng="act"):
        kw = {}
        if bias is not None:
            kw["bias"] = bias
        if scale is not None:
            kw["scale"] = scale
        return self.S.add(eng, lambda e: e.activation(out, in_, func, **kw), r, w, excl=self.pk(out, in_))

    def raw(self, eng, fn, r, w, aps=()):
        return self.S.add(eng, fn, r, w, excl=self.pk(*aps))

    def tt(self, eng, out, in0, in1, op, r, w):
        return self.S.add(eng, lambda e: e.tensor_tensor(out, in0, in1, op), r, w, excl=self.pk(out, in0, in1))

    def ts(self, eng, out, in0, s1, op0, r, w, s2=None, op1=None):
        if op1 is None:
            return self.S.add(eng, lambda e: e.tensor_scalar(out, in0, s1, None, op0), r, w, excl=self.pk(out, in0))
        return self.S.add(eng, lambda e: e.tensor_scalar(out, in0, s1, s2, op0, op1), r, w, excl=self.pk(out, in0))

    def stt(self, out, in0, scalar, in1, op0, op1, r, w):
        return self.S.add("dve", lambda e: e.scalar_tensor_tensor(out, in0, scalar, in1, op0, op1), r, w,
                          excl=self.pk(out, in0, in1))

    def cp(self, eng, out, in_, r, w):
        return self.S.add(eng, lambda e: e.tensor_copy(out, in_), r, w, excl=self.pk(out, in_))

    def memset(self, eng, ap, val, w):
        return self.S.add(eng, lambda e: e.memset(ap, val), (), w, excl=self.pk(ap))

    def recip(self, out, in_, r, w):
        return self.S.add("dve", lambda e: e.reciprocal(out, in_), r, w, excl=self.pk(out, in_))

    def dma(self, out, in_, r, w, q="sp", **kw):
        return self.S.add(q, lambda e: e.dma_start(out, in_, **kw), r, w, dma=True)


def stage_consts(C, st):
    nc = C.nc
    C.ident = C.sb(st, "ident", [128, 128], F32)
    C.identb = C.sb(st, "identb", [128, 128], BF16)
    C.ones = C.sb(st, "ones", [128, 128], F32)
    C.dma(C.ident[:], C.d["c_ident"][:, :], ["d:c_ident"], ["ident"])
    C.cp("dve", C.identb[:], C.ident[:], ["ident"], ["identb"])
    C.memset("dve", C.ones[:], 1.0, ["ones"])


def stage_transpose(C, src, dst, to_feature_major):
    L = C.L
    with contextlib.ExitStack() as st:
        ib = [C.sb(st, f"ti{i}", [128, 1024], F32) for i in range(2)]
        ob = [C.sb(st, f"to{i}", [128, 1024], F32) for i in range(2)]
        pp = [C.ps(st, f"tp{i}", [128, 512], F32) for i in range(4)]
        if to_feature_major:
            dstv = dst.rearrange("(c p) l -> p c l", p=128)
            for t in range(L // 128):
                b = t % 2
                C.dma(ib[b][:], src[t * 128:(t + 1) * 128, :], [("d", src.name, t)], [("ti", b)])
                for c in range(8):
                    p = pp[(c // 4 + 2 * b) % 4]
                    C.tr(p[:, (c % 4) * 128:(c % 4 + 1) * 128], ib[b][:, c * 128:(c + 1) * 128], C.ident[:],
                         [("ti", b), "ident"], [("tp", (c // 4 + 2 * b) % 4)])
                for h in range(2):
                    k = (h + 2 * b) % 4
                    C.cp("dve" if h == 0 else "act", ob[b][:, h * 512:(h + 1) * 512], pp[k][:], [("tp", k)], [("to", b)]) \
                        if h == 0 else C.act(ob[b][:, h * 512:(h + 1) * 512], pp[k][:], AF.Copy, [("tp", k)], [("to", b)])
                C.dma(dstv[:, :, t * 128:(t + 1) * 128], ob[b][:].rearrange("p (c l) -> p c l", c=8),
                      [("to", b)], [("d", dst.name, t)], q="pool")
        else:
            srcv = src.rearrange("(c p) l -> p c l", p=128)
            for t in range(L // 128):
                b = t % 2
                C.dma(ib[b][:].rearrange("p (c l) -> p c l", c=8), srcv[:, :, t * 128:(t + 1) * 128],
                      [("d", src.name, t)], [("ti", b)])
                for c in range(8):
                    k = (c // 4 + 2 * b) % 4
                    C.tr(pp[k][:, (c % 4) * 128:(c % 4 + 1) * 128], ib[b][:, c * 128:(c + 1) * 128], C.ident[:],
                         [("ti", b), "ident"], [("tp", k)])
                for h in range(2):
                    k = (h + 2 * b) % 4
                    if h == 0:
                        C.cp("dve", ob[b][:, h * 512:(h + 1) * 512], pp[k][:], [("tp", k)], [("to", b)])
                    else:
                        C.act(ob[b][:, h * 512:(h + 1) * 512], pp[k][:], AF.Copy, [("tp", k)], [("to", b)])
                C.dma(dst[t * 128:(t + 1) * 128, :], ob[b][:], [("to", b)], [("d", dst.name, t)], q="pool")
    C.S.barrier()


def stage_mod(C, l, st_persist):
    if not hasattr(C, "modT"):
        C.modT = [C.sb(st_persist, f"modT{i}", [128, 72], F32) for i in range(DEPTH)]
        C.mod1p = [C.sb(st_persist, f"mod1p{i}", [128, 72], F32) for i in range(DEPTH)]
        C.modhg = [C.sb(st_persist, f"modhg{i}", [128, 72], F32) for i in range(DEPTH)]
        C.lnw = C.sb(st_persist, "lnw", [128, DEPTH * 3 * 8], F32)
        C.lnb = C.sb(st_persist, "lnb", [128, DEPTH * 3 * 8], F32)
        C.dma(C.lnw[:], C.d["ln_w"][:, :], [], ["lnw"])
        C.dma(C.lnb[:], C.d["ln_b"][:, :], [], ["lnb"])
    with contextlib.ExitStack() as st:
        cond = C.sb(st, "cond", [128, 8], F32)
        adab = C.sb(st, "adab", [128, 72], F32)
        wb = [C.sb(st, f"adaw{i}", [128, 8, 1024], F32) for i in range(2)]
        pm = C.ps(st, "pmod", [128, 512], F32)
        C.dma(cond[:], C.d["c"][:, :], [], ["cond"])
        C.dma(adab[:], C.d["ada_b"][l], [], ["adab"])
        C.act(cond[:], cond[:], AF.Silu, ["cond"], ["cond"])
        wv = C.d["ada_w"][l].rearrange("(c p) n -> p c n", p=128)
        for blk in range(9):
            b = blk % 2
            C.dma(wb[b][:], wv[:, :, blk * 1024:(blk + 1) * 1024], [], [("adaw", b)], q="sp" if b == 0 else "pool")
            for j in range(8):
                col = blk * 8 + j
                for k in range(8):
                    C.mm(pm[:, col:col + 1], wb[b][:, k, j * 128:(j + 1) * 128], cond[:, k:k + 1], k == 0, k == 7,
                         [("adaw", b), "cond"], ["pmod"])
        C.tt("dve", C.modT[l][:], pm[:, 0:72], adab[:], ALU.add, ["pmod", "adab"], [("modT", l)])
        C.ts("dve", C.mod1p[l][:], C.modT[l][:], 1.0, ALU.add, [("modT", l)], [("mod1p", l)])
        C.ts("dve", C.modhg[l][:], C.modT[l][:], 0.5, ALU.mult, [("modT", l)], [("modhg", l)])
    C.S.barrier()


def load_w_bf16(C, st, dst_fn, src, nk, ncols, key, blk=1408):
    stg = [C.sb(st, f"stg{i}", [128, blk], F32) for i in range(3)]
    i = 0
    for k in range(nk):
        for c0 in range(0, ncols, blk):
            c1 = min(ncols, c0 + blk)
            b = i % 3
            C.dma(stg[b][:, 0:c1 - c0], src[k * 128:(k + 1) * 128, c0:c1], [], [("stg", b)],
                  q="sp" if i % 2 == 0 else "act")
            C.cp("pool", dst_fn(k, c0, c1), stg[b][:, 0:c1 - c0], [("stg", b)], [key])
            i += 1


def stage_ffn(C, l, which, src, dst, mi, lni):
    L = C.L
    TT = 256
    NT = L // TT
    with contextlib.ExitStack() as st:
        wi = C.sb(st, "wi", [128, 8, 2 * D_FF], BF16)
        wo = C.sb(st, "wo", [128, NFC, D], BF16)
        with contextlib.ExitStack() as st2:
            load_w_bf16(C, st2, lambda k, a, b: wi[:, k, a:b], C.d["ffn_wi"][l, which], 8, 2 * D_FF, "wi")
            load_w_bf16(C, st2, lambda k, a, b: wo[:, k, a:b], C.d["ffn_wo"][l, which], NFC, D, "wo", blk=1024)
            C.S.barrier()
        xt = [C.sb(st, f"xt{i}", [128, 8, TT], F32) for i in range(3)]
        hTb = [C.sb(st, f"hT{i}", [128, 8, TT], BF16) for i in range(2)]
        sg = [C.sb(st, f"sg{i}", [128, TT], F32) for i in range(4)]
        actT = C.sb(st, "actT", [128, NFC, TT], BF16)
        zsq = C.sb(st, "zsq", [128, 8, TT], F32)
        mean = C.sb(st, "mean", [128, TT], F32)
        msq = C.sb(st, "msq", [128, TT], F32)
        rstd = C.sb(st, "rstd", [128, TT], F32)
        pgu = [C.ps(st, f"pgu{i}", [128, 512], F32) for i in range(4)]
        py = [C.ps(st, f"py{i}", [128, 512], F32) for i in range(4)]
        srcv = src.rearrange("(c p) l -> p c l", p=128)
        dstv = dst.rearrange("(c p) l -> p c l", p=128)
        mhg = C.modhg[l]

        def load(t):
            C.dma(xt[t % 3][:], srcv[:, :, t * TT:(t + 1) * TT], [("d", src.name, t)], [("xt", t % 3)])

        def prep(t):
            x, kx = xt[t % 3], ("xt", t % 3)
            modulate_tile(C, hTb[t % 2], x, kx, l, mi, TT, hkey=("hT", t % 2))
            C.ts("pool", x[:], x[:], float(ALPHA), ALU.mult, [kx], [kx])

        def tail_pieces(t):
            x, kx = xt[t % 3], ("xt", t % 3)
            ps = layer_norm_pieces(C, x, kx, TT, zsq, mean, msq, rstd, py[0], ("py", 0), py[1], ("py", 1), l, lni)
            ps.append(lambda: C.dma(dstv[:, :, t * TT:(t + 1) * TT], x[:], [kx], [("d", dst.name, t)], q="pool"))
            return ps

        load(0)
        prep(0)
        pend = []
        for t in range(NT):
            x, kx = xt[t % 3], ("xt", t % 3)
            hT, kh = hTb[t % 2], ("hT", t % 2)
            if t + 1 < NT:
                load(t + 1)
            for f in range(NFC):
                fb = (t * NFC + f) % 4
                for k in range(8):
                    C.mm(pgu[fb][:, 0:TT], wi[:, k, f * 128:(f + 1) * 128], hT[:, k, :], k == 0, k == 7,
                         ["wi", kh], [("pgu", fb)])
                for k in range(8):
                    C.mm(pgu[fb][:, TT:2 * TT], wi[:, k, D_FF + f * 128:D_FF + (f + 1) * 128], hT[:, k, :], k == 0, k == 7,
                         ["wi", kh], [("pgu", fb)])
                C.act(sg[fb][:], pgu[fb][:, 0:TT], AF.Silu, [("pgu", fb)], [("sg", fb)])
                C.tt("dve", actT[:, f, :], sg[fb][:], pgu[fb][:, TT:2 * TT], ALU.mult, [("sg", fb), ("pgu", fb)], ["actT"])
                if f >= 1 and pend:
                    pend.pop(0)()
                if f == 16 and t + 1 < NT:
                    prep(t + 1)
            while pend:
                pend.pop(0)()
            for d in range(8):
                pyd = py[d // 2][:, (d % 2) * TT:(d % 2 + 1) * TT]
                for f in range(NFC):
                    C.mm(pyd, wo[:, f, d * 128:(d + 1) * 128], actT[:, f, :], f == 0, f == NFC - 1,
                         ["wo", "actT"], [("py", d // 2)])
                C.stt(x[:, d, :], pyd, mhg[:, 8 * (mi + 2) + d:8 * (mi + 2) + d + 1], x[:, d, :], ALU.mult, ALU.add,
                      [("py", d // 2), kx, ("modhg", l)], [kx])
            pend = tail_pieces(t)
        while pend:
            pend.pop(0)()
    C.S.barrier()


def layer_norm_pieces(C, x, kx, TT, zsq, mean, msq, rstd, ps1, k1, ps2, k2, l, lni):
    o = (l * 3 + lni) * 8
    P = []
    P.append(lambda: C.act(zsq[:], x[:], AF.Square, [kx], ["zsq"]))

    def stats():
        for d in range(8):
            C.mm(ps1[:, 0:TT], C.ones[:], x[:, d, :], d == 0, d == 7, ["ones", kx], [k1])
        for d in range(8):
            C.mm(ps2[:, 0:TT], C.ones[:], zsq[:, d, :], d == 0, d == 7, ["ones", "zsq"], [k2])
        C.ts("dve", mean[:], ps1[:, 0:TT], 1.0 / D, ALU.mult, [k1], ["mean"])
        C.tt("dve", msq[:], mean[:], mean[:], ALU.mult, ["mean"], ["msq"])
        C.stt(msq[:], ps2[:, 0:TT], 1.0 / D, msq[:], ALU.mult, ALU.subtract, [k2, "msq"], ["msq"])
        C.ts("dve", msq[:], msq[:], float(LN_EPS), ALU.add, ["msq"], ["msq"])
    P.append(stats)

    def rs():
        C.act(msq[:], msq[:], AF.Sqrt, ["msq"], ["msq"])
        C.recip(rstd[:], msq[:], ["msq"], ["rstd"])
    P.append(rs)
    for d in range(8):
        def nrm(d=d):
            C.tt("dve", x[:, d, :], x[:, d, :], mean[:], ALU.subtract, [kx, "mean"], [kx])
            C.tt("pool", x[:, d, :], x[:, d, :], rstd[:], ALU.mult, [kx, "rstd"], [kx])
            C.act(x[:, d, :], x[:, d, :], AF.Identity, [kx, "lnw", "lnb"], [kx],
                  bias=C.lnb[:, o + d:o + d + 1], scale=C.lnw[:, o + d:o + d + 1])
        P.append(nrm)
    return P


def layer_norm_fm(C, x, kx, TT, zsq, mean, msq, rstd, ps1, k1, ps2, k2, l, lni):
    C.act(zsq[:], x[:], AF.Square, [kx], ["zsq"])
    for d in range(8):
        C.mm(ps1[:, 0:TT], C.ones[:], x[:, d, :], d == 0, d == 7, ["ones", kx], [k1])
    for d in range(8):
        C.mm(ps2[:, 0:TT], C.ones[:], zsq[:, d, :], d == 0, d == 7, ["ones", "zsq"], [k2])
    C.ts("dve", mean[:], ps1[:, 0:TT], 1.0 / D, ALU.mult, [k1], ["mean"])
    C.tt("dve", msq[:], mean[:], mean[:], ALU.mult, ["mean"], ["msq"])
    C.stt(msq[:], ps2[:, 0:TT], 1.0 / D, msq[:], ALU.mult, ALU.subtract, [k2, "msq"], ["msq"])
    C.ts("dve", msq[:], msq[:], float(LN_EPS), ALU.add, ["msq"], ["msq"])
    C.act(msq[:], msq[:], AF.Sqrt, ["msq"], ["msq"])
    C.recip(rstd[:], msq[:], ["msq"], ["rstd"])
    o = (l * 3 + lni) * 8
    for d in range(8):
        C.tt("dve", x[:, d, :], x[:, d, :], mean[:], ALU.subtract, [kx, "mean"], [kx])
        C.tt("pool", x[:, d, :], x[:, d, :], rstd[:], ALU.mult, [kx, "rstd"], [kx])
        C.act(x[:, d, :], x[:, d, :], AF.Identity, [kx, "lnw", "lnb"], [kx],
              bias=C.lnb[:, o + d:o + d + 1], scale=C.lnw[:, o + d:o + d + 1])


RW_DIM = 512
A_COLS = 1920
B_COLS = 1536
XBC = 1024
O_B = A_COLS
O_Z = A_COLS + B_COLS
O_X = O_Z + 512
O_DT = O_X + XBC
O_G = O_DT + 16
IN_COLS = O_G + 3 * D
NEG = -30000.0


def modulate_tile(C, hT, x, kx, l, mi, TT, hkey="hT"):
    for c in range(8):
        C.act(hT[:, c, :], x[:, c, :], AF.Identity, [kx, ("modT", l), ("mod1p", l)], [hkey],
              bias=C.modT[l][:, 8 * mi + c:8 * mi + c + 1],
              scale=C.mod1p[l][:, 8 * (mi + 1) + c:8 * (mi + 1) + c + 1])


def stage_inproj(C, l, src):
    L = C.L
    TT = 256
    NT = L // TT
    pa, pb, px, zt, dtt = C.d["projA"], C.d["projB"], C.d["projX"], C.d["z_tok"], C.d["dt_tok"]
    with contextlib.ExitStack() as st:
        w = C.sb(st, "win", [128, 8, O_G], BF16)
        with contextlib.ExitStack() as st2:
            load_w_bf16(C, st2, lambda k, a, b: w[:, k, a:b], C.d["mix_w_in"][l][:, 0:O_G], 8, O_G, "win", blk=1252)
            C.S.barrier()
        xt = [C.sb(st, f"xt{i}", [128, 8, TT], F32) for i in range(2)]
        hTb = [C.sb(st, f"hT{i}", [128, 8, TT], BF16) for i in range(2)]
        pj = [C.sb(st, f"pj{i}", [128, 35, TT], BF16) for i in range(2)]
        zs = [C.sb(st, f"zs{i}", [128, 2, 512], BF16) for i in range(2)]
        ds = [C.sb(st, f"ds{i}", [128, 2, 16], F32) for i in range(2)]
        pp = [C.ps(st, f"pp{i}", [128, 512], F32) for i in range(4)]
        pz = [C.ps(st, f"pz{i}", [128, 512], F32) for i in range(2)]
        pd = C.ps(st, "pd", [128, 512], F32)
        srcv = src.rearrange("(c p) l -> p c l", p=128)
        cols = [j * 128 for j in range(15)] + [O_B + j * 128 for j in range(12)] + [O_X + j * 128 for j in range(8)]
        C.dma(xt[0][:], srcv[:, :, 0:TT], [("d", src.name, 0)], [("xt", 0)])
        modulate_tile(C, hTb[0], xt[0], ("xt", 0), l, 3, TT, hkey=("hT", 0))
        for t in range(NT):
            b = t % 2
            hT = hTb[b]
            if t + 1 < NT:
                C.dma(xt[1 - b][:], srcv[:, :, (t + 1) * TT:(t + 2) * TT], [("d", src.name, t + 1)], [("xt", 1 - b)])
            for jj in range(18):
                if jj == 10 and t + 1 < NT:
                    modulate_tile(C, hTb[1 - b], xt[1 - b], ("xt", 1 - b), l, 3, TT, hkey=("hT", 1 - b))
                bank = jj % 4
                for half in range(2):
                    j = jj * 2 + half
                    if j >= 35:
                        continue
                    for k in range(8):
                        C.mm(pp[bank][:, half * TT:(half + 1) * TT], w[:, k, cols[j]:cols[j] + 128], hT[:, k, :],
                             k == 0, k == 7, ["win", ("hT", b)], [("pp", bank)])
                n = 2 if jj * 2 + 1 < 35 else 1
                o = pj[b][:, jj * 2:jj * 2 + n, :]
                i_ = pp[bank][:, 0:n * TT].rearrange("p (c l) -> p c l", c=n)
                if jj % 2 == 0:
                    C.act(o, i_, AF.Copy, [("pp", bank)], [("pj", b)])
                else:
                    C.cp("dve", o, i_, [("pp", bank)], [("pj", b)])
            for blk in range(2):
                for k in range(8):
                    C.mm(pz[blk][:], hT[:, k, blk * 128:(blk + 1) * 128], w[:, k, O_Z:O_Z + 512], k == 0, k == 7,
                         ["win", ("hT", b)], [("pz", blk)])
                C.cp("dve", zs[b][:, blk, :], pz[blk][:], [("pz", blk)], [("zs", b)])
                for k in range(8):
                    C.mm(pd[:, blk * 16:(blk + 1) * 16], hT[:, k, blk * 128:(blk + 1) * 128], w[:, k, O_DT:O_DT + 16],
                         k == 0, k == 7, ["win", ("hT", b)], ["pd"])
            C.act(ds[b][:], pd[:, 0:32].rearrange("p (c h) -> p c h", c=2), AF.Copy, ["pd"], [("ds", b)])
            sl = slice(t * TT, (t + 1) * TT)
            C.dma(pa.rearrange("(c p) l -> p c l", p=128)[:, :, sl], pj[b][:, 0:15, :], [("pj", b)], [("d", "projA", t)], q="sp")
            C.dma(pb.rearrange("(c p) l -> p c l", p=128)[:, :, sl], pj[b][:, 15:27, :], [("pj", b)], [("d", "projB", t)], q="sp")
            C.dma(px.rearrange("(c p) l -> p c l", p=128)[:, :, sl], pj[b][:, 27:35, :], [("pj", b)], [("d", "projX", t)], q="sp")
            C.dma(zt.rearrange("(c p) n -> p c n", p=128)[:, 2 * t:2 * t + 2, :], zs[b][:], [("zs", b)], [("d", "z_tok", t)], q="pool")
            C.dma(dtt.rearrange("(c p) n -> p c n", p=128)[:, 2 * t:2 * t + 2, :], ds[b][:], [("ds", b)], [("d", "dt_tok", t)], q="pool")
    C.S.barrier()

def stage_ssd(C, l):
    L = C.L
    NQ = L // 128
    px, zt, dtt = C.d["projX"], C.d["z_tok"], C.d["dt_tok"]
    bcT, xbt, ycT = C.d["ssd_bcT"], C.d["ssd_xbtok"], C.d["ycT"]
    with contextlib.ExitStack() as st:
        cw = C.sb(st, "cw", [128, 40], F32)
        cb = C.sb(st, "cb", [128, 8], F32)
        dg = C.sb(st, "dg", [128, 8, 5, 128], BF16)
        C.dma(cw[:], C.d["ssd_cw"][l], [], ["cw"])
        C.dma(cb[:], C.d["ssd_cb"][l], [], ["cb"])
        for c in range(8):
            for k in range(5):
                C.ts("pool", dg[:, c, k, :], C.ident[:], cw[:, c * 5 + k:c * 5 + k + 1], ALU.mult, ["cw", "ident"], ["dg"])
        TA = 512
        xin = [C.sb(st, f"xin{i}", [128, 8, TA + 4], BF16) for i in range(2)]
        xc = [C.sb(st, f"xc{i}", [128, 8, TA], BF16) for i in range(2)]
        tokb = [C.sb(st, f"tokb{i}", [128, 4, 768], BF16) for i in range(2)]
        pc = [C.ps(st, f"pc{i}", [128, 512], F32) for i in range(4)]
        pT = [C.ps(st, f"pT{i}", [128, 1024], BF16) for i in range(2)]
        pxv = px.rearrange("(c p) l -> p c l", p=128)
        NTA = L // TA
        for t in range(NTA):
            b = t % 2
            kx = ("xin", b)
            lo = t * TA - 2
            hi = (t + 1) * TA + 2
            c0, c1 = 0, TA + 4
            if t == 0:
                C.memset("pool", xin[b][:, :, 0:2], 0.0, [kx])
                lo, c0 = 0, 2
            if t == NTA - 1:
                C.memset("pool", xin[b][:, :, TA + 2:TA + 4], 0.0, [kx])
                hi, c1 = L, TA + 2
            C.dma(xin[b][:, :, c0:c1], pxv[:, :, lo:hi], [("d", "projX", i) for i in range(max(0, 2 * t - 1), min(L // 256, 2 * t + 3))], [kx])
            for c in range(8):
                bank = c % 4
                for k in range(5):
                    C.mm(pc[bank][:], dg[:, c, k, :], xin[b][:, c, k:k + TA], k == 0, k == 4, ["dg", kx], [("pc", bank)])
                C.act(xc[b][:, c, :], pc[bank][:], AF.Silu, [("pc", bank), "cb"], [("xc", b)], bias=cb[:, c:c + 1])
            C.dma(bcT.rearrange("(c p) l -> p c l", p=128)[:, :, t * TA:(t + 1) * TA], xc[b][:, 4:8, :], [("xc", b)],
                  [("d", "bcT", t)], q="pool")
            for blk in range(4):
                pb_ = (t * 4 + blk) % 2
                for c in range(6):
                    C.tr(pT[pb_][:, c * 128:(c + 1) * 128], xc[b][:, c, blk * 128:(blk + 1) * 128], C.identb[:],
                         [("xc", b), "identb"], [("pT", pb_)])
                if blk % 2 == 0:
                    C.cp("dve", tokb[b][:, blk, :], pT[pb_][:, 0:768], [("pT", pb_)], [("tokb", b)])
                else:
                    C.act(tokb[b][:, blk, :], pT[pb_][:, 0:768], AF.Copy, [("pT", pb_)], [("tokb", b)])
            C.dma(xbt.rearrange("(c p) n -> p c n", p=128)[:, 4 * t:4 * t + 4, :], tokb[b][:], [("tokb", b)],
                  [("d", "xbtok", t)], q="pool")
    C.S.barrier()
    with contextlib.ExitStack() as st:
        masks = C.sb(st, "masks", [128, 6, 128], F32)
        C.dma(masks[:], C.d["c_masks"][:, :, :], [], ["masks"])
        dtb = C.sb(st, "dtb", [128, 16], F32)
        alog = C.sb(st, "alog", [128, 16], F32)
        Dbc = C.sb(st, "Dbc", [128, 512], F32)
        nw = C.sb(st, "nw", [128, 512], F32)
        C.dma(dtb[:], C.d["ssd_dtb"][l], [], ["dtb"])
        C.dma(alog[:], C.d["ssd_alog"][l], [], ["alog"])
        C.dma(Dbc[:], C.d["ssd_Dbc"][l], [], ["Dbc"])
        C.dma(nw[:], C.d["ssd_nw"][l], [], ["nw"])
        dtv = C.sb(st, "dtv", [128, NQ, 16], F32)
        av = C.sb(st, "av", [128, NQ, 16], F32)
        Ev = C.sb(st, "Ev", [128, 5, NQ * 16], F32)
        Wf = C.sb(st, "Wf", [128, NQ, 8], F32)
        Wb = C.sb(st, "Wb", [128, NQ, 8], F32)
        stF = C.sb(st, "stF", [128, NQ, 512], BF16)
        stB = C.sb(st, "stB", [128, NQ, 512], BF16)
        state = C.sb(st, "state", [128, 512], F32)
        pD = [C.ps(st, f"pD{i}", [128, 512], F32) for i in range(2)]
        pe5 = pD
        C.dma(dtv[:], dtt.rearrange("(q p) h -> p q h", p=128), [("d", "dt_tok", i) for i in range(L // 256)], ["dtv"])
        C.tt("dve", dtv[:], dtv[:], dtb[:].unsqueeze(1).broadcast_to([128, NQ, 16]), ALU.add, ["dtv", "dtb"], ["dtv"])
        C.act(dtv[:], dtv[:], AF.Exp, ["dtv"], ["dtv"])
        C.act(dtv[:], dtv[:], AF.Ln, ["dtv"], ["dtv"], bias=1.0)
        C.act(alog[:], alog[:], AF.Exp, ["alog"], ["alog"])
        C.ts("dve", alog[:], alog[:], -1.0, ALU.mult, ["alog"], ["alog"])
        C.tt("dve", av[:], dtv[:], alog[:].unsqueeze(1).broadcast_to([128, NQ, 16]), ALU.mult, ["dtv", "alog"], ["av"])
        lhs5 = [masks[:, 1, :], masks[:, 3, :], masks[:, 2, :], masks[:, 0, :], C.ones[:]]
        for g8 in range(NQ // 8):
            rhs = av[:, g8 * 8:(g8 + 1) * 8, :]
            for kind in range(5):
                bank = kind // 4
                C.mm(pe5[bank][:, (kind % 4) * 128:(kind % 4 + 1) * 128], lhs5[kind], rhs, True, True,
                     ["masks", "ones", "av"], [("pD", bank)])
            C.act(Ev[:, 0:4, g8 * 128:(g8 + 1) * 128], pe5[0][:].rearrange("p (k n) -> p k n", k=4), AF.Exp,
                  [("pD", 0)], ["Ev"])
            C.act(Ev[:, 4, g8 * 128:(g8 + 1) * 128], pe5[1][:, 0:128], AF.Exp, [("pD", 1)], ["Ev"])
        Ev4 = lambda kind: Ev[:, kind, :].rearrange("p (q h) -> p q h", h=16)
        C.tt("dve", Wf[:], dtv[:, :, 0:8], Ev4(3)[:, :, 0:8], ALU.mult, ["dtv", "Ev"], ["Wf"])
        C.tt("dve", Wb[:], dtv[:, :, 8:16], Ev4(2)[:, :, 8:16], ALU.mult, ["dtv", "Ev"], ["Wb"])
        tok = [C.sb(st, f"tok{i}", [128, 768], BF16) for i in range(3)]
        xdd = [C.sb(st, f"xdd{i}", [128, 512], BF16) for i in range(2)]
        stmp = C.sb(st, "stmp", [128, 512], F32)
        pst = [C.ps(st, f"pst{i}", [128, 512], F32) for i in range(2)]
        xbv = xbt.rearrange("(q p) n -> p q n", p=128)
        ld = 0
        ya = C.sb(st, "ya", [128, 512], F32)
        yb = C.sb(st, "yb", [128, 512], F32)
        state2 = [state, ya]
        stmp2 = [stmp, yb]
        orders = [list(range(NQ)), list(range(NQ - 1, -1, -1))]
        for dr in range(2):
            stX = stF if dr == 0 else stB
            C.memset("pool", stX[:, orders[dr][0], :], 0.0, [("stX", dr)])
            C.memset("pool", state2[dr][:], 0.0, [("state", dr)])
        for i in range(NQ - 1):
            for dr in range(2):
                stX, W = (stF, Wf) if dr == 0 else (stB, Wb)
                q = orders[dr][i]
                tb = ld % 3
                ld += 1
                C.dma(tok[tb][:], xbv[:, q, :], [], [("tok", tb)])
                C.tt("dve", xdd[dr][:].rearrange("p (h e) -> p h e", h=8), tok[tb][:, 0:512].rearrange("p (h e) -> p h e", h=8),
                     W[:, q, :].unsqueeze(2).broadcast_to([128, 8, 64]), ALU.mult, [("tok", tb), "Wf", "Wb"], [("xdd", dr)])
                for g in range(2):
                    C.mm(pst[dr][:, g * 256:(g + 1) * 256], tok[tb][:, 512 + g * 128:512 + (g + 1) * 128],
                         xdd[dr][:, g * 256:(g + 1) * 256], True, True, [("tok", tb), ("xdd", dr)], [("pst", dr)])
                C.tt("pool", stmp2[dr][:].rearrange("p (h e) -> p h e", h=8), state2[dr][:].rearrange("p (h e) -> p h e", h=8),
                     Ev4(4)[:, q, dr * 8:dr * 8 + 8].unsqueeze(2).broadcast_to([128, 8, 64]), ALU.mult, [("state", dr), "Ev"], [("stmp", dr)])
                C.tt("dve", state2[dr][:], stmp2[dr][:], pst[dr][:], ALU.add, [("stmp", dr), ("pst", dr)], [("state", dr)])
                C.act(stX[:, orders[dr][i + 1], :], state2[dr][:], AF.Copy, [("state", dr)], [("stX", dr)])
        C.S.barrier()
        bc = [C.sb(st, f"bc{i}", [128, 4, 128], BF16) for i in range(2)]
        zb = [C.sb(st, f"zb{i}", [128, 512], BF16) for i in range(2)]
        Ah = [C.sb(st, f"Ah{i}", [128, 128], F32) for i in range(4)]
        Lt = [C.sb(st, f"Lt{i}", [128, 4, 128], F32) for i in range(2)]
        Mt = [C.sb(st, f"Mt{i}", [128, 4, 128], BF16) for i in range(2)]
        xd = [C.sb(st, f"xd{i}", [128, 512], BF16) for i in range(2)]
        sz = C.sb(st, "sz", [128, 512], F32)
        ss = C.sb(st, "ss", [128, 2], F32)
        rs = C.sb(st, "rs", [128, 2], F32)
        y3 = C.sb(st, "y3", [128, 512], BF16)
        yT = [C.sb(st, f"yT{i}", [128, 4, 128], BF16) for i in range(2)]
        pcb = C.ps(st, "pcb", [128, 512], F32)
        pY = C.ps(st, "pY", [128, 512], F32)
        pO = pst
        pT = C.ps(st, "pT", [128, 1024], BF16)
        bcv = bcT.rearrange("(c p) l -> p c l", p=128)
        ztv = zt.rearrange("(q p) n -> p q n", p=128)
        ycv = ycT.rearrange("(c p) l -> p c l", p=128)
        nD = 0
        nA = 0
        for q in range(NQ):
            b = q % 2
            tb = ld % 3
            ld += 1
            C.dma(tok[tb][:], xbv[:, q, :], [("d", "xbtok", q // 4)], [("tok", tb)])
            C.dma(bc[b][:], bcv[:, :, q * 128:(q + 1) * 128], [("d", "bcT", q // 4)], [("bc", b)])
            C.dma(zb[b][:], ztv[:, q, :], [("d", "z_tok", q // 2)], [("zb", b)], q="act")
            for g in range(2):
                C.mm(pcb[:, g * 128:(g + 1) * 128], bc[b][:, g, :], bc[b][:, 2 + g, :], g == 0, True,
                     [("bc", b)], ["pcb"], skip_group_check=True)
            first_y = True
            for dr in range(2):
                xb_ = dr
                C.tt("pool", xd[xb_][:].rearrange("p (h e) -> p h e", h=8), tok[tb][:, 0:512].rearrange("p (h e) -> p h e", h=8),
                     dtv[:, q, dr * 8:dr * 8 + 8].unsqueeze(2).broadcast_to([128, 8, 64]), ALU.mult, [("tok", tb), "dtv"], [("xd", xb_)])
                mA = masks[:, 0, :] if dr == 0 else masks[:, 2, :]
                mR = masks[:, 1, :] if dr == 0 else masks[:, 3, :]
                mN = masks[:, 4, :] if dr == 0 else masks[:, 5, :]
                for g in range(2):
                    db = nD % 2
                    nD += 1
                    for e in range(4):
                        h = g * 4 + e
                        ab = nA % 4
                        nA += 1
                        C.ts("dve" if e % 2 == 0 else "pool", Ah[ab][:], mA, av[:, q, dr * 8 + h:dr * 8 + h + 1], ALU.mult,
                             ["masks", "av"], [("Ah", ab)])
                        C.mm(pD[db][:, e * 128:(e + 1) * 128], Ah[ab][:], mR, e == 0, False, [("Ah", ab), "masks"], [("pD", db)],
                             skip_group_check=True)
                        C.mm(pD[db][:, e * 128:(e + 1) * 128], C.ident[:], mN, False, True, ["ident", "masks"], [("pD", db)],
                             skip_group_check=True)
                    C.act(Lt[db][:], pD[db][:].rearrange("p (e l) -> p e l", e=4), AF.Exp, [("pD", db)], [("Lt", db)])
                    C.tt("dve", Mt[db][:], Lt[db][:], pcb[:, g * 128:(g + 1) * 128].unsqueeze(1).broadcast_to([128, 4, 128]), ALU.mult,
                         [("Lt", db), "pcb"], [("Mt", db)])
                    for e in range(4):
                        h = g * 4 + e
                        C.mm(pY[:, h * 64:(h + 1) * 64], Mt[db][:, e, :], xd[xb_][:, h * 64:(h + 1) * 64], first_y, True,
                             [("Mt", db), ("xd", xb_)], ["pY"], skip_group_check=True)
                        first_y = False
                stX = stF if dr == 0 else stB
                for g in range(2):
                    C.mm(pO[dr][:, g * 256:(g + 1) * 256], bc[b][:, 2 + g, :], stX[:, q, g * 256:(g + 1) * 256], g == 0, True,
                         [("bc", b), ("stX", dr)], [("pst", dr)], skip_group_check=True)
            v3 = lambda ap: ap.rearrange("p (h e) -> p h e", h=8)
            C.tt("pool", ya[:], tok[tb][:, 0:512], Dbc[:], ALU.mult, [("tok", tb), "Dbc"], ["ya"])
            C.tt("dve", v3(yb[:]), v3(pO[0][:]), Ev4(0)[:, q, 0:8].unsqueeze(2).broadcast_to([128, 8, 64]), ALU.mult,
                 [("pst", 0), "Ev"], ["yb"])
            C.tt("pool", ya[:], ya[:], yb[:], ALU.add, ["ya", "yb"], ["ya"])
            C.tt("dve", v3(yb[:]), v3(pO[1][:]), Ev4(1)[:, q, 8:16].unsqueeze(2).broadcast_to([128, 8, 64]), ALU.mult,
                 [("pst", 1), "Ev"], ["yb"])
            C.tt("pool", ya[:], ya[:], yb[:], ALU.add, ["ya", "yb"], ["ya"])
            C.tt("dve", ya[:], ya[:], pY[:], ALU.add, ["ya", "pY"], ["ya"])
            C.act(sz[:], zb[b][:], AF.Silu, [("zb", b)], ["sz"])
            C.tt("pool", ya[:], ya[:], sz[:], ALU.mult, ["ya", "sz"], ["ya"])
            for g in range(2):
                C.S.add("act", (lambda g=g: (lambda e: e.activation(sz[:, g * 256:(g + 1) * 256], ya[:, g * 256:(g + 1) * 256],
                                                                  AF.Square, accum_out=ss[:, g:g + 1])))(), ["ya"], ["sz", "ss"])
            C.ts("dve", ss[:], ss[:], 1.0 / 256, ALU.mult, ["ss"], ["ss"], s2=1e-5, op1=ALU.add)
            C.act(ss[:], ss[:], AF.Sqrt, ["ss"], ["ss"])
            C.recip(rs[:], ss[:], ["ss"], ["rs"])
            for g in range(2):
                C.stt(y3[:, g * 256:(g + 1) * 256], ya[:, g * 256:(g + 1) * 256], rs[:, g:g + 1], nw[:, g * 256:(g + 1) * 256],
                      ALU.mult, ALU.mult, ["ya", "rs", "nw"], ["y3"])
            for c in range(4):
                C.tr(pT[:, c * 128:(c + 1) * 128], y3[:, c * 128:(c + 1) * 128], C.identb[:], ["y3", "identb"], ["pT"])
            C.act(yT[b][:], pT[:, 0:512].rearrange("p (c l) -> p c l", c=4), AF.Copy, ["pT"], [("yT", b)])
            C.dma(ycv[:, :, q * 128:(q + 1) * 128], yT[b][:], [("yT", b)], [("d", "ycT", q)], q="pool")
    C.S.barrier()

PI = math.pi


def dw_conv_diag(C, st, name, wdram, nch, K):
    cw = C.sb(st, name + "cw", [128, nch * K], F32)
    dg = C.sb(st, name + "dg", [128, nch, K, 128], BF16)
    C.dma(cw[:], wdram, [], [name + "cw"])
    for c in range(nch):
        for k in range(K):
            C.ts("pool", dg[:, c, k, :], C.ident[:], cw[:, c * K + k:c * K + k + 1], ALU.mult, [name + "cw", "ident"], [name + "dg"])
    return dg


def load_halo(C, buf, kx, view, t, TA, halo, L, dkeys):
    NTA = L // TA
    lo, hi = t * TA - halo, (t + 1) * TA + halo
    c0, c1 = 0, TA + 2 * halo
    if t == 0:
        C.memset("pool", buf[:, :, 0:halo], 0.0, [kx])
        lo, c0 = 0, halo
    if t == NTA - 1:
        C.memset("pool", buf[:, :, TA + halo:TA + 2 * halo], 0.0, [kx])
        hi, c1 = L, TA + halo
    C.dma(buf[:, :, c0:c1], view[:, :, lo:hi], dkeys, [kx])


def hy_filter(C, l):
    L = C.L
    TT = 512
    NT = L // TT
    with contextlib.ExitStack() as st:
        w1 = C.sb(st, "w1", [33, 64], F32)
        w2 = C.sb(st, "w2", [64, 64], F32)
        w3 = C.sb(st, "w3", [64, 512], F32)
        bb = C.sb(st, "bb", [64, 2], F32)
        fq = C.sb(st, "fq", [64, 2], F32)
        nd = C.sb(st, "nd", [128, 4], F32)
        hbi = C.sb(st, "hbi", [128, 4], F32)
        C.dma(w1[:], C.d["hy_f_w1"][l], [], ["w1"])
        C.dma(w2[:], C.d["hy_f_w2"][l], [], ["w2"])
        C.dma(w3[:], C.d["hy_f_w3"][l], [], ["w3"])
        C.dma(bb[:], C.d["hy_fb"][l], [], ["bb"])
        C.dma(fq[:], C.d["hy_freq"][l], [], ["fq"])
        C.dma(nd[:], C.d["c_ndelta"][:, :], [], ["nd"])
        C.dma(hbi[:], C.d["hy_bias"][l], [], ["hbi"])
        C.tt("dve", bb[:], bb[:], fq[:], ALU.mult, ["bb", "fq"], ["bb"])
        hbuf = C.sb(st, "hbuf", [128, 4, L], F32)
        ft = [C.sb(st, f"ft{i}", [33, TT], F32) for i in range(2)]
        dt_ = [C.sb(st, f"dtl{i}", [128, TT], F32) for i in range(2)]
        u = C.sb(st, "u", [64, TT], F32)
        m = C.sb(st, "m", [64, TT], F32)
        h1 = C.sb(st, "h1", [64, TT], F32)
        wd = [C.sb(st, f"wd{i}", [128, TT], F32) for i in range(2)]
        junk = C.sb(st, "junk", [128, TT], F32)
        sp = C.sb(st, "sp", [128, 4, NT], F32)
        ssum = C.sb(st, "ssum", [128, 4], F32)
        rs = C.sb(st, "rs", [128, 4], F32)
        p12 = C.ps(st, "p12", [128, 512], F32)
        p3 = [C.ps(st, f"p3{i}", [128, 512], F32) for i in range(2)]

        def sin_layer(src_ps, j):
            C.act(u[:], src_ps, AF.Identity, ["p12", "fq", "bb"], ["u"], bias=bb[:, j:j + 1], scale=fq[:, j:j + 1])
            C.ts("dve", m[:], u[:], PI, ALU.is_gt, ["u"], ["m"], s2=-2 * PI, op1=ALU.mult)
            C.tt("dve", u[:], u[:], m[:], ALU.add, ["u", "m"], ["u"])
            C.ts("dve", m[:], u[:], -PI, ALU.is_lt, ["u"], ["m"], s2=2 * PI, op1=ALU.mult)
            C.tt("dve", u[:], u[:], m[:], ALU.add, ["u", "m"], ["u"])
            C.ts("dve", u[:], u[:], 3.14159, ALU.min, ["u"], ["u"], s2=-3.14159, op1=ALU.max)
            C.act(h1[:], u[:], AF.Sin, ["u"], ["h1"])

        for t in range(NT):
            b = t % 2
            sl = slice(t * TT, (t + 1) * TT)
            C.dma(ft[b][:], C.d["c_featsT"][:, sl], [], [("ft", b)])
            C.dma(dt_[b][:], C.d["c_distb"][:, sl], [], [("dtl", b)])
            C.mm(p12[0:64, :], w1[:], ft[b][:], True, True, ["w1", ("ft", b)], ["p12"])
            sin_layer(p12[0:64, :], 0)
            C.mm(p12[0:64, :], w2[:], h1[:], True, True, ["w2", "h1"], ["p12"])
            sin_layer(p12[0:64, :], 1)
            for c in range(4):
                pb_ = c % 2
                C.mm(p3[pb_][:], w3[:, c * 128:(c + 1) * 128], h1[:], True, True, ["w3", "h1"], [("p3", pb_)])
                C.act(wd[pb_][:], dt_[b][:], AF.Exp, [("dtl", b), "nd"], [("wd", pb_)], scale=nd[:, c:c + 1])
                C.tt("dve", hbuf[:, c, sl], p3[pb_][:], wd[pb_][:], ALU.mult, [("p3", pb_), ("wd", pb_)], ["hbuf"])
                C.S.add("act", (lambda c=c, t=t, sl=sl: (lambda e: e.activation(junk[:], hbuf[:, c, sl], AF.Square,
                                                                           accum_out=sp[:, c, t:t + 1])))(), ["hbuf"], ["junk", "sp"])
        C.S.add("dve", lambda e: e.tensor_reduce(ssum[:], sp[:], AX.X, ALU.add), ["sp"], ["ssum"])
        C.ts("dve", ssum[:], ssum[:], 1e-12, ALU.add, ["ssum"], ["ssum"])
        C.act(ssum[:], ssum[:], AF.Sqrt, ["ssum"], ["ssum"])
        C.recip(rs[:], ssum[:], ["ssum"], ["rs"])
        hb = [C.sb(st, f"hb{i}", [128, 2048], BF16) for i in range(2)]
        hv = C.d["hy_hT"].rearrange("(c p) l -> p c l", p=128)
        i = 0
        for c in range(4):
            C.ts("dve", hbuf[:, c, L // 2:L // 2 + 1], hbuf[:, c, L // 2:L // 2 + 1], rs[:, c:c + 1], ALU.mult, ["hbuf", "rs", "hbi"], ["hbuf"],
                 s2=hbi[:, c:c + 1], op1=ALU.add)
            C.recip(junk[:, 0:1], rs[:, c:c + 1], ["rs"], ["junk"])
            C.ts("dve", hbuf[:, c, L // 2:L // 2 + 1], hbuf[:, c, L // 2:L // 2 + 1], junk[:, 0:1], ALU.mult, ["hbuf", "junk"], ["hbuf"])
            for s in range(L // 2048):
                b = i % 2
                i += 1
                C.act(hb[b][:], hbuf[:, c, s * 2048:(s + 1) * 2048], AF.Identity, ["hbuf", "rs"], [("hb", b)], scale=rs[:, c:c + 1])
                C.dma(hv[:, c, s * 2048:(s + 1) * 2048], hb[b][:], [("hb", b)], [("d", "hy_hT", i)], q="pool")
    C.S.barrier()


def hy_front(C, l):
    L = C.L
    TA = 512
    NTA = L // TA
    with contextlib.ExitStack() as st:
        dg = dw_conv_diag(C, st, "hy", C.d["hy_cw"][l], 12, 3)
        cb = C.sb(st, "hycb", [128, 12], F32)
        C.dma(cb[:], C.d["hy_cb"][l], [], ["hycb"])
        xin = [C.sb(st, f"hxin{i}", [128, 12, TA + 2], BF16) for i in range(2)]
        x1 = C.sb(st, "hx1", [128, 4, TA], F32)
        ob = [C.sb(st, f"hob{i}", [128, 8, TA], BF16) for i in range(2)]
        pc = [C.ps(st, f"hpc{i}", [128, 512], F32) for i in range(4)]
        pv = C.d["projB"].rearrange("(c p) l -> p c l", p=128)
        x0v = C.d["hy_x0T"].rearrange("(c p) l -> p c l", p=128)
        zv = C.d["hy_zT"].rearrange("(c p) l -> p c l", p=128)
        for t in range(NTA):
            b = t % 2
            kx = ("hxin", b)
            load_halo(C, xin[b], kx, pv, t, TA, 1, L, [("d", "projB", i) for i in range(max(0, 2 * t - 1), min(L // 256, 2 * t + 3))])
            for c in range(12):
                bank = c % 4
                for k in range(3):
                    C.mm(pc[bank][:], dg[:, c, k, :], xin[b][:, c, k:k + TA], k == 0, k == 2, ["hydg", kx], [("hpc", bank)])
                if c < 4:
                    C.act(ob[b][:, c, :], pc[bank][:], AF.Identity, [("hpc", bank), "hycb"], [("hob", b)], bias=cb[:, c:c + 1])
                elif c < 8:
                    C.act(x1[:, c - 4, :], pc[bank][:], AF.Identity, [("hpc", bank), "hycb"], ["hx1"], bias=cb[:, c:c + 1])
                else:
                    C.stt(ob[b][:, c - 4, :], pc[bank][:], cb[:, c:c + 1], x1[:, c - 8, :], ALU.add, ALU.mult,
                          [("hpc", bank), "hycb", "hx1"], [("hob", b)])
            sl = slice(t * TA, (t + 1) * TA)
            C.dma(x0v[:, :, sl], ob[b][:, 0:4, :], [("hob", b)], [("d", "hy_x0T", t)], q="pool")
            C.dma(zv[:, :, sl], ob[b][:, 4:8, :], [("hob", b)], [("d", "hy_zT", t)], q="pool")
    C.S.barrier()


def hy_fft(C, l, filt):
    L = C.L
    assert L == 8192
    NP = 256
    with contextlib.ExitStack() as st:
        F256 = C.sb(st, "F256", [128, 512], BF16)
        Tw = C.sb(st, "Tw", [128, 768], F32)
        L4 = C.sb(st, "L4", [128, 4, 64], BF16)
        G3 = C.sb(st, "G3", [128, 3, 128], BF16)
        TAB = C.sb(st, "TAB", [128, 2, 2, 128], F32)
        FIN = C.sb(st, "FIN", [128, 2, 2, 128], BF16)
        for nm, t_ in (("F256", F256), ("Tw", Tw), ("L4", L4), ("G3", G3), ("TAB", TAB), ("FIN", FIN)):
            C.dma(t_[:], C.d["c_" + nm], [], [nm])
        zA = C.sb(st, "zA", [128, 512, 64], BF16)
        src = C.d["hy_hT"] if filt else C.d["hy_zT"]
        for g in range(16):
            C.dma(zA[:, g * 32:(g + 1) * 32, :], src[g * 32:(g + 1) * 32, :].rearrange("c (p j) -> p c j", j=64),
                  [], [("zA", g)], q="sp" if g % 2 == 0 else "act")
        if not filt:
            x0A = C.sb(st, "x0A", [128, 512, 64], BF16)
            for g in range(16):
                C.dma(x0A[:, g * 32:(g + 1) * 32, :], C.d["hy_x0T"][g * 32:(g + 1) * 32, :].rearrange("c (p j) -> p c j", j=64),
                      [], [("x0A", g)], q="sp" if g % 2 == 0 else "act")
        NB = 3
        t12 = [C.sb(st, f"t12{i}", [128, 2, 512], BF16) for i in range(NB)]
        pB = [C.ps(st, f"pB{i}", [128, 512], F32) for i in range(2)]
        pZ = [C.ps(st, f"pZ{i}", [128, 512], F32) for i in range(2)]
        if filt:
            Hfb = [C.sb(st, f"Hfb{i}", [128, 768], BF16) for i in range(2)]
        else:
            Hf = [C.sb(st, f"Hf{i}", [128, 768], BF16) for i in range(4)]
            u12 = [C.sb(st, f"u12{i}", [128, 2, 512], BF16) for i in range(NB)]
            w12 = [C.sb(st, f"w12{i}", [128, 2, 4, 128], BF16) for i in range(NB)]
            pQ = [C.ps(st, f"pQ{i}", [128, 512], F32) for i in range(2)]
            pY = [C.ps(st, f"pY{i}", [128, 512], F32) for i in range(2)]
        for i in range(NP):
            b = i % 2
            nb = i % NB
            if not filt:
                hb3 = i % 4
                C.dma(Hf[hb3][:], C.d["hy_Hf"][i], [], [("Hf", hb3)], q="sp")
            for hh in range(2):
                C.mm(pB[b][64 * hh:64 * hh + 64, :], zA[:, 2 * i + hh, :], F256[:], True, True, [("zA", (2 * i) // 32), "F256"], [("pB", b)])
            C.tt("dve", t12[nb][:, 0, :], pB[b][:], Tw[:, 0:512], ALU.mult, [("pB", b), "Tw"], [("t12", nb)])
            C.tt("dve", t12[nb][:, 1, :], pB[b][:], Tw[:, 256:768], ALU.mult, [("pB", b), "Tw"], [("t12", nb)])
            for hh in range(2):
                ps_ = slice(64 * hh, 64 * hh + 64)
                t1a, t1b = t12[nb][ps_, 0, 0:256], t12[nb][ps_, 0, 256:512]
                t2a, t2b = t12[nb][ps_, 1, 0:256], t12[nb][ps_, 1, 256:512]
                cc, nc_, ss, ns = L4[ps_, 0, :], L4[ps_, 1, :], L4[ps_, 2, :], L4[ps_, 3, :]
                zr, zi = pZ[b][ps_, 0:256], pZ[b][ps_, 256:512]
                seq = [(zr, cc, t1a), (zr, nc_, t1b), (zr, ss, t2a), (zr, ss, t2b),
                       (zi, cc, t2a), (zi, cc, t2b), (zi, ns, t1a), (zi, ss, t1b)]
                for j, (o, lt, rh) in enumerate(seq):
                    C.mm(o, lt, rh, j == 0, j == 7, ["L4", ("t12", nb)], [("pZ", b)], skip_group_check=True)
            if filt:
                C.act(Hfb[b][:, 0:512], pZ[b][:], AF.Copy, [("pZ", b)], [("Hfb", b)])
                C.cp("dve", Hfb[b][:, 512:768], pZ[b][:, 0:256], [("pZ", b)], [("Hfb", b)])
                C.dma(C.d["hy_Hf"][i], Hfb[b][:], [("Hfb", b)], [("d", "hy_Hf", i)], q="act")
                continue
            C.tt("dve", u12[nb][:, 0, :], pZ[b][:], Hf[hb3][:, 0:512], ALU.mult, [("pZ", b), ("Hf", hb3)], [("u12", nb)])
            C.tt("dve", u12[nb][:, 1, :], pZ[b][:], Hf[hb3][:, 256:768], ALU.mult, [("pZ", b), ("Hf", hb3)], [("u12", nb)])
            for hh in range(2):
                ps_ = slice(64 * hh, 64 * hh + 64)
                for kh in range(2):
                    o = pQ[b][:, (hh * 2 + kh) * 128:(hh * 2 + kh + 1) * 128]
                    ks = slice(kh * 128, (kh + 1) * 128)
                    ks2 = slice(256 + kh * 128, 256 + (kh + 1) * 128)
                    seq = [(u12[nb][ps_, 0, ks], G3[ps_, 0, :]), (u12[nb][ps_, 0, ks2], G3[ps_, 1, :]),
                           (u12[nb][ps_, 1, ks], G3[ps_, 2, :]), (u12[nb][ps_, 1, ks2], G3[ps_, 2, :])]
                    for j, (lt, rh) in enumerate(seq):
                        C.mm(o, lt, rh, j == 0, j == 3, [("u12", nb), "G3"], [("pQ", b)], skip_group_check=True)
            q4 = pQ[b][:].rearrange("p (g n) -> p g n", g=4)
            for ab in range(2):
                for hh in range(2):
                    C.tt("dve", w12[nb][:, ab, 2 * hh:2 * hh + 2, :], q4[:, 2 * hh:2 * hh + 2, :], TAB[:, ab, :, :], ALU.mult,
                         [("pQ", b), "TAB"], [("w12", nb)])
            yb_ = (i // 4) % 2
            for hh in range(2):
                slot = (2 * i + hh) % 8
                o = pY[yb_][:, slot * 64:(slot + 1) * 64]
                n = 0
                for kh in range(2):
                    for ab in range(2):
                        for half in range(2):
                            C.mm(o, FIN[:, kh, ab, :], w12[nb][:, ab, hh * 2 + kh, half * 64:(half + 1) * 64], n == 0, n == 7,
                                 ["FIN", ("w12", nb)], [("pY", yb_)], skip_group_check=True)
                            n += 1
            if i % 4 == 3:
                c0 = 2 * i - 6
                C.tt("dve", x0A[:, c0:c0 + 8, :], x0A[:, c0:c0 + 8, :], pY[yb_][:].rearrange("p (c n) -> p c n", c=8), ALU.mult,
                     [("x0A", c0 // 32), ("pY", yb_)], [("x0A", c0 // 32)])
                if (c0 + 8) % 32 == 0:
                    g = c0 // 32
                    C.dma(C.d["ybT"][g * 32:(g + 1) * 32, :].rearrange("c (p j) -> p c j", j=64), x0A[:, g * 32:(g + 1) * 32, :],
                          [("x0A", g)], [("d", "ybT", g)], q="sp")
    C.S.barrier()


def stage_hyena(C, l):
    hy_filter(C, l)
    hy_front(C, l)
    hy_fft(C, l, True)
    hy_fft(C, l, False)

RW_C = 0.6065306597126334
RW_GN_EPS = 64e-5


def rw_front(C, l):
    L = C.L
    TA = 512
    NTA = L // TA
    with contextlib.ExitStack() as st:
        mu = C.sb(st, "mu", [128, 2, 15], F32)
        wm = C.sb(st, "wm", [128, 15], F32)
        C.dma(mu[:], C.d["rw_mu"][l], [], ["mu"])
        C.tt("dve", wm[:], mu[:, 0, :], mu[:, 1, :], ALU.add, ["mu"], ["wm"])
        C.ts("dve", wm[:], wm[:], -1.0, ALU.mult, ["wm"], ["wm"], s2=1.0, op1=ALU.add)
        dg = C.sb(st, "rdg", [128, 15, 3, 128], BF16)
        for c in range(15):
            C.ts("pool", dg[:, c, 0, :], C.ident[:], mu[:, 0, c:c + 1], ALU.mult, ["mu", "ident"], ["rdg"])
            C.ts("pool", dg[:, c, 1, :], C.ident[:], wm[:, c:c + 1], ALU.mult, ["wm", "ident"], ["rdg"])
            C.ts("pool", dg[:, c, 2, :], C.ident[:], mu[:, 1, c:c + 1], ALU.mult, ["mu", "ident"], ["rdg"])
        lw = C.sb(st, "lw", [128, 3, 512], BF16)
        with contextlib.ExitStack() as st2:
            stg = C.sb(st2, "lstg", [128, 3, 512], F32)
            C.dma(stg[:, 0, :], C.d["rw_w2"][l], [], ["lstg"])
            C.dma(stg[:, 1, :], C.d["rw_a2"][l], [], ["lstg"])
            C.dma(stg[:, 2, :], C.d["rw_g2"][l], [], ["lstg"])
            C.cp("dve", lw[:], stg[:], ["lstg"], ["lw"])
            C.S.barrier()
        bcs = C.sb(st, "bcs", [128, 8, 512], F32)
        C.dma(bcs[:, 0:7, :], C.d["rw_bc"][l], [], ["bcs"])
        C.dma(bcs[:, 7, :], C.d["rw_bc"][l][:, 6, :], [], ["bcs"])
        C.ts("dve", bcs[:, 6, :], bcs[:, 5, :], -1.0, ALU.mult, ["bcs"], ["bcs"], s2=1.0, op1=ALU.add)
        xin = [C.sb(st, f"rxin{i}", [128, 15, TA + 2], BF16) for i in range(2)]
        sh = C.sb(st, "rsh", [128, 15, TA], BF16)
        rkv = [C.sb(st, f"rkv{i}", [128, 1536], BF16) for i in range(2)]
        T = {n: C.sb(st, "r_" + n, [128, 512], F32) for n in
             ["tmp", "sg0", "sg1", "as0", "as1", "kk", "sq", "kkn", "t", "kd0", "kd1", "rkx"]}
        s8 = C.sb(st, "s8", [128, 8], F32)
        oc = [C.sb(st, f"oc{i}", [128, 5, 512], BF16) for i in range(2)]
        od = [C.sb(st, f"od{i}", [128, 2, 2, 512], BF16) for i in range(2)]
        osg = [C.sb(st, f"osg{i}", [128, 2, 512], F32) for i in range(2)]
        pc = [C.ps(st, f"rpc{i}", [128, 512], F32) for i in range(2)]
        pT = C.ps(st, "rpT", [128, 2048], BF16)
        pl = [C.ps(st, f"rpl{i}", [128, 512], F32) for i in range(4)]
        pv = C.d["projA"].rearrange("(c p) l -> p c l", p=128)
        v3 = lambda ap: ap.rearrange("p (h e) -> p h e", h=8)
        bc8 = lambda ap: ap.unsqueeze(2).broadcast_to([128, 8, 64])
        nblk = 0
        for t in range(NTA):
            b = t % 2
            kx = ("rxin", b)
            load_halo(C, xin[b], kx, pv, t, TA, 1, L, [("d", "projA", i) for i in range(max(0, 2 * t - 1), min(L // 256, 2 * t + 3))])
            for c in range(15):
                bank = c % 2
                for k in range(3):
                    C.mm(pc[bank][:], dg[:, c, k, :], xin[b][:, c, k:k + TA], k == 0, k == 2, ["rdg", kx], [("rpc", bank)])
                fn = AF.Tanh if c == 12 else (AF.Sigmoid if c == 14 else AF.Copy)
                if c % 2 == 0 or c >= 12:
                    C.act(sh[:, c, :], pc[bank][:], fn, [("rpc", bank)], ["rsh"])
                else:
                    C.cp("dve", sh[:, c, :], pc[bank][:], [("rpc", bank)], ["rsh"])
            import os
            RWF = int(os.environ.get("RWF", "9"))
            for blk in range(4):
                if RWF < 2:
                    break
                ob = nblk % 2
                nblk += 1
                bs = slice(blk * 128, (blk + 1) * 128)
                for c in range(12):
                    C.tr(pT[:, c * 128:(c + 1) * 128], sh[:, c, bs], C.identb[:], ["rsh", "identb"], ["rpT"])
                C.cp("dve", rkv[ob][:, 0:768], pT[:, 0:768], ["rpT"], [("rkv", ob)])
                C.act(rkv[ob][:, 768:1536], pT[:, 768:1536], AF.Copy, ["rpT"], [("rkv", ob)])
                r_ = rkv[ob][:, 0:512]
                k_ = rkv[ob][:, 512:1024]
                v_ = rkv[ob][:, 1024:1536]
                krkv = ("rkv", ob)
                if RWF < 3:
                    continue
                for d in range(2):
                    ps_ = slice(64 * d, 64 * d + 64)
                    C.mm(pl[d][:], sh[ps_, 12, bs], lw[ps_, 0, :], True, True, ["rsh", "lw"], [("rpl", d)])
                    C.mm(pl[2 + d][:], sh[ps_, 13, bs], lw[ps_, 1, :], True, True, ["rsh", "lw"], [("rpl", 2 + d)])
                for d in range(2):
                    C.tt("dve", T["tmp"][:], pl[d][:], bcs[:, d, :], ALU.add, [("rpl", d), "bcs"], ["r_tmp"])
                    C.act(osg[ob][:, d, :], T["tmp"][:], AF.Sigmoid, ["r_tmp"], [("osg", ob)])
                    C.tt("dve", T["tmp"][:], pl[2 + d][:], bcs[:, 2 + d, :], ALU.add, [("rpl", 2 + d), "bcs"], ["r_tmp"])
                    C.act(T[f"as{d}"][:], T["tmp"][:], AF.Sigmoid, ["r_tmp"], [f"r_as{d}"])
                C.mm(pl[0][:], sh[:, 14, bs], lw[:, 2, :], True, True, ["rsh", "lw"], [("rpl", 0)])
                C.act(oc[ob][:, 3, :], pl[0][:], AF.Copy, [("rpl", 0)], [("oc", ob)])
                C.cp("pool", oc[ob][:, 0, :], r_, [krkv], [("oc", ob)])
                C.cp("pool", oc[ob][:, 1, :], v_, [krkv], [("oc", ob)])
                if RWF < 4:
                    continue
                C.tt("dve", T["kk"][:], k_, bcs[:, 4, :], ALU.mult, [krkv, "bcs"], ["r_kk"])
                C.tt("pool", T["sq"][:], T["kk"][:], T["kk"][:], ALU.mult, ["r_kk"], ["r_sq"])
                C.S.add("dve", lambda e: e.tensor_reduce(s8[:], v3(T["sq"][:]), AX.X, ALU.add), ["r_sq"], ["s8"])
                C.act(s8[:], s8[:], AF.Sqrt, ["s8"], ["s8"])
                C.ts("dve", s8[:], s8[:], 1e-12, ALU.max, ["s8"], ["s8"])
                C.recip(s8[:], s8[:], ["s8"], ["s8"])
                C.tt("dve", v3(T["kkn"][:]), v3(T["kk"][:]), bc8(s8[:]), ALU.mult, ["r_kk", "s8"], ["r_kkn"])
                C.cp("pool", oc[ob][:, 2, :], T["kkn"][:], ["r_kkn"], [("oc", ob)])
                if RWF < 5:
                    continue
                for d in range(2):
                    a_ = T[f"as{d}"]
                    C.tt("pool", T["t"][:], a_[:], bcs[:, 5, :], ALU.mult, [f"r_as{d}", "bcs"], ["r_t"])
                    C.tt("pool", T["t"][:], T["t"][:], bcs[:, 6, :], ALU.add, ["r_t", "bcs"], ["r_t"])
                    C.tt("dve", T[f"kd{d}"][:], k_, T["t"][:], ALU.mult, [krkv, "r_t"], [f"r_kd{d}"])
                    C.cp("pool", od[ob][:, d, 0, :], T[f"kd{d}"][:], [f"r_kd{d}"], [("od", ob)])
                    C.tt("dve", od[ob][:, d, 1, :], T["kkn"][:], a_[:], ALU.mult, ["r_kkn", f"r_as{d}"], [("od", ob)])
                if RWF == 6:
                    row = t * 4 + blk
                    rs_ = slice(row * 128, (row + 1) * 128)
                    C.dma(C.d["rw_c"][rs_], oc[ob][:], [("oc", ob)], [("d", "rw_c", row)], q="pool")
                    for d in range(2):
                        C.dma(C.d["rw_d"][d, rs_], od[ob][:, d], [("od", ob)], [("d", "rw_d", d, row)], q="pool")
                        C.dma(C.d["rw_sg"][d, rs_], osg[ob][:, d, :], [("osg", ob)], [("d", "rw_sg", d, row)], q="pool")
                if RWF <= 6:
                    continue
                C.tt("pool", T["kd0"][:], T["kd0"][:], T["kd1"][:], ALU.add, ["r_kd0", "r_kd1"], ["r_kd0"])
                C.tt("dve", T["rkx"][:], r_, bcs[:, 7, :], ALU.mult, [krkv, "bcs"], ["r_rkx"])
                C.tt("pool", T["rkx"][:], T["rkx"][:], T["kd0"][:], ALU.mult, ["r_rkx", "r_kd0"], ["r_rkx"])
                C.S.add("dve", lambda e: e.tensor_reduce(s8[:], v3(T["rkx"][:]), AX.X, ALU.add), ["r_rkx"], ["s8"])
                C.tt("dve", v3(oc[ob][:, 4, :]), v3(v_), bc8(s8[:]), ALU.mult, [krkv, "s8"], [("oc", ob)])
                row = t * 4 + blk
                rs_ = slice(row * 128, (row + 1) * 128)
                if os.environ.get("RWD", "1") == "0":
                    continue
                C.dma(C.d["rw_c"][rs_], oc[ob][:], [("oc", ob)], [("d", "rw_c", row)], q="pool")
                for d in range(2):
                    C.dma(C.d["rw_d"][d, rs_], od[ob][:, d], [("od", ob)], [("d", "rw_d", d, row)], q="pool")
                    C.dma(C.d["rw_sg"][d, rs_], osg[ob][:, d, :], [("osg", ob)], [("d", "rw_sg", d, row)], q="pool")
    C.S.barrier()


def rw_scan(C, l):
    L = C.L
    NT = L // 128
    v3 = lambda ap: ap.rearrange("p (h e) -> p h e", h=8)
    bc8 = lambda ap: ap.unsqueeze(2).broadcast_to([128, 8, 64])
    with contextlib.ExitStack() as st:
        rwm = C.sb(st, "rwm", [128, 2, 5, 128], F32)
        C.dma(rwm[:, 0], C.d["c_rwm"][0], [], ["rwm"])
        C.dma(rwm[:, 1], C.d["c_rwm"][1], [], ["rwm"])
        lnx = C.sb(st, "lnx", [128, 2, 512], F32)
        C.dma(lnx[:], C.d["rw_lnx"][l], [], ["lnx"])
        id64 = C.sb(st, "id64", [64, 16, 64], F32)
        C.cp("dve", id64[:], C.ident[0:64, 0:64].unsqueeze(1).broadcast_to([64, 16, 64]), ["ident"], ["id64"])
        ci = [C.sb(st, f"ci{i}", [128, 5, 512], BF16) for i in range(2)]
        di = [C.sb(st, f"di{i}", [128, 2, 512], BF16) for i in range(2)]
        sgi = [C.sb(st, f"sgi{i}", [128, 512], F32) for i in range(2)]
        yfi = [C.sb(st, f"yfi{i}", [128, 512], F32) for i in range(2)]
        E = C.sb(st, "E", [128, 4, 512], F32)
        tk = C.sb(st, "tk", [128, 6, 512], BF16)
        fT = C.sb(st, "fT", [128, 4, 2, 4, 64], BF16)
        NMb = C.sb(st, "NMb", [128, 8, 128], BF16)
        NMk = C.sb(st, "NMk", [128, 8, 128], BF16)
        PP = [C.sb(st, f"PP{i}", [128, 8, 128], BF16) for i in range(2)]
        Xs = C.sb(st, "Xs", [128, 8, 128], BF16)
        gam = C.sb(st, "gam", [64, 8, 2], F32)
        dgam = C.sb(st, "dgam", [64, 16, 64], F32)
        Gs = C.sb(st, "Gs", [64, 16, 64], F32)
        Hs = C.sb(st, "Hs", [64, 16, 64], F32)
        Qs = C.sb(st, "Qs", [64, 16, 64], BF16)
        ST32 = C.sb(st, "ST32", [64, 512], F32)
        STb = C.sb(st, "STb", [64, 512], BF16)
        yo = [C.sb(st, f"yo{i}", [128, 512], F32) for i in range(2)]
        W = {n: C.sb(st, "w_" + n, [128, 512], F32) for n in ["y", "sq", "t"]}
        m8 = C.sb(st, "m8", [128, 8], F32)
        q8 = C.sb(st, "q8", [128, 8], F32)
        y3 = C.sb(st, "ry3", [128, 512], BF16)
        yT = [C.sb(st, f"ryT{i}", [128, 4, 128], BF16) for i in range(2)]
        pT = C.ps(st, "spT", [128, 2048], BF16)
        pA = C.ps(st, "spA", [128, 1024], F32)
        pB = C.ps(st, "spB", [128, 1024], F32)
        pC = C.ps(st, "spC", [128, 1024], F32)
        cv = C.d["rw_c"].rearrange("(q p) s n -> p q s n", p=128)
        yav = C.d["yaT"].rearrange("(c p) l -> p c l", p=128)
        it = 0
        for dr in range(2):
            order = list(range(NT)) if dr == 0 else list(range(NT - 1, -1, -1))
            dv = C.d["rw_d"][dr].rearrange("(q p) s n -> p q s n", p=128)
            sv = C.d["rw_sg"][dr].rearrange("(q p) n -> p q n", p=128)
            yfv = C.d["rw_yf"].rearrange("(q p) n -> p q n", p=128)
            C.memset("dve", ST32[:], 0.0, ["ST32"])
            C.memset("dve", STb[:], 0.0, ["STb"])
            M = rwm[:, dr]
            INC, STR, STRT = M[:, 0, :], M[:, 1, :], M[:, 2, :]
            mask1 = M[:, 3, :]
            maskT = M[:, 4, 0:64]
            ind = M[:, 4, 64:66]
            for q in order:
                b = it % 2
                it += 1
                C.dma(ci[b][:], cv[:, q], [], [("ci", b)])
                C.dma(di[b][:], dv[:, q], [], [("di", b)])
                C.dma(sgi[b][:], sv[:, q], [], [("sgi", b)], q="act")
                if dr == 1:
                    C.dma(yfi[b][:], yfv[:, q], [("d", "rw_yf", q)], [("yfi", b)], q="act")
                r_, v_, kkn_ = ci[b][:, 0, :], ci[b][:, 1, :], ci[b][:, 2, :]
                kd_, bd_ = di[b][:, 0, :], di[b][:, 1, :]
                kci, kdi, ksg = ("ci", b), ("di", b), ("sgi", b)
                kA = [("pA", 0), ("pA", 1)]
                kB = [("pB", 0), ("pB", 1)]
                kX = [("Xs", 0), ("Xs", 1)]
                kP = lambda i: [("PP", i, 0), ("PP", i, 1)]
                C.mm(pA[:, 0:512], INC, sgi[b][:], True, True, ["rwm", ksg], [("pA", 0)])
                C.mm(pA[:, 512:1024], STR, sgi[b][:], True, True, ["rwm", ksg], [("pA", 1)])
                C.mm(pB[:, 0:512], STRT, sgi[b][:], True, True, ["rwm", ksg], [("pB", 0)])
                for h in range(8):
                    C.mm(pB[0:64, 512 + 2 * h:512 + 2 * h + 2], sgi[b][:, h * 64:(h + 1) * 64], ind, h == 0, True, [ksg, "rwm"], [("pB", 1)],
                         skip_group_check=True)
                C.act(E[:, 0, :], pA[:, 512:1024], AF.Exp, [("pA", 1)], ["E"], scale=-RW_C)
                C.act(E[:, 1, :], pA[:, 0:512], AF.Exp, [("pA", 0)], ["E"], scale=-RW_C)
                C.act(E[:, 2, :], pA[:, 0:512], AF.Exp, [("pA", 0)], ["E"], scale=RW_C)
                C.act(E[:, 3, :], pB[:, 0:512], AF.Exp, [("pB", 0)], ["E"], scale=-RW_C)
                C.act(gam[:], pB[0:64, 512:528].rearrange("p (h c) -> p h c", c=2), AF.Exp, [("pB", 1)], ["gam"], scale=-RW_C)
                C.stt(tk[:, 0, :], kkn_, -1.0, E[:, 0, :], ALU.mult, ALU.mult, [kci, "E"], ["tk"])
                C.tt("pool", tk[:, 1, :], r_, E[:, 1, :], ALU.mult, [kci, "E"], ["tk"])
                C.tt("dve", tk[:, 2, :], bd_, E[:, 2, :], ALU.mult, [kdi, "E"], ["tk"])
                C.tt("pool", tk[:, 3, :], kd_, E[:, 2, :], ALU.mult, [kdi, "E"], ["tk"])
                C.tt("dve", tk[:, 4, :], bd_, E[:, 3, :], ALU.mult, [kdi, "E"], ["tk"])
                C.tt("pool", tk[:, 5, :], kd_, E[:, 3, :], ALU.mult, [kdi, "E"], ["tk"])
                for x in range(4):
                    for hp in range(4):
                        o = pT[:, (hp * 4 + x) * 128:(hp * 4 + x + 1) * 128]
                        C.tr(o, tk[:, x, hp * 128:(hp + 1) * 128], C.identb[:], ["tk", "identb"], ["spT"])
                src4 = pT[:].rearrange("p (hp x c t) -> p hp x c t", hp=4, x=4, c=2)
                for c in range(2):
                    if c == 0:
                        C.cp("dve", fT[:, :, c, :, :], src4[:, :, :, c, :], ["spT"], ["fT"])
                    else:
                        C.act(fT[:, :, c, :, :], src4[:, :, :, c, :], AF.Copy, ["spT"], ["fT"])
                for par, c, h in [(par, c, h) for par in range(2) for c in range(2) for h in range(par, 8, 2)]:
                    hp, base = h // 2, 64 * (h % 2)
                    ps_ = slice(base, base + 64)
                    oc_ = slice(64 * c, 64 * c + 64)
                    C.mm(pA[oc_, h * 128:(h + 1) * 128], fT[ps_, hp, c, 2, :], fT[ps_, hp, c, 0:2, :], h % 4 == 0, True, ["fT"], [("pA", h // 4)],
                         skip_group_check=True)
                    C.mm(pB[oc_, h * 128:(h + 1) * 128], fT[ps_, hp, c, 3, :], fT[ps_, hp, c, 0:2, :], h % 4 == 0, True, ["fT"], [("pB", h // 4)],
                         skip_group_check=True)
                    C.mm(pC[oc_, h * 64:(h + 1) * 64], fT[ps_, hp, c, 0, :], fT[ps_, hp, c, 2, :], h == 0, True, ["fT"], ["pC"],
                         skip_group_check=True)
                C.tt("dve", NMb[:], pA[:].rearrange("p (h n) -> p h n", h=8), mask1.unsqueeze(1).broadcast_to([128, 8, 128]), ALU.mult,
                     kA + ["rwm"], ["NMb"])
                C.tt("dve", NMk[:], pB[:].rearrange("p (h n) -> p h n", h=8), mask1.unsqueeze(1).broadcast_to([128, 8, 128]), ALU.mult,
                     kB + ["rwm"], ["NMk"])
                C.cp("pool", PP[0][:, :, 0:64], NMb[:, :, 0:64], ["NMb"], kP(0))
                C.tt("dve", PP[0][:, :, 64:128], pC[:, 0:512].rearrange("p (h n) -> p h n", h=8), maskT.unsqueeze(1).broadcast_to([128, 8, 64]),
                     ALU.mult, ["pC", "rwm"], kP(0))
                for h in range(8):
                    C.mm(pA[:, h * 128:h * 128 + 64], C.identb[:], tk[:, 0, h * 64:(h + 1) * 64], h % 4 == 0, True, ["identb", "tk", "NMb"], [("pA", h // 4)],
                         skip_group_check=True)
                for c in range(2):
                    oc_ = slice(64 * c, 64 * c + 64)
                    for h in range(8):
                        C.mm(pA[oc_, h * 128 + 64:(h + 1) * 128], NMk[oc_, h, 0:64], ci[b][oc_, 1, h * 64:(h + 1) * 64], False, True,
                             ["NMk", kci], [("pA", h // 4)], skip_group_check=True)
                pA3 = pA[:].rearrange("p (h n) -> p h n", h=8)
                pB3 = pB[:].rearrange("p (h n) -> p h n", h=8)
                for g2 in range(2):
                    hs = slice(4 * g2, 4 * g2 + 4)
                    C.act(Xs[:, hs, :], pA3[:, hs, :], AF.Copy, [("pA", g2)], [("Xs", g2)])
                for rd in range(6):
                    pc_, pn_ = PP[rd % 2], PP[(rd + 1) % 2]
                    for g2 in range(2):
                        hs = slice(4 * g2, 4 * g2 + 4)
                        kc, kn = ("PP", rd % 2, g2), ("PP", (rd + 1) % 2, g2)
                        for c in range(2):
                            oc_ = slice(64 * c, 64 * c + 64)
                            for h in range(4 * g2, 4 * g2 + 4):
                                C.mm(pA[oc_, h * 128:(h + 1) * 128], pc_[oc_, h, 0:64], Xs[oc_, h, :], False, True, [kc, ("Xs", g2)], [("pA", g2)],
                                     skip_group_check=True)
                        if rd < 5:
                            for c in range(2):
                                oc_ = slice(64 * c, 64 * c + 64)
                                for h in range(4 * g2, 4 * g2 + 4):
                                    C.mm(pB[oc_, h * 128:h * 128 + 64], pc_[oc_, h, 64:128], pc_[oc_, h, 0:64], h % 4 == 0, True, [kc], [("pB", g2)],
                                         skip_group_check=True)
                                    C.mm(pB[oc_, h * 128 + 64:(h + 1) * 128], pc_[oc_, h, 0:64], pc_[oc_, h, 64:128], False, True, [kc], [("pB", g2)],
                                         skip_group_check=True)
                            C.cp("dve", pn_[:, hs, :], pB3[:, hs, :], [("pB", g2)], [kn])
                        C.act(Xs[:, hs, :], pA3[:, hs, :], AF.Copy, [("pA", g2)], [("Xs", g2)])
                for c in range(2):
                    oc_ = slice(64 * c, 64 * c + 64)
                    for h in range(8):
                        g = c * 8 + h
                        fs = (g % 8 == 0)
                        kx_ = ("Xs", h // 4)
                        C.mm(pC[0:64, g * 64:(g + 1) * 64], Xs[oc_, h, 0:64], tk[oc_, 4, h * 64:(h + 1) * 64], fs, True, [kx_, "tk"], ["pC", "pC2"],
                             skip_group_check=True)
                        C.mm(pB[0:64, g * 64:(g + 1) * 64], tk[oc_, 4, h * 64:(h + 1) * 64], Xs[oc_, h, 64:128], fs, False, [kx_, "tk"], [("pB", c)],
                             skip_group_check=True)
                        C.mm(pB[0:64, g * 64:(g + 1) * 64], tk[oc_, 5, h * 64:(h + 1) * 64], ci[b][oc_, 1, h * 64:(h + 1) * 64], False, True,
                             ["tk", kci], [("pB", c)], skip_group_check=True)
                        C.mm(pA[0:64, g * 64:(g + 1) * 64], Xs[oc_, h, 0:64], NMb[oc_, h, 64:128], fs, False, [kx_, "NMb"], [("pA", c)],
                             skip_group_check=True)
                for par, c, h in [(par, c, h) for par in range(2) for c in range(2) for h in range(par, 8, 2)]:
                    hp, base = h // 2, 64 * (h % 2)
                    ps_ = slice(base, base + 64)
                    g = c * 8 + h
                    C.mm(pA[0:64, g * 64:(g + 1) * 64], C.identb[ps_, base:base + 64], fT[ps_, hp, c, 1, :], False, True, ["identb", "fT"], [("pA", c)],
                         skip_group_check=True)
                for c in range(2):
                    C.tt("pool", dgam[:, c * 8:(c + 1) * 8, :], id64[:, 0:8, :], gam[:, :, c].unsqueeze(2).broadcast_to([64, 8, 64]), ALU.mult,
                         ["id64", "gam"], ["dgam"])
                C.tt("dve", Gs[:], dgam[:], pC[0:64, :].rearrange("p (g n) -> p g n", g=16), ALU.add, ["dgam", "pC", "pC2"], ["Gs"])
                C.act(Hs[:], pB[0:64, :].rearrange("p (g n) -> p g n", g=16), AF.Copy, kB, ["Hs"])
                C.act(Qs[:], pA[0:64, :].rearrange("p (g n) -> p g n", g=16), AF.Copy, kA, ["Qs"])
                for c in ([0, 1] if dr == 0 else [1, 0]):
                    oc_ = slice(64 * c, 64 * c + 64)
                    for h in range(8):
                        o = pC[oc_, h * 64:(h + 1) * 64]
                        C.mm(o, NMb[oc_, h, 64:128], Xs[oc_, h, 64:128], h == 0, False, ["NMb", ("Xs", h // 4), "Gs"], ["pC"], skip_group_check=True)
                        C.mm(o, NMk[oc_, h, 64:128], ci[b][oc_, 1, h * 64:(h + 1) * 64], False, False, ["NMk", kci], ["pC"], skip_group_check=True)
                        C.mm(o, Qs[:, c * 8 + h, :], STb[:, h * 64:(h + 1) * 64], False, True, ["Qs", "STb"], ["pC"], skip_group_check=True)
                    for h in range(8):
                        C.mm(pC[0:64, 512 + h * 64:512 + (h + 1) * 64], Gs[:, c * 8 + h, :], ST32[:, h * 64:(h + 1) * 64], h == 0, True,
                             ["Gs", "ST32"], ["pC2"], skip_group_check=True)
                    C.tt("dve", ST32[:], pC[0:64, 512:1024], Hs[:, c * 8:(c + 1) * 8, :], ALU.add, ["pC2", "Hs"], ["ST32"])
                    C.act(STb[:], ST32[:], AF.Copy, ["ST32"], ["STb"])
                if dr == 0:
                    ob = it % 2
                    C.act(yo[ob][:], pC[:, 0:512], AF.Copy, ["pC"], [("yo", ob)])
                    C.dma(yfv[:, q], yo[ob][:], [("yo", ob)], [("d", "rw_yf", q)], q="pool")
                    continue
                y = W["y"]
                C.tt("dve", y[:], pC[:, 0:512], yfi[b][:], ALU.add, ["pC", ("yfi", b)], ["w_y"])
                C.S.add("dve", lambda e: e.tensor_reduce(m8[:], v3(y[:]), AX.X, ALU.add), ["w_y"], ["m8"])
                C.ts("dve", m8[:], m8[:], 1.0 / 64, ALU.mult, ["m8"], ["m8"])
                C.tt("pool", v3(y[:]), v3(y[:]), bc8(m8[:]), ALU.subtract, ["w_y", "m8"], ["w_y"])
                C.tt("pool", W["sq"][:], y[:], y[:], ALU.mult, ["w_y"], ["w_sq"])
                C.S.add("dve", lambda e: e.tensor_reduce(q8[:], v3(W["sq"][:]), AX.X, ALU.add), ["w_sq"], ["q8"])
                C.ts("dve", q8[:], q8[:], 1.0 / 64, ALU.mult, ["q8"], ["q8"], s2=RW_GN_EPS, op1=ALU.add)
                C.act(q8[:], q8[:], AF.Sqrt, ["q8"], ["q8"])
                C.recip(q8[:], q8[:], ["q8"], ["q8"])
                C.tt("dve", v3(y[:]), v3(y[:]), bc8(q8[:]), ALU.mult, ["w_y", "q8"], ["w_y"])
                C.tt("pool", y[:], y[:], lnx[:, 0, :], ALU.mult, ["w_y", "lnx"], ["w_y"])
                C.tt("pool", y[:], y[:], lnx[:, 1, :], ALU.add, ["w_y", "lnx"], ["w_y"])
                C.tt("dve", y[:], y[:], ci[b][:, 4, :], ALU.add, ["w_y", kci], ["w_y"])
                C.tt("dve", y3[:], y[:], ci[b][:, 3, :], ALU.mult, ["w_y", kci], ["ry3"])
                for c4 in range(4):
                    C.tr(pT[:, c4 * 128:(c4 + 1) * 128], y3[:, c4 * 128:(c4 + 1) * 128], C.identb[:], ["ry3", "identb"], ["spT"])
                C.act(yT[b][:], pT[:, 0:512].rearrange("p (c l) -> p c l", c=4), AF.Copy, ["spT"], [("ryT", b)])
                C.dma(yav[:, :, q * 128:(q + 1) * 128], yT[b][:], [("ryT", b)], [("d", "yaT", q)], q="pool")
            C.S.barrier()
    C.S.barrier()


def stage_rwkv(C, l):
    rw_front(C, l)
    if not getattr(C, "no_rwscan", False):
        rw_scan(C, l)

def stage_out(C, l, src, dst):
    L = C.L
    TT = 256
    NT = L // TT
    with contextlib.ExitStack() as st:
        wg = C.sb(st, "wg", [128, 8, 3 * D], BF16)
        wp = C.sb(st, "wp", [128, 3, 4, D], BF16)
        wo = C.sb(st, "wout", [128, 8, D], BF16)
        with contextlib.ExitStack() as st2:
            load_w_bf16(C, st2, lambda k, a, b: wg[:, k, a:b], C.d["mix_w_in"][l][:, O_G:IN_COLS], 8, 3 * D, "wg", blk=1024)
            for br, nm in enumerate(["proj_a", "proj_b", "proj_c"]):
                load_w_bf16(C, st2, (lambda br: (lambda k, a, b: wp[:, br, k, a:b]))(br), C.d[nm][l], 4, D, "wp", blk=1024)
            load_w_bf16(C, st2, lambda k, a, b: wo[:, k, a:b], C.d["mix_w_out"][l], 8, D, "wout", blk=1024)
            C.S.barrier()
        xt = [C.sb(st, f"xt{i}", [128, 8, TT], F32) for i in range(3)]
        hTb = [C.sb(st, f"hT{i}", [128, 8, TT], BF16) for i in range(2)]
        yin = [C.sb(st, f"yin{i}", [128, 3, 4, TT], BF16) for i in range(2)]
        sg = [C.sb(st, f"sg{i}", [128, TT], F32) for i in range(4)]
        tm = [C.sb(st, f"tm{i}", [128, TT], F32) for i in range(6)]
        mT = C.sb(st, "mT", [128, 8, TT], BF16)
        zsq = C.sb(st, "zsq", [128, 8, TT], F32)
        mean = C.sb(st, "mean", [128, TT], F32)
        msq = C.sb(st, "msq", [128, TT], F32)
        rstd = C.sb(st, "rstd", [128, TT], F32)
        pgp = [C.ps(st, f"pgp{i}", [128, 512], F32) for i in range(4)]
        py = [C.ps(st, f"py{i}", [128, 512], F32) for i in range(4)]
        srcv = src.rearrange("(c p) l -> p c l", p=128)
        dstv = dst.rearrange("(c p) l -> p c l", p=128)
        yv = [C.d[n].rearrange("(c p) l -> p c l", p=128) for n in ["yaT", "ybT", "ycT"]]
        mg = C.modT[l]

        def load(t):
            sl = slice(t * TT, (t + 1) * TT)
            C.dma(xt[t % 3][:], srcv[:, :, sl], [("d", src.name, t)], [("xt", t % 3)])
            for br in range(3):
                C.dma(yin[t % 2][:, br], yv[br][:, :, sl], [], [("yin", t % 2)], q="act" if br == 1 else "sp")

        def prep(t):
            x, kx = xt[t % 3], ("xt", t % 3)
            modulate_tile(C, hTb[t % 2], x, kx, l, 3, TT, hkey=("hT", t % 2))
            C.ts("pool", x[:], x[:], float(ALPHA), ALU.mult, [kx], [kx])

        def tail_pieces(t):
            x, kx = xt[t % 3], ("xt", t % 3)
            ps = layer_norm_pieces(C, x, kx, TT, zsq, mean, msq, rstd, py[0], ("py", 0), py[1], ("py", 1), l, 1)
            ps.append(lambda: C.dma(dstv[:, :, t * TT:(t + 1) * TT], x[:], [kx], [("d", dst.name, t)], q="pool"))
            return ps

        load(0)
        prep(0)
        pend = []
        n = 0
        for t in range(NT):
            b = t % 2
            x, kx = xt[t % 3], ("xt", t % 3)
            hT, kh = hTb[b], ("hT", b)
            if t + 1 < NT:
                load(t + 1)
            for d in range(8):
                for br in range(3):
                    gb = n % 4
                    n += 1
                    tb = (d % 2) * 3 + br
                    for k in range(8):
                        C.mm(pgp[gb][:, 0:TT], wg[:, k, br * D + d * 128:br * D + (d + 1) * 128], hT[:, k, :], k == 0, k == 7,
                             ["wg", kh], [("pgp", gb)])
                    for k in range(4):
                        C.mm(pgp[gb][:, TT:2 * TT], wp[:, br, k, d * 128:(d + 1) * 128], yin[b][:, br, k, :], k == 0, k == 3,
                             ["wp", ("yin", b)], [("pgp", gb)])
                    C.act(sg[gb][:], pgp[gb][:, 0:TT], AF.Sigmoid, [("pgp", gb)], [("sg", gb)])
                    C.tt("dve", tm[tb][:], sg[gb][:], pgp[gb][:, TT:2 * TT], ALU.mult, [("sg", gb), ("pgp", gb)], [("tm", tb)])
                    if (d, br) >= (0, 1) and pend:
                        pend.pop(0)()
                    if (d, br) == (5, 0) and t + 1 < NT:
                        prep(t + 1)
                t0_ = (d % 2) * 3
                C.tt("dve", tm[t0_][:], tm[t0_][:], tm[t0_ + 1][:], ALU.add, [("tm", t0_), ("tm", t0_ + 1)], [("tm", t0_)])
                C.tt("dve", mT[:, d, :], tm[t0_][:], tm[t0_ + 2][:], ALU.add, [("tm", t0_), ("tm", t0_ + 2)], ["mT"])
            while pend:
                pend.pop(0)()
            for d in range(8):
                pyd = py[d // 2][:, (d % 2) * TT:(d % 2 + 1) * TT]
                for k in range(8):
                    C.mm(pyd, wo[:, k, d * 128:(d + 1) * 128], mT[:, k, :], k == 0, k == 7, ["wout", "mT"], [("py", d // 2)])
                C.stt(x[:, d, :], pyd, mg[:, 40 + d:40 + d + 1], x[:, d, :], ALU.mult, ALU.add, [("py", d // 2), kx, ("modT", l)], [kx])
            pend = tail_pieces(t)
        while pend:
            pend.pop(0)()
    C.S.barrier()

STAGES = ["ffn1", "inproj", "ssd", "hyena", "rwkv", "out", "ffn2"]


def build(L, upto="ffn2", depth=DEPTH, debug=()):
    C = Ctx(L)
    C.no_rwscan = 'norwscan' in debug
    dbg = lambda n: n in debug
    C.din("x", [L, D])
    C.din("c", [128, 8])
    C.din("c_ident", [128, 128])
    C.din("c_masks", [128, 6, 128])
    C.din("ada_w", [DEPTH, D, 9 * D])
    C.din("ada_b", [DEPTH, 128, 72])
    C.din("ln_w", [128, DEPTH * 3 * 8])
    C.din("ln_b", [128, DEPTH * 3 * 8])
    C.din("ffn_wi", [DEPTH, 2, D, 2 * D_FF])
    C.din("ffn_wo", [DEPTH, 2, D_FF, D])
    C.din("mix_w_in", [DEPTH, D, IN_COLS])
    C.din("ssd_cw", [DEPTH, 128, 40])
    C.din("ssd_cb", [DEPTH, 128, 8])
    C.din("ssd_dtb", [DEPTH, 128, 16])
    C.din("ssd_alog", [DEPTH, 128, 16])
    C.din("ssd_Dbc", [DEPTH, 128, 512])
    C.din("ssd_nw", [DEPTH, 128, 512])
    C.din("c_featsT", [33, L])
    C.din("c_distb", [128, L])
    C.din("c_ndelta", [128, 4])
    C.din("c_F256", [128, 512], BF16)
    C.din("c_Tw", [128, 768])
    C.din("c_L4", [128, 4, 64], BF16)
    C.din("c_G3", [128, 3, 128], BF16)
    C.din("c_TAB", [128, 2, 2, 128])
    C.din("c_FIN", [128, 2, 2, 128], BF16)
    C.din("hy_f_w1", [DEPTH, 33, 64])
    C.din("hy_f_w2", [DEPTH, 64, 64])
    C.din("hy_f_w3", [DEPTH, 64, 512])
    C.din("hy_fb", [DEPTH, 64, 2])
    C.din("hy_freq", [DEPTH, 64, 2])
    C.din("hy_bias", [DEPTH, 128, 4])
    C.din("hy_cw", [DEPTH, 128, 36])
    C.din("hy_cb", [DEPTH, 128, 12])
    C.din("proj_a", [DEPTH, 512, D])
    C.din("proj_b", [DEPTH, 512, D])
    C.din("proj_c", [DEPTH, 512, D])
    C.din("mix_w_out", [DEPTH, D, D])
    C.din("c_rwm", [2, 128, 5, 128])
    C.din("rw_mu", [DEPTH, 128, 2, 15])
    C.din("rw_w2", [DEPTH, 128, 512])
    C.din("rw_a2", [DEPTH, 128, 512])
    C.din("rw_g2", [DEPTH, 128, 512])
    C.din("rw_bc", [DEPTH, 128, 7, 512])
    C.din("rw_lnx", [DEPTH, 128, 2, 512])
    C.dout("out", [L, D])
    xa = C.dscr("xTa", [D, L], debug=dbg("xTa"))
    xb = C.dscr("xTb", [D, L], debug=dbg("xTb"))
    C.dscr("projA", [A_COLS, L], BF16, debug=dbg("projA"))
    C.dscr("projB", [B_COLS, L], BF16, debug=dbg("projB"))
    C.dscr("projX", [XBC, L], BF16, debug=dbg("projX"))
    C.dscr("z_tok", [L, 512], BF16, debug=dbg("z_tok"))
    C.dscr("dt_tok", [L, 16], F32, debug=dbg("dt_tok"))
    C.dscr("ssd_bcT", [512, L], BF16, debug=dbg("ssd_bcT"))
    C.dscr("ssd_xbtok", [L, 768], BF16, debug=dbg("ssd_xbtok"))
    C.dscr("ycT", [512, L], BF16, debug=dbg("ycT"))
    C.dscr("hy_hT", [512, L], BF16, debug=dbg("hy_hT"))
    C.dscr("hy_x0T", [512, L], BF16, debug=dbg("hy_x0T"))
    C.dscr("hy_zT", [512, L], BF16, debug=dbg("hy_zT"))
    C.dscr("hy_Hf", [256, 128, 768], BF16, debug=dbg("hy_Hf"))
    C.dscr("ybT", [512, L], BF16, debug=dbg("ybT"))
    C.dscr("rw_c", [L, 5, 512], BF16, debug=dbg("rw_c"))
    C.dscr("rw_d", [2, L, 2, 512], BF16, debug=dbg("rw_d"))
    C.dscr("rw_sg", [2, L, 512], F32, debug=dbg("rw_sg"))
    C.dscr("rw_yf", [L, 512], F32, debug=dbg("rw_yf"))
    C.dscr("yaT", [512, L], BF16, debug=dbg("yaT"))
    last = STAGES.index(upto)
    with contextlib.ExitStack() as stp:
        stage_consts(C, stp)
        stage_transpose(C, C.d["x"], xa, True)
        cur, nxt = xa, xb
        for l in range(depth):
            stage_mod(C, l, stp)
            stage_ffn(C, l, 0, cur, nxt, 0, 0)
            cur, nxt = nxt, cur
            if last >= 1:
                stage_inproj(C, l, cur)
            if last >= 2 and "nossd" not in debug:
                stage_ssd(C, l)
            if last >= 3 and "nohy" not in debug:
                stage_hyena(C, l)
            if last >= 4:
                stage_rwkv(C, l)
            if last >= 5:
                stage_out(C, l, cur, nxt)
                cur, nxt = nxt, cur
            if last >= 6:
                stage_ffn(C, l, 1, cur, nxt, 6, 2)
                cur, nxt = nxt, cur
        stage_transpose(C, cur, C.d["out"], False)
        C.S.emit()
    return C


def fm(v, ncol):
    v = np.asarray(v, np.float32)
    sh = v.shape[:-1]
    return np.ascontiguousarray(np.swapaxes(v.reshape(sh + (ncol, 128)), -1, -2))


def bc128(v):
    v = np.asarray(v, np.float32)
    return np.ascontiguousarray(np.broadcast_to(v[:, None, :], (v.shape[0], 128, v.shape[1])))


def const_masks():
    k = np.arange(128)[:, None]
    s = np.arange(128)[None, :]
    m = np.zeros((128, 6, 128), np.float32)
    m[:, 0] = k > s
    m[:, 1] = k <= s
    m[:, 2] = k < s
    m[:, 3] = k >= s
    m[:, 4] = NEG * (s < k)
    m[:, 5] = NEG * (s > k)
    return m


def hyena_consts(L):
    bf = ml_dtypes.bfloat16
    m = {}
    t = np.linspace(0.0, 1.0, L, dtype=np.float32)
    w = (2.0 * np.float32(math.pi) * np.arange(L, dtype=np.float32) / np.float32(L)).astype(np.float32)
    f = np.linspace(1e-4, 15, 16, dtype=np.float32)
    wf = (w[:, None] * f).astype(np.float32)
    feats = np.concatenate([t[:, None], np.cos(wf), np.sin(wf)], -1).astype(np.float32)
    m["c_featsT"] = np.ascontiguousarray(feats.T)
    dist = np.abs(2.0 * t - 1.0).astype(np.float32)
    m["c_distb"] = np.ascontiguousarray(np.broadcast_to(dist[None, :], (128, L)))
    deltas = np.abs(np.linspace(math.log(1e-2) / 1.5, math.log(1e-2) / 0.3, 512, dtype=np.float32))
    m["c_ndelta"] = fm(-deltas, 4)
    N = 16384
    N2 = 64
    N1 = N // N2
    n1 = np.arange(128, dtype=np.float64)[:, None]
    k1 = np.arange(N1, dtype=np.float64)[None, :]
    m["c_F256"] = np.concatenate([np.cos(2 * np.pi * n1 * k1 / N1), -np.sin(2 * np.pi * n1 * k1 / N1)], 1).astype(bf)
    n2 = np.arange(N2, dtype=np.float64)[:, None]
    Tr = np.cos(2 * np.pi * n2 * k1 / N)
    Ti = -np.sin(2 * np.pi * n2 * k1 / N)
    T = np.concatenate([Tr, Ti, Tr], 1)
    m["c_Tw"] = np.concatenate([T, T], 0).astype(np.float32)
    k2 = np.arange(N2, dtype=np.float64)[None, :]
    c64 = np.cos(2 * np.pi * n2 * k2 / N2)
    s64 = np.sin(2 * np.pi * n2 * k2 / N2)
    L4 = np.stack([c64, -c64, s64, -s64], 1)
    m["c_L4"] = np.concatenate([L4, L4], 0).astype(bf)
    G = np.stack([np.concatenate([c64, s64], 1), np.concatenate([-c64, -s64], 1), np.concatenate([-s64, c64], 1)], 1)
    m["c_G3"] = np.concatenate([G, G], 0).astype(bf)
    kk = np.arange(N1, dtype=np.float64)[:, None]
    nn = np.arange(N2, dtype=np.float64)[None, :]
    tc = np.cos(2 * np.pi * kk * nn / N)
    ts = np.sin(2 * np.pi * kk * nn / N)
    TA = np.concatenate([tc, -ts], 1).reshape(2, 128, 128)
    TB = np.concatenate([ts, tc], 1).reshape(2, 128, 128)
    m["c_TAB"] = np.ascontiguousarray(np.transpose(np.stack([TA, TB], 0), (2, 0, 1, 3))).astype(np.float32)
    n1w = np.arange(N1 // 4, N1 // 4 + 128, dtype=np.float64)[None, :]
    cf = (np.cos(2 * np.pi * kk * n1w / N1) / N).reshape(2, 128, 128)
    sf = (-np.sin(2 * np.pi * kk * n1w / N1) / N).reshape(2, 128, 128)
    m["c_FIN"] = np.ascontiguousarray(np.transpose(np.stack([cf, sf], 1), (2, 0, 1, 3))).astype(bf)
    return m


def rwkv_masks():
    m = np.zeros((2, 128, 5, 128), np.float32)
    s = np.arange(64)[:, None]
    t = np.arange(64)[None, :]
    for d in range(2):
        inc = (s <= t) if d == 0 else (s >= t)
        strict = (s < t) if d == 0 else (s > t)
        inc = inc.astype(np.float32)
        strict = strict.astype(np.float32)
        for c in range(2):
            sl = slice(64 * c, 64 * c + 64)
            m[d, sl, 0, sl] = inc
            m[d, sl, 1, sl] = strict
            m[d, sl, 2, sl] = strict.T
            m[d, sl, 3, 0:64] = strict
            m[d, sl, 3, 64:128] = inc
            m[d, sl, 4, 0:64] = strict.T
            m[d, sl, 4, 64 + c] = 1.0
    return m


def make_shared(inp, L):
    m = hyena_consts(L)
    m["c_rwm"] = rwkv_masks()
    for nm in ["proj_a", "proj_b", "proj_c", "mix_w_out"]:
        m[nm] = np.ascontiguousarray(np.asarray(inp[nm], np.float32))
    mu = fm(inp["rw_mu"], 15)
    m["rw_mu"] = np.ascontiguousarray(np.transpose(mu, (0, 2, 1, 3)))
    m["rw_w2"] = np.ascontiguousarray(np.asarray(inp["rw_w2"], np.float32).reshape(DEPTH, 128, 512))
    m["rw_a2"] = np.ascontiguousarray(np.asarray(inp["rw_a2"], np.float32).reshape(DEPTH, 128, 512))
    m["rw_g2"] = np.ascontiguousarray(np.asarray(inp["rw_g2"], np.float32))
    w0 = np.asarray(inp["rw_w0"], np.float32)
    a0 = np.asarray(inp["rw_a0"], np.float32)
    vecs = np.stack([w0[:, 0], w0[:, 1], a0[:, 0], a0[:, 1], np.asarray(inp["rw_kk"], np.float32), np.asarray(inp["rw_ka"], np.float32),
                     np.asarray(inp["rw_rk"], np.float32).reshape(DEPTH, 512)], 1)
    m["rw_bc"] = np.ascontiguousarray(np.broadcast_to(vecs[:, None], (DEPTH, 128, 7, 512)))
    lnx = np.stack([np.asarray(inp["rw_lnx_w"], np.float32), np.asarray(inp["rw_lnx_b"], np.float32)], 1)
    m["rw_lnx"] = np.ascontiguousarray(np.broadcast_to(lnx[:, None], (DEPTH, 128, 2, 512)))
    f32 = lambda a: np.ascontiguousarray(np.asarray(a, dtype=np.float32))
    m["hy_f_w1"] = f32(inp["hy_f_w1"])
    m["hy_f_w2"] = f32(inp["hy_f_w2"])
    m["hy_f_w3"] = f32(inp["hy_f_w3"])
    m["hy_fb"] = f32(np.stack([np.asarray(inp["hy_f_b1"]), np.asarray(inp["hy_f_b2"])], -1))
    m["hy_freq"] = f32(np.swapaxes(np.asarray(inp["hy_freq"]), 1, 2))
    m["hy_bias"] = fm(inp["hy_bias"], 4)
    hcw = fm(inp["hy_conv_w"], 12)
    m["hy_cw"] = np.ascontiguousarray(np.transpose(hcw, (0, 2, 3, 1)).reshape(DEPTH, 128, 36))
    m["hy_cb"] = fm(inp["hy_conv_b"], 12)
    m["c_ident"] = np.eye(128, dtype=np.float32)
    m["c_masks"] = const_masks()
    m["ada_w"] = f32(inp["ada_w"])
    m["ada_b"] = fm(inp["ada_b"], 72)
    lw = fm(np.asarray(inp["ln_w"]).reshape(DEPTH * 3, D), 8)
    lb = fm(np.asarray(inp["ln_b"]).reshape(DEPTH * 3, D), 8)
    m["ln_w"] = np.ascontiguousarray(np.transpose(lw, (1, 0, 2)).reshape(128, -1))
    m["ln_b"] = np.ascontiguousarray(np.transpose(lb, (1, 0, 2)).reshape(128, -1))
    m["ffn_wi"] = f32(inp["ffn_wi"])
    m["ffn_wo"] = f32(inp["ffn_wo"])
    m["mix_w_in"] = f32(inp["mix_w_in"])
    cw = fm(inp["ssd_conv_w"], 8)
    m["ssd_cw"] = np.ascontiguousarray(np.transpose(cw, (0, 2, 3, 1)).reshape(DEPTH, 128, 40))
    m["ssd_cb"] = fm(inp["ssd_conv_b"], 8)
    m["ssd_dtb"] = bc128(np.asarray(inp["ssd_dt_bias"]).reshape(DEPTH, 16))
    m["ssd_alog"] = bc128(np.asarray(inp["ssd_A_log"]).reshape(DEPTH, 16))
    m["ssd_Dbc"] = bc128(np.repeat(np.asarray(inp["ssd_D"]), 64, axis=-1))
    m["ssd_nw"] = bc128(inp["ssd_norm_w"])
    return m


def make_inputs(inp, b, shared=None):
    m = dict(shared if shared is not None else make_shared(inp, inp['x'].shape[1]))
    m["x"] = np.ascontiguousarray(np.asarray(inp["x"][b], dtype=np.float32))
    m["c"] = fm(inp["c"][b], 8)
    return m


def kernel(**inputs):
    inp = {k: np.asarray(v) for k, v in inputs.items()}
    B, L, _ = inp["x"].shape
    C = build(L)
    shared = make_shared(inp, L)
    in_maps = [make_inputs(inp, b, shared) for b in range(B)]
    res = run_bass_kernel_spmd(C.nc, in_maps, core_ids=list(range(B)))
    return np.stack([r["out"] for r in res.results], axis=0).astype(np.float32)
```

```python
import contextlib
import math
import numpy as np
import ml_dtypes
import concourse.bass as bass
import concourse.mybir as mybir
from concourse.bass_utils import run_bass_kernel_spmd

F32 = mybir.dt.float32
BF16 = mybir.dt.bfloat16
AF = mybir.ActivationFunctionType
ALU = mybir.AluOpType
AX = mybir.AxisListType

D = 1024
DEPTH = 2
ALPHA = (2 * DEPTH) ** 0.25
LN_EPS = 1e-5
D_FF = 2816
NFC = D_FF // 128

ENGS = ["pe", "act", "dve", "pool", "sp"]
NDSEM = 12


class Op:
    __slots__ = ("eng", "fn", "idx", "dma", "waits", "has_dep", "ms", "dsem", "dval", "prewait")

    def __init__(self, eng, fn, dma):
        self.eng = eng
        self.fn = fn
        self.dma = dma
        self.waits = []
        self.has_dep = False
        self.ms = None
        self.dsem = None
        self.dval = None
        self.prewait = None


class Sched:
    def __init__(self, nc, same_engine_sync=True):
        self.nc = nc
        self.ops = {e: [] for e in ENGS}
        self.last_w = {}
        self.readers = {}
        self.known = {e: {f: -1 for f in ENGS} for e in ENGS}
        self.known_dma = {e: set() for e in ENGS}
        self.ndma = {e: 0 for e in ENGS}
        self.dma_hist = {e: [] for e in ENGS}
        self.same = same_engine_sync
        self.all_dma = []
        self.pending_barrier = {e: None for e in ENGS}
        self.excl_last = {}
        self.pe_rg = {}

    def add(self, eng, fn, reads=(), writes=(), dma=False, excl=(), rowid=None, rgkeys=()):
        op = Op(eng, fn, dma)
        op.idx = len(self.ops[eng])
        deps = []
        for k in excl:
            d_ = self.excl_last.setdefault(k, {})
            for e2, o in d_.items():
                if e2 != eng:
                    deps.append((o, "excl"))
            d_[eng] = op
        if eng == "pe" and rowid is not None:
            for k in rgkeys:
                prev = self.pe_rg.get(k)
                if prev is not None and prev[0] != rowid:
                    o = prev[1]
                    if self.known["pe"]["pe"] < o.idx:
                        self.known["pe"]["pe"] = o.idx
                        o.has_dep = True
                        op.waits.append(o)
                self.pe_rg[k] = (rowid, op)
        for k in reads:
            w = self.last_w.get(k)
            if w is not None:
                deps.append((w, "raw"))
        for k in writes:
            w = self.last_w.get(k)
            if w is not None:
                deps.append((w, "waw"))
            r = self.readers.get(k)
            if r:
                for o in r[0].values():
                    deps.append((o, "war"))
                for o in r[1]:
                    deps.append((o, "war"))
        pb = self.pending_barrier[eng]
        if pb is not None:
            for o in pb:
                deps.append((o, "bar"))
            self.pending_barrier[eng] = None
        for d, kind in deps:
            self._need(op, d, kind)
        for k in reads:
            r = self.readers.setdefault(k, ({}, []))
            if dma:
                r[1].append(op)
            else:
                r[0][eng] = op
        for k in writes:
            self.last_w[k] = op
            self.readers[k] = ({}, [])
        if dma:
            n = self.ndma[eng]
            self.ndma[eng] = n + 1
            nd = 4 if eng == "pool" else NDSEM
            op.dsem = n % nd
            op.dval = 16 * (n // nd + 1)
            if n >= nd:
                prev = self.dma_hist[eng][n - nd]
                if prev not in self.known_dma[eng]:
                    op.prewait = prev
                    self.known_dma[eng].add(prev)
            self.dma_hist[eng].append(op)
            self.all_dma.append(op)
        self.ops[eng].append(op)
        return op

    def _need(self, op, d, kind):
        e = op.eng
        if d is op:
            return
        if d.dma:
            if d in self.known_dma[e]:
                return
            self.known_dma[e].add(d)
            op.waits.append(d)
            return
        if d.eng == e:
            if e == "pe" or kind in ("war", "waw") or e not in self.same:
                return
        if self.known[e][d.eng] >= d.idx:
            return
        self.known[e][d.eng] = d.idx
        d.has_dep = True
        op.waits.append(d)

    def barrier(self):
        lst = []
        for e in ENGS:
            for o in reversed(self.ops[e]):
                if not o.dma and o.fn is not None:
                    lst.append(o)
                    break
        lst.extend(self.all_dma)
        self.all_dma = []
        for e in ENGS:
            prev = self.pending_barrier[e]
            self.pending_barrier[e] = (list(prev) if prev else []) + lst
        self.last_w = {}
        self.readers = {}

    def emit(self):
        nc = self.nc
        self.barrier()
        self.add("sp", None)
        for e in ENGS:
            c = 0
            for o in self.ops[e]:
                if not o.dma and o.has_dep:
                    c += 1
                    o.ms = c
            print(f"[sched] {e}: {len(self.ops[e])} ops, {c} milestones, {self.ndma[e]} dmas", flush=True)
        with contextlib.ExitStack() as st:
            csem = {e: st.enter_context(nc.semaphore(f"c_{e}")) for e in ENGS}
            dsem = {e: [st.enter_context(nc.semaphore(f"d_{e}_{i}")) for i in range(NDSEM)]
                    for e in ENGS if self.ndma[e] > 0}
            block = st.enter_context(nc.Block())

            def run(e, eng):
                for o in self.ops[e]:
                    if o.prewait is not None:
                        p = o.prewait
                        eng.wait_ge(dsem[p.eng][p.dsem], p.dval)
                    for d in o.waits:
                        if d.dma:
                            eng.wait_ge(dsem[d.eng][d.dsem], d.dval)
                        else:
                            eng.wait_ge(csem[d.eng], d.ms)
                    if o.fn is None:
                        continue
                    ins = o.fn(eng)
                    if o.dma:
                        ins.then_inc(dsem[e][o.dsem], 16)
                    elif o.ms is not None:
                        ins.then_inc(csem[e], 1)

            @block.tensor
            def _(eng):
                run("pe", eng)

            @block.scalar
            def _(eng):
                run("act", eng)

            @block.vector
            def _(eng):
                run("dve", eng)

            @block.gpsimd
            def _(eng):
                run("pool", eng)

            @block.sync
            def _(eng):
                run("sp", eng)


class Ctx:
    def __init__(self, L):
        self.L = L
        self.nc = bass.Bass("TRN2", target_bir_lowering=False)
        import os
        self.S = Sched(self.nc, same_engine_sync=os.environ.get("SAME", "act,dve,pool").split(","))
        self.d = {}
        self.uid = 0

    def din(self, name, shape, dt=F32):
        self.d[name] = self.nc.dram_tensor(name, list(shape), dt, kind="ExternalInput").ap()
        return self.d[name]

    def dout(self, name, shape, dt=F32):
        self.d[name] = self.nc.dram_tensor(name, list(shape), dt, kind="ExternalOutput").ap()
        return self.d[name]

    def dscr(self, name, shape, dt=F32, debug=False):
        kind = "ExternalOutput" if debug else "Internal"
        self.d[name] = self.nc.dram_tensor(name, list(shape), dt, kind=kind).ap()
        return self.d[name]

    def sb(self, st, name, shape, dt):
        self.uid += 1
        return st.enter_context(self.nc.sbuf_tensor(f"{name}_{self.uid}", list(shape), dt))

    def ps(self, st, name, shape, dt=F32):
        self.uid += 1
        return st.enter_context(self.nc.psum_tensor(f"{name}_{self.uid}", list(shape), dt))

    @staticmethod
    def pk(*aps):
        keys = []
        for ap in aps:
            if ap is None or not hasattr(ap, "space") or str(ap.space) != "PSUM":
                continue
            pat = ap.ap
            row = pat[0][0]
            col0 = ap.offset % row
            ext = 0
            for stp, cnt in pat[1:]:
                ext += abs(stp) * (cnt - 1)
            es = 2 if ap.dtype == BF16 else 4
            b0 = (col0 * es) // 2048
            b1 = ((col0 + ext) * es + es - 1) // 2048
            for bk in range(b0, b1 + 1):
                keys.append(("PS", ap.tensor.name, bk))
        return keys

    @staticmethod
    def rowid(ap):
        pat = ap.ap
        row = pat[0][0]
        return (ap.offset // row, pat[0][1])

    @classmethod
    def pkq(cls, ap):
        pat = ap.ap
        row = pat[0][0]
        p0 = ap.offset // row
        p1 = p0 + pat[0][1] - 1
        return [(k, q) for k in cls.pk(ap) for q in range(p0 // 32, p1 // 32 + 1)]

    def mm(self, out, lhsT, rhs, start, stop, r, w, **kw):
        return self.S.add("pe", lambda e: e.matmul(out, lhsT, rhs, start=start, stop=stop, **kw), r, w,
                          excl=self.pk(out), rowid=self.rowid(lhsT), rgkeys=self.pkq(out))

    def tr(self, out, in_, ident, r, w):
        return self.S.add("pe", lambda e: e.transpose(out, in_, ident), r, w, excl=self.pk(out), rowid=self.rowid(in_), rgkeys=self.pkq(out))

    def act(self, out, in_, func, r, w, bias=None, scale=None, eng="act"):
        kw = {}
        if bias is not None:
            kw["bias"] = bias
        if scale is not None:
            kw["scale"] = scale
        return self.S.add(eng, lambda e: e.activation(out, in_, func, **kw), r, w, excl=self.pk(out, in_))

    def raw(self, eng, fn, r, w, aps=()):
        return self.S.add(eng, fn, r, w, excl=self.pk(*aps))

    def tt(self, eng, out, in0, in1, op, r, w):
        return self.S.add(eng, lambda e: e.tensor_tensor(out, in0, in1, op), r, w, excl=self.pk(out, in0, in1))

    def ts(self, eng, out, in0, s1, op0, r, w, s2=None, op1=None):
        if op1 is None:
            return self.S.add(eng, lambda e: e.tensor_scalar(out, in0, s1, None, op0), r, w, excl=self.pk(out, in0))
        return self.S.add(eng, lambda e: e.tensor_scalar(out, in0, s1, s2, op0, op1), r, w, excl=self.pk(out, in0))

    def stt(self, out, in0, scalar, in1, op0, op1, r, w):
        return self.S.add("dve", lambda e: e.scalar_tensor_tensor(out, in0, scalar, in1, op0, op1), r, w,
                          excl=self.pk(out, in0, in1))

    def cp(self, eng, out, in_, r, w):
        return self.S.add(eng, lambda e: e.tensor_copy(out, in_), r, w, excl=self.pk(out, in_))

    def memset(self, eng, ap, val, w):
        return self.S.add(eng, lambda e: e.memset(ap, val), (), w, excl=self.pk(ap))

    def recip(self, out, in_, r, w):
        return self.S.add("dve", lambda e: e.reciprocal(out, in_), r, w, excl=self.pk(out, in_))

    def dma(self, out, in_, r, w, q="sp", **kw):
        return self.S.add(q, lambda e: e.dma_start(out, in_, **kw), r, w, dma=True)


def stage_consts(C, st):
    nc = C.nc
    C.ident = C.sb(st, "ident", [128, 128], F32)
    C.identb = C.sb(st, "identb", [128, 128], BF16)
    C.ones = C.sb(st, "ones", [128, 128], F32)
    C.dma(C.ident[:], C.d["c_ident"][:, :], ["d:c_ident"], ["ident"])
    C.cp("dve", C.identb[:], C.ident[:], ["ident"], ["identb"])
    C.memset("dve", C.ones[:], 1.0, ["ones"])


def stage_transpose(C, src, dst, to_feature_major):
    L = C.L
    with contextlib.ExitStack() as st:
        ib = [C.sb(st, f"ti{i}", [128, 1024], F32) for i in range(2)]
        ob = [C.sb(st, f"to{i}", [128, 1024], F32) for i in range(2)]
        pp = [C.ps(st, f"tp{i}", [128, 512], F32) for i in range(4)]
        if to_feature_major:
            dstv = dst.rearrange("(c p) l -> p c l", p=128)
            for t in range(L // 128):
                b = t % 2
                C.dma(ib[b][:], src[t * 128:(t + 1) * 128, :], [("d", src.name, t)], [("ti", b)])
                for c in range(8):
                    p = pp[(c // 4 + 2 * b) % 4]
                    C.tr(p[:, (c % 4) * 128:(c % 4 + 1) * 128], ib[b][:, c * 128:(c + 1) * 128], C.ident[:],
                         [("ti", b), "ident"], [("tp", (c // 4 + 2 * b) % 4)])
                for h in range(2):
                    k = (h + 2 * b) % 4
                    C.cp("dve" if h == 0 else "act", ob[b][:, h * 512:(h + 1) * 512], pp[k][:], [("tp", k)], [("to", b)]) \
                        if h == 0 else C.act(ob[b][:, h * 512:(h + 1) * 512], pp[k][:], AF.Copy, [("tp", k)], [("to", b)])
                C.dma(dstv[:, :, t * 128:(t + 1) * 128], ob[b][:].rearrange("p (c l) -> p c l", c=8),
                      [("to", b)], [("d", dst.name, t)], q="pool")
        else:
            srcv = src.rearrange("(c p) l -> p c l", p=128)
            for t in range(L // 128):
                b = t % 2
                C.dma(ib[b][:].rearrange("p (c l) -> p c l", c=8), srcv[:, :, t * 128:(t + 1) * 128],
                      [("d", src.name, t)], [("ti", b)])
                for c in range(8):
                    k = (c // 4 + 2 * b) % 4
                    C.tr(pp[k][:, (c % 4) * 128:(c % 4 + 1) * 128], ib[b][:, c * 128:(c + 1) * 128], C.ident[:],
                         [("ti", b), "ident"], [("tp", k)])
                for h in range(2):
                    k = (h + 2 * b) % 4
                    if h == 0:
                        C.cp("dve", ob[b][:, h * 512:(h + 1) * 512], pp[k][:], [("tp", k)], [("to", b)])
                    else:
                        C.act(ob[b][:, h * 512:(h + 1) * 512], pp[k][:], AF.Copy, [("tp", k)], [("to", b)])
                C.dma(dst[t * 128:(t + 1) * 128, :], ob[b][:], [("to", b)], [("d", dst.name, t)], q="pool")
    C.S.barrier()


def stage_mod(C, l, st_persist):
    if not hasattr(C, "modT"):
        C.modT = [C.sb(st_persist, f"modT{i}", [128, 72], F32) for i in range(DEPTH)]
        C.mod1p = [C.sb(st_persist, f"mod1p{i}", [128, 72], F32) for i in range(DEPTH)]
        C.modhg = [C.sb(st_persist, f"modhg{i}", [128, 72], F32) for i in range(DEPTH)]
        C.lnw = C.sb(st_persist, "lnw", [128, DEPTH * 3 * 8], F32)
        C.lnb = C.sb(st_persist, "lnb", [128, DEPTH * 3 * 8], F32)
        C.dma(C.lnw[:], C.d["ln_w"][:, :], [], ["lnw"])
        C.dma(C.lnb[:], C.d["ln_b"][:, :], [], ["lnb"])
    with contextlib.ExitStack() as st:
        cond = C.sb(st, "cond", [128, 8], F32)
        adab = C.sb(st, "adab", [128, 72], F32)
        wb = [C.sb(st, f"adaw{i}", [128, 8, 1024], F32) for i in range(2)]
        pm = C.ps(st, "pmod", [128, 512], F32)
        C.dma(cond[:], C.d["c"][:, :], [], ["cond"])
        C.dma(adab[:], C.d["ada_b"][l], [], ["adab"])
        C.act(cond[:], cond[:], AF.Silu, ["cond"], ["cond"])
        wv = C.d["ada_w"][l].rearrange("(c p) n -> p c n", p=128)
        for blk in range(9):
            b = blk % 2
            C.dma(wb[b][:], wv[:, :, blk * 1024:(blk + 1) * 1024], [], [("adaw", b)], q="sp" if b == 0 else "pool")
            for j in range(8):
                col = blk * 8 + j
                for k in range(8):
                    C.mm(pm[:, col:col + 1], wb[b][:, k, j * 128:(j + 1) * 128], cond[:, k:k + 1], k == 0, k == 7,
                         [("adaw", b), "cond"], ["pmod"])
        C.tt("dve", C.modT[l][:], pm[:, 0:72], adab[:], ALU.add, ["pmod", "adab"], [("modT", l)])
        C.ts("dve", C.mod1p[l][:], C.modT[l][:], 1.0, ALU.add, [("modT", l)], [("mod1p", l)])
        C.ts("dve", C.modhg[l][:], C.modT[l][:], 0.5, ALU.mult, [("modT", l)], [("modhg", l)])
    C.S.barrier()


def load_w_bf16(C, st, dst_fn, src, nk, ncols, key, blk=1408):
    stg = [C.sb(st, f"stg{i}", [128, blk], F32) for i in range(3)]
    i = 0
    for k in range(nk):
        for c0 in range(0, ncols, blk):
            c1 = min(ncols, c0 + blk)
            b = i % 3
            C.dma(stg[b][:, 0:c1 - c0], src[k * 128:(k + 1) * 128, c0:c1], [], [("stg", b)],
                  q="sp" if i % 2 == 0 else "act")
            C.cp("pool", dst_fn(k, c0, c1), stg[b][:, 0:c1 - c0], [("stg", b)], [key])
            i += 1


def stage_ffn(C, l, which, src, dst, mi, lni):
    L = C.L
    TT = 256
    NT = L // TT
    with contextlib.ExitStack() as st:
        wi = C.sb(st, "wi", [128, 8, 2 * D_FF], BF16)
        wo = C.sb(st, "wo", [128, NFC, D], BF16)
        with contextlib.ExitStack() as st2:
            load_w_bf16(C, st2, lambda k, a, b: wi[:, k, a:b], C.d["ffn_wi"][l, which], 8, 2 * D_FF, "wi")
            load_w_bf16(C, st2, lambda k, a, b: wo[:, k, a:b], C.d["ffn_wo"][l, which], NFC, D, "wo", blk=1024)
            C.S.barrier()
        xt = [C.sb(st, f"xt{i}", [128, 8, TT], F32) for i in range(3)]
        hTb = [C.sb(st, f"hT{i}", [128, 8, TT], BF16) for i in range(2)]
        sg = [C.sb(st, f"sg{i}", [128, TT], F32) for i in range(4)]
        actT = C.sb(st, "actT", [128, NFC, TT], BF16)
        zsq = C.sb(st, "zsq", [128, 8, TT], F32)
        mean = C.sb(st, "mean", [128, TT], F32)
        msq = C.sb(st, "msq", [128, TT], F32)
        rstd = C.sb(st, "rstd", [128, TT], F32)
        pgu = [C.ps(st, f"pgu{i}", [128, 512], F32) for i in range(4)]
        py = [C.ps(st, f"py{i}", [128, 512], F32) for i in range(4)]
        srcv = src.rearrange("(c p) l -> p c l", p=128)
        dstv = dst.rearrange("(c p) l -> p c l", p=128)
        mhg = C.modhg[l]

        def load(t):
            C.dma(xt[t % 3][:], srcv[:, :, t * TT:(t + 1) * TT], [("d", src.name, t)], [("xt", t % 3)])

        def prep(t):
            x, kx = xt[t % 3], ("xt", t % 3)
            modulate_tile(C, hTb[t % 2], x, kx, l, mi, TT, hkey=("hT", t % 2))
            C.ts("pool", x[:], x[:], float(ALPHA), ALU.mult, [kx], [kx])

        def tail_pieces(t):
            x, kx = xt[t % 3], ("xt", t % 3)
            ps = layer_norm_pieces(C, x, kx, TT, zsq, mean, msq, rstd, py[0], ("py", 0), py[1], ("py", 1), l, lni)
            ps.append(lambda: C.dma(dstv[:, :, t * TT:(t + 1) * TT], x[:], [kx], [("d", dst.name, t)], q="pool"))
            return ps

        load(0)
        prep(0)
        pend = []
        for t in range(NT):
            x, kx = xt[t % 3], ("xt", t % 3)
            hT, kh = hTb[t % 2], ("hT", t % 2)
            if t + 1 < NT:
                load(t + 1)
            for f in range(NFC):
                fb = (t * NFC + f) % 4
                for k in range(8):
                    C.mm(pgu[fb][:, 0:TT], wi[:, k, f * 128:(f + 1) * 128], hT[:, k, :], k == 0, k == 7,
                         ["wi", kh], [("pgu", fb)])
                for k in range(8):
                    C.mm(pgu[fb][:, TT:2 * TT], wi[:, k, D_FF + f * 128:D_FF + (f + 1) * 128], hT[:, k, :], k == 0, k == 7,
                         ["wi", kh], [("pgu", fb)])
                C.act(sg[fb][:], pgu[fb][:, 0:TT], AF.Silu, [("pgu", fb)], [("sg", fb)])
                C.tt("dve", actT[:, f, :], sg[fb][:], pgu[fb][:, TT:2 * TT], ALU.mult, [("sg", fb), ("pgu", fb)], ["actT"])
                if f >= 1 and pend:
                    pend.pop(0)()
                if f == 16 and t + 1 < NT:
                    prep(t + 1)
            while pend:
                pend.pop(0)()
            for d in range(8):
                pyd = py[d // 2][:, (d % 2) * TT:(d % 2 + 1) * TT]
                for f in range(NFC):
                    C.mm(pyd, wo[:, f, d * 128:(d + 1) * 128], actT[:, f, :], f == 0, f == NFC - 1,
                         ["wo", "actT"], [("py", d // 2)])
                C.stt(x[:, d, :], pyd, mhg[:, 8 * (mi + 2) + d:8 * (mi + 2) + d + 1], x[:, d, :], ALU.mult, ALU.add,
                      [("py", d // 2), kx, ("modhg", l)], [kx])
            pend = tail_pieces(t)
        while pend:
            pend.pop(0)()
    C.S.barrier()


def layer_norm_pieces(C, x, kx, TT, zsq, mean, msq, rstd, ps1, k1, ps2, k2, l, lni):
    o = (l * 3 + lni) * 8
    P = []
    P.append(lambda: C.act(zsq[:], x[:], AF.Square, [kx], ["zsq"]))

    def stats():
        for d in range(8):
            C.mm(ps1[:, 0:TT], C.ones[:], x[:, d, :], d == 0, d == 7, ["ones", kx], [k1])
        for d in range(8):
            C.mm(ps2[:, 0:TT], C.ones[:], zsq[:, d, :], d == 0, d == 7, ["ones", "zsq"], [k2])
        C.ts("dve", mean[:], ps1[:, 0:TT], 1.0 / D, ALU.mult, [k1], ["mean"])
        C.tt("dve", msq[:], mean[:], mean[:], ALU.mult, ["mean"], ["msq"])
        C.stt(msq[:], ps2[:, 0:TT], 1.0 / D, msq[:], ALU.mult, ALU.subtract, [k2, "msq"], ["msq"])
        C.ts("dve", msq[:], msq[:], float(LN_EPS), ALU.add, ["msq"], ["msq"])
    P.append(stats)

    def rs():
        C.act(msq[:], msq[:], AF.Sqrt, ["msq"], ["msq"])
        C.recip(rstd[:], msq[:], ["msq"], ["rstd"])
    P.append(rs)
    for d in range(8):
        def nrm(d=d):
            C.tt("dve", x[:, d, :], x[:, d, :], mean[:], ALU.subtract, [kx, "mean"], [kx])
            C.tt("pool", x[:, d, :], x[:, d, :], rstd[:], ALU.mult, [kx, "rstd"], [kx])
            C.act(x[:, d, :], x[:, d, :], AF.Identity, [kx, "lnw", "lnb"], [kx],
                  bias=C.lnb[:, o + d:o + d + 1], scale=C.lnw[:, o + d:o + d + 1])
        P.append(nrm)
    return P


def layer_norm_fm(C, x, kx, TT, zsq, mean, msq, rstd, ps1, k1, ps2, k2, l, lni):
    C.act(zsq[:], x[:], AF.Square, [kx], ["zsq"])
    for d in range(8):
        C.mm(ps1[:, 0:TT], C.ones[:], x[:, d, :], d == 0, d == 7, ["ones", kx], [k1])
    for d in range(8):
        C.mm(ps2[:, 0:TT], C.ones[:], zsq[:, d, :], d == 0, d == 7, ["ones", "zsq"], [k2])
    C.ts("dve", mean[:], ps1[:, 0:TT], 1.0 / D, ALU.mult, [k1], ["mean"])
    C.tt("dve", msq[:], mean[:], mean[:], ALU.mult, ["mean"], ["msq"])
    C.stt(msq[:], ps2[:, 0:TT], 1.0 / D, msq[:], ALU.mult, ALU.subtract, [k2, "msq"], ["msq"])
    C.ts("dve", msq[:], msq[:], float(LN_EPS), ALU.add, ["msq"], ["msq"])
    C.act(msq[:], msq[:], AF.Sqrt, ["msq"], ["msq"])
    C.recip(rstd[:], msq[:], ["msq"], ["rstd"])
    o = (l * 3 + lni) * 8
    for d in range(8):
        C.tt("dve", x[:, d, :], x[:, d, :], mean[:], ALU.subtract, [kx, "mean"], [kx])
        C.tt("pool", x[:, d, :], x[:, d, :], rstd[:], ALU.mult, [kx, "rstd"], [kx])
        C.act(x[:, d, :], x[:, d, :], AF.Identity, [kx, "lnw", "lnb"], [kx],
              bias=C.lnb[:, o + d:o + d + 1], scale=C.lnw[:, o + d:o + d + 1])


RW_DIM = 512
A_COLS = 1920
B_COLS = 1536
XBC = 1024
O_B = A_COLS
O_Z = A_COLS + B_COLS
O_X = O_Z + 512
O_DT = O_X + XBC
O_G = O_DT + 16
IN_COLS = O_G + 3 * D
NEG = -30000.0


def modulate_tile(C, hT, x, kx, l, mi, TT, hkey="hT"):
    for c in range(8):
        C.act(hT[:, c, :], x[:, c, :], AF.Identity, [kx, ("modT", l), ("mod1p", l)], [hkey],
              bias=C.modT[l][:, 8 * mi + c:8 * mi + c + 1],
              scale=C.mod1p[l][:, 8 * (mi + 1) + c:8 * (mi + 1) + c + 1])


def stage_inproj(C, l, src):
    L = C.L
    TT = 256
    NT = L // TT
    pa, pb, px, zt, dtt = C.d["projA"], C.d["projB"], C.d["projX"], C.d["z_tok"], C.d["dt_tok"]
    with contextlib.ExitStack() as st:
        w = C.sb(st, "win", [128, 8, O_G], BF16)
        with contextlib.ExitStack() as st2:
            load_w_bf16(C, st2, lambda k, a, b: w[:, k, a:b], C.d["mix_w_in"][l][:, 0:O_G], 8, O_G, "win", blk=1252)
            C.S.barrier()
        xt = [C.sb(st, f"xt{i}", [128, 8, TT], F32) for i in range(2)]
        hTb = [C.sb(st, f"hT{i}", [128, 8, TT], BF16) for i in range(2)]
        pj = [C.sb(st, f"pj{i}", [128, 35, TT], BF16) for i in range(2)]
        zs = [C.sb(st, f"zs{i}", [128, 2, 512], BF16) for i in range(2)]
        ds = [C.sb(st, f"ds{i}", [128, 2, 16], F32) for i in range(2)]
        pp = [C.ps(st, f"pp{i}", [128, 512], F32) for i in range(4)]
        pz = [C.ps(st, f"pz{i}", [128, 512], F32) for i in range(2)]
        pd = C.ps(st, "pd", [128, 512], F32)
        srcv = src.rearrange("(c p) l -> p c l", p=128)
        cols = [j * 128 for j in range(15)] + [O_B + j * 128 for j in range(12)] + [O_X + j * 128 for j in range(8)]
        C.dma(xt[0][:], srcv[:, :, 0:TT], [("d", src.name, 0)], [("xt", 0)])
        modulate_tile(C, hTb[0], xt[0], ("xt", 0), l, 3, TT, hkey=("hT", 0))
        for t in range(NT):
            b = t % 2
            hT = hTb[b]
            if t + 1 < NT:
                C.dma(xt[1 - b][:], srcv[:, :, (t + 1) * TT:(t + 2) * TT], [("d", src.name, t + 1)], [("xt", 1 - b)])
            for jj in range(18):
                if jj == 10 and t + 1 < NT:
                    modulate_tile(C, hTb[1 - b], xt[1 - b], ("xt", 1 - b), l, 3, TT, hkey=("hT", 1 - b))
                bank = jj % 4
                for half in range(2):
                    j = jj * 2 + half
                    if j >= 35:
                        continue
                    for k in range(8):
                        C.mm(pp[bank][:, half * TT:(half + 1) * TT], w[:, k, cols[j]:cols[j] + 128], hT[:, k, :],
                             k == 0, k == 7, ["win", ("hT", b)], [("pp", bank)])
                n = 2 if jj * 2 + 1 < 35 else 1
                o = pj[b][:, jj * 2:jj * 2 + n, :]
                i_ = pp[bank][:, 0:n * TT].rearrange("p (c l) -> p c l", c=n)
                if jj % 2 == 0:
                    C.act(o, i_, AF.Copy, [("pp", bank)], [("pj", b)])
                else:
                    C.cp("dve", o, i_, [("pp", bank)], [("pj", b)])
            for blk in range(2):
                for k in range(8):
                    C.mm(pz[blk][:], hT[:, k, blk * 128:(blk + 1) * 128], w[:, k, O_Z:O_Z + 512], k == 0, k == 7,
                         ["win", ("hT", b)], [("pz", blk)])
                C.cp("dve", zs[b][:, blk, :], pz[blk][:], [("pz", blk)], [("zs", b)])
                for k in range(8):
                    C.mm(pd[:, blk * 16:(blk + 1) * 16], hT[:, k, blk * 128:(blk + 1) * 128], w[:, k, O_DT:O_DT + 16],
                         k == 0, k == 7, ["win", ("hT", b)], ["pd"])
            C.act(ds[b][:], pd[:, 0:32].rearrange("p (c h) -> p c h", c=2), AF.Copy, ["pd"], [("ds", b)])
            sl = slice(t * TT, (t + 1) * TT)
            C.dma(pa.rearrange("(c p) l -> p c l", p=128)[:, :, sl], pj[b][:, 0:15, :], [("pj", b)], [("d", "projA", t)], q="sp")
            C.dma(pb.rearrange("(c p) l -> p c l", p=128)[:, :, sl], pj[b][:, 15:27, :], [("pj", b)], [("d", "projB", t)], q="sp")
            C.dma(px.rearrange("(c p) l -> p c l", p=128)[:, :, sl], pj[b][:, 27:35, :], [("pj", b)], [("d", "projX", t)], q="sp")
            C.dma(zt.rearrange("(c p) n -> p c n", p=128)[:, 2 * t:2 * t + 2, :], zs[b][:], [("zs", b)], [("d", "z_tok", t)], q="pool")
            C.dma(dtt.rearrange("(c p) n -> p c n", p=128)[:, 2 * t:2 * t + 2, :], ds[b][:], [("ds", b)], [("d", "dt_tok", t)], q="pool")
    C.S.barrier()

def stage_ssd(C, l):
    L = C.L
    NQ = L // 128
    px, zt, dtt = C.d["projX"], C.d["z_tok"], C.d["dt_tok"]
    bcT, xbt, ycT = C.d["ssd_bcT"], C.d["ssd_xbtok"], C.d["ycT"]
    with contextlib.ExitStack() as st:
        cw = C.sb(st, "cw", [128, 40], F32)
        cb = C.sb(st, "cb", [128, 8], F32)
        dg = C.sb(st, "dg", [128, 8, 5, 128], BF16)
        C.dma(cw[:], C.d["ssd_cw"][l], [], ["cw"])
        C.dma(cb[:], C.d["ssd_cb"][l], [], ["cb"])
        for c in range(8):
            for k in range(5):
                C.ts("pool", dg[:, c, k, :], C.ident[:], cw[:, c * 5 + k:c * 5 + k + 1], ALU.mult, ["cw", "ident"], ["dg"])
        TA = 512
        xin = [C.sb(st, f"xin{i}", [128, 8, TA + 4], BF16) for i in range(2)]
        xc = [C.sb(st, f"xc{i}", [128, 8, TA], BF16) for i in range(2)]
        tokb = [C.sb(st, f"tokb{i}", [128, 4, 768], BF16) for i in range(2)]
        pc = [C.ps(st, f"pc{i}", [128, 512], F32) for i in range(4)]
        pT = [C.ps(st, f"pT{i}", [128, 1024], BF16) for i in range(2)]
        pxv = px.rearrange("(c p) l -> p c l", p=128)
        NTA = L // TA
        for t in range(NTA):
            b = t % 2
            kx = ("xin", b)
            lo = t * TA - 2
            hi = (t + 1) * TA + 2
            c0, c1 = 0, TA + 4
            if t == 0:
                C.memset("pool", xin[b][:, :, 0:2], 0.0, [kx])
                lo, c0 = 0, 2
            if t == NTA - 1:
                C.memset("pool", xin[b][:, :, TA + 2:TA + 4], 0.0, [kx])
                hi, c1 = L, TA + 2
            C.dma(xin[b][:, :, c0:c1], pxv[:, :, lo:hi], [("d", "projX", i) for i in range(max(0, 2 * t - 1), min(L // 256, 2 * t + 3))], [kx])
            for c in range(8):
                bank = c % 4
                for k in range(5):
                    C.mm(pc[bank][:], dg[:, c, k, :], xin[b][:, c, k:k + TA], k == 0, k == 4, ["dg", kx], [("pc", bank)])
                C.act(xc[b][:, c, :], pc[bank][:], AF.Silu, [("pc", bank), "cb"], [("xc", b)], bias=cb[:, c:c + 1])
            C.dma(bcT.rearrange("(c p) l -> p c l", p=128)[:, :, t * TA:(t + 1) * TA], xc[b][:, 4:8, :], [("xc", b)],
                  [("d", "bcT", t)], q="pool")
            for blk in range(4):
                pb_ = (t * 4 + blk) % 2
                for c in range(6):
                    C.tr(pT[pb_][:, c * 128:(c + 1) * 128], xc[b][:, c, blk * 128:(blk + 1) * 128], C.identb[:],
                         [("xc", b), "identb"], [("pT", pb_)])
                if blk % 2 == 0:
                    C.cp("dve", tokb[b][:, blk, :], pT[pb_][:, 0:768], [("pT", pb_)], [("tokb", b)])
                else:
                    C.act(tokb[b][:, blk, :], pT[pb_][:, 0:768], AF.Copy, [("pT", pb_)], [("tokb", b)])
            C.dma(xbt.rearrange("(c p) n -> p c n", p=128)[:, 4 * t:4 * t + 4, :], tokb[b][:], [("tokb", b)],
                  [("d", "xbtok", t)], q="pool")
    C.S.barrier()
    with contextlib.ExitStack() as st:
        masks = C.sb(st, "masks", [128, 6, 128], F32)
        C.dma(masks[:], C.d["c_masks"][:, :, :], [], ["masks"])
        dtb = C.sb(st, "dtb", [128, 16], F32)
        alog = C.sb(st, "alog", [128, 16], F32)
        Dbc = C.sb(st, "Dbc", [128, 512], F32)
        nw = C.sb(st, "nw", [128, 512], F32)
        C.dma(dtb[:], C.d["ssd_dtb"][l], [], ["dtb"])
        C.dma(alog[:], C.d["ssd_alog"][l], [], ["alog"])
        C.dma(Dbc[:], C.d["ssd_Dbc"][l], [], ["Dbc"])
        C.dma(nw[:], C.d["ssd_nw"][l], [], ["nw"])
        dtv = C.sb(st, "dtv", [128, NQ, 16], F32)
        av = C.sb(st, "av", [128, NQ, 16], F32)
        Ev = C.sb(st, "Ev", [128, 5, NQ * 16], F32)
        Wf = C.sb(st, "Wf", [128, NQ, 8], F32)
        Wb = C.sb(st, "Wb", [128, NQ, 8], F32)
        stF = C.sb(st, "stF", [128, NQ, 512], BF16)
        stB = C.sb(st, "stB", [128, NQ, 512], BF16)
        state = C.sb(st, "state", [128, 512], F32)
        pD = [C.ps(st, f"pD{i}", [128, 512], F32) for i in range(2)]
        pe5 = pD
        C.dma(dtv[:], dtt.rearrange("(q p) h -> p q h", p=128), [("d", "dt_tok", i) for i in range(L // 256)], ["dtv"])
        C.tt("dve", dtv[:], dtv[:], dtb[:].unsqueeze(1).broadcast_to([128, NQ, 16]), ALU.add, ["dtv", "dtb"], ["dtv"])
        C.act(dtv[:], dtv[:], AF.Exp, ["dtv"], ["dtv"])
        C.act(dtv[:], dtv[:], AF.Ln, ["dtv"], ["dtv"], bias=1.0)
        C.act(alog[:], alog[:], AF.Exp, ["alog"], ["alog"])
        C.ts("dve", alog[:], alog[:], -1.0, ALU.mult, ["alog"], ["alog"])
        C.tt("dve", av[:], dtv[:], alog[:].unsqueeze(1).broadcast_to([128, NQ, 16]), ALU.mult, ["dtv", "alog"], ["av"])
        lhs5 = [masks[:, 1, :], masks[:, 3, :], masks[:, 2, :], masks[:, 0, :], C.ones[:]]
        for g8 in range(NQ // 8):
            rhs = av[:, g8 * 8:(g8 + 1) * 8, :]
            for kind in range(5):
                bank = kind // 4
                C.mm(pe5[bank][:, (kind % 4) * 128:(kind % 4 + 1) * 128], lhs5[kind], rhs, True, True,
                     ["masks", "ones", "av"], [("pD", bank)])
            C.act(Ev[:, 0:4, g8 * 128:(g8 + 1) * 128], pe5[0][:].rearrange("p (k n) -> p k n", k=4), AF.Exp,
                  [("pD", 0)], ["Ev"])
            C.act(Ev[:, 4, g8 * 128:(g8 + 1) * 128], pe5[1][:, 0:128], AF.Exp, [("pD", 1)], ["Ev"])
        Ev4 = lambda kind: Ev[:, kind, :].rearrange("p (q h) -> p q h", h=16)
        C.tt("dve", Wf[:], dtv[:, :, 0:8], Ev4(3)[:, :, 0:8], ALU.mult, ["dtv", "Ev"], ["Wf"])
        C.tt("dve", Wb[:], dtv[:, :, 8:16], Ev4(2)[:, :, 8:16], ALU.mult, ["dtv", "Ev"], ["Wb"])
        tok = [C.sb(st, f"tok{i}", [128, 768], BF16) for i in range(3)]
        xdd = [C.sb(st, f"xdd{i}", [128, 512], BF16) for i in range(2)]
        stmp = C.sb(st, "stmp", [128, 512], F32)
        pst = [C.ps(st, f"pst{i}", [128, 512], F32) for i in range(2)]
        xbv = xbt.rearrange("(q p) n -> p q n", p=128)
        ld = 0
        ya = C.sb(st, "ya", [128, 512], F32)
        yb = C.sb(st, "yb", [128, 512], F32)
        state2 = [state, ya]
        stmp2 = [stmp, yb]
        orders = [list(range(NQ)), list(range(NQ - 1, -1, -1))]
        for dr in range(2):
            stX = stF if dr == 0 else stB
            C.memset("pool", stX[:, orders[dr][0], :], 0.0, [("stX", dr)])
            C.memset("pool", state2[dr][:], 0.0, [("state", dr)])
        for i in range(NQ - 1):
            for dr in range(2):
                stX, W = (stF, Wf) if dr == 0 else (stB, Wb)
                q = orders[dr][i]
                tb = ld % 3
                ld += 1
                C.dma(tok[tb][:], xbv[:, q, :], [], [("tok", tb)])
                C.tt("dve", xdd[dr][:].rearrange("p (h e) -> p h e", h=8), tok[tb][:, 0:512].rearrange("p (h e) -> p h e", h=8),
                     W[:, q, :].unsqueeze(2).broadcast_to([128, 8, 64]), ALU.mult, [("tok", tb), "Wf", "Wb"], [("xdd", dr)])
                for g in range(2):
                    C.mm(pst[dr][:, g * 256:(g + 1) * 256], tok[tb][:, 512 + g * 128:512 + (g + 1) * 128],
                         xdd[dr][:, g * 256:(g + 1) * 256], True, True, [("tok", tb), ("xdd", dr)], [("pst", dr)])
                C.tt("pool", stmp2[dr][:].rearrange("p (h e) -> p h e", h=8), state2[dr][:].rearrange("p (h e) -> p h e", h=8),
                     Ev4(4)[:, q, dr * 8:dr * 8 + 8].unsqueeze(2).broadcast_to([128, 8, 64]), ALU.mult, [("state", dr), "Ev"], [("stmp", dr)])
                C.tt("dve", state2[dr][:], stmp2[dr][:], pst[dr][:], ALU.add, [("stmp", dr), ("pst", dr)], [("state", dr)])
                C.act(stX[:, orders[dr][i + 1], :], state2[dr][:], AF.Copy, [("state", dr)], [("stX", dr)])
        C.S.barrier()
        bc = [C.sb(st, f"bc{i}", [128, 4, 128], BF16) for i in range(2)]
        zb = [C.sb(st, f"zb{i}", [128, 512], BF16) for i in range(2)]
        Ah = [C.sb(st, f"Ah{i}", [128, 128], F32) for i in range(4)]
        Lt = [C.sb(st, f"Lt{i}", [128, 4, 128], F32) for i in range(2)]
        Mt = [C.sb(st, f"Mt{i}", [128, 4, 128], BF16) for i in range(2)]
        xd = [C.sb(st, f"xd{i}", [128, 512], BF16) for i in range(2)]
        sz = C.sb(st, "sz", [128, 512], F32)
        ss = C.sb(st, "ss", [128, 2], F32)
        rs = C.sb(st, "rs", [128, 2], F32)
        y3 = C.sb(st, "y3", [128, 512], BF16)
        yT = [C.sb(st, f"yT{i}", [128, 4, 128], BF16) for i in range(2)]
        pcb = C.ps(st, "pcb", [128, 512], F32)
        pY = C.ps(st, "pY", [128, 512], F32)
        pO = pst
        pT = C.ps(st, "pT", [128, 1024], BF16)
        bcv = bcT.rearrange("(c p) l -> p c l", p=128)
        ztv = zt.rearrange("(q p) n -> p q n", p=128)
        ycv = ycT.rearrange("(c p) l -> p c l", p=128)
        nD = 0
        nA = 0
        for q in range(NQ):
            b = q % 2
            tb = ld % 3
            ld += 1
            C.dma(tok[tb][:], xbv[:, q, :], [("d", "xbtok", q // 4)], [("tok", tb)])
            C.dma(bc[b][:], bcv[:, :, q * 128:(q + 1) * 128], [("d", "bcT", q // 4)], [("bc", b)])
            C.dma(zb[b][:], ztv[:, q, :], [("d", "z_tok", q // 2)], [("zb", b)], q="act")
            for g in range(2):
                C.mm(pcb[:, g * 128:(g + 1) * 128], bc[b][:, g, :], bc[b][:, 2 + g, :], g == 0, True,
                     [("bc", b)], ["pcb"], skip_group_check=True)
            first_y = True
            for dr in range(2):
                xb_ = dr
                C.tt("pool", xd[xb_][:].rearrange("p (h e) -> p h e", h=8), tok[tb][:, 0:512].rearrange("p (h e) -> p h e", h=8),
                     dtv[:, q, dr * 8:dr * 8 + 8].unsqueeze(2).broadcast_to([128, 8, 64]), ALU.mult, [("tok", tb), "dtv"], [("xd", xb_)])
                mA = masks[:, 0, :] if dr == 0 else masks[:, 2, :]
                mR = masks[:, 1, :] if dr == 0 else masks[:, 3, :]
                mN = masks[:, 4, :] if dr == 0 else masks[:, 5, :]
                for g in range(2):
                    db = nD % 2
                    nD += 1
                    for e in range(4):
                        h = g * 4 + e
                        ab = nA % 4
                        nA += 1
                        C.ts("dve", Ah[ab][:], mA, av[:, q, dr * 8 + h:dr * 8 + h + 1], ALU.mult,
                             ["masks", "av"], [("Ah", ab)])
                        C.mm(pD[db][:, e * 128:(e + 1) * 128], Ah[ab][:], mR, e == 0, False, [("Ah", ab), "masks"], [("pD", db)],
                             skip_group_check=True)
                        C.mm(pD[db][:, e * 128:(e + 1) * 128], C.ident[:], mN, False, True, ["ident", "masks"], [("pD", db)],
                             skip_group_check=True)
                    C.act(Lt[db][:], pD[db][:].rearrange("p (e l) -> p e l", e=4), AF.Exp, [("pD", db)], [("Lt", db)])
                    C.tt("dve", Mt[db][:], Lt[db][:], pcb[:, g * 128:(g + 1) * 128].unsqueeze(1).broadcast_to([128, 4, 128]), ALU.mult,
                         [("Lt", db), "pcb"], [("Mt", db)])
                    for e in range(4):
                        h = g * 4 + e
                        C.mm(pY[:, h * 64:(h + 1) * 64], Mt[db][:, e, :], xd[xb_][:, h * 64:(h + 1) * 64], first_y, True,
                             [("Mt", db), ("xd", xb_)], ["pY"], skip_group_check=True)
                        first_y = False
                stX = stF if dr == 0 else stB
                for g in range(2):
                    C.mm(pO[dr][:, g * 256:(g + 1) * 256], bc[b][:, 2 + g, :], stX[:, q, g * 256:(g + 1) * 256], g == 0, True,
                         [("bc", b), ("stX", dr)], [("pst", dr)], skip_group_check=True)
            v3 = lambda ap: ap.rearrange("p (h e) -> p h e", h=8)
            C.tt("pool", ya[:], tok[tb][:, 0:512], Dbc[:], ALU.mult, [("tok", tb), "Dbc"], ["ya"])
            C.tt("dve", v3(yb[:]), v3(pO[0][:]), Ev4(0)[:, q, 0:8].unsqueeze(2).broadcast_to([128, 8, 64]), ALU.mult,
                 [("pst", 0), "Ev"], ["yb"])
            C.tt("pool", ya[:], ya[:], yb[:], ALU.add, ["ya", "yb"], ["ya"])
            C.tt("dve", v3(yb[:]), v3(pO[1][:]), Ev4(1)[:, q, 8:16].unsqueeze(2).broadcast_to([128, 8, 64]), ALU.mult,
                 [("pst", 1), "Ev"], ["yb"])
            C.tt("pool", ya[:], ya[:], yb[:], ALU.add, ["ya", "yb"], ["ya"])
            C.tt("dve", ya[:], ya[:], pY[:], ALU.add, ["ya", "pY"], ["ya"])
            C.act(sz[:], zb[b][:], AF.Silu, [("zb", b)], ["sz"])
            C.tt("pool", ya[:], ya[:], sz[:], ALU.mult, ["ya", "sz"], ["ya"])
            for g in range(2):
                C.S.add("act", (lambda g=g: (lambda e: e.activation(sz[:, g * 256:(g + 1) * 256], ya[:, g * 256:(g + 1) * 256],
                                                                  AF.Square, accum_out=ss[:, g:g + 1])))(), ["ya"], ["sz", "ss"])
            C.ts("dve", ss[:], ss[:], 1.0 / 256, ALU.mult, ["ss"], ["ss"], s2=1e-5, op1=ALU.add)
            C.act(ss[:], ss[:], AF.Sqrt, ["ss"], ["ss"])
            C.recip(rs[:], ss[:], ["ss"], ["rs"])
            for g in range(2):
                C.stt(y3[:, g * 256:(g + 1) * 256], ya[:, g * 256:(g + 1) * 256], rs[:, g:g + 1], nw[:, g * 256:(g + 1) * 256],
                      ALU.mult, ALU.mult, ["ya", "rs", "nw"], ["y3"])
            for c in range(4):
                C.tr(pT[:, c * 128:(c + 1) * 128], y3[:, c * 128:(c + 1) * 128], C.identb[:], ["y3", "identb"], ["pT"])
            C.act(yT[b][:], pT[:, 0:512].rearrange("p (c l) -> p c l", c=4), AF.Copy, ["pT"], [("yT", b)])
            C.dma(ycv[:, :, q * 128:(q + 1) * 128], yT[b][:], [("yT", b)], [("d", "ycT", q)], q="pool")
    C.S.barrier()

PI = math.pi


def dw_conv_diag(C, st, name, wdram, nch, K):
    cw = C.sb(st, name + "cw", [128, nch * K], F32)
    dg = C.sb(st, name + "dg", [128, nch, K, 128], BF16)
    C.dma(cw[:], wdram, [], [name + "cw"])
    for c in range(nch):
        for k in range(K):
            C.ts("pool", dg[:, c, k, :], C.ident[:], cw[:, c * K + k:c * K + k + 1], ALU.mult, [name + "cw", "ident"], [name + "dg"])
    return dg


def load_halo(C, buf, kx, view, t, TA, halo, L, dkeys):
    NTA = L // TA
    lo, hi = t * TA - halo, (t + 1) * TA + halo
    c0, c1 = 0, TA + 2 * halo
    if t == 0:
        C.memset("pool", buf[:, :, 0:halo], 0.0, [kx])
        lo, c0 = 0, halo
    if t == NTA - 1:
        C.memset("pool", buf[:, :, TA + halo:TA + 2 * halo], 0.0, [kx])
        hi, c1 = L, TA + halo
    C.dma(buf[:, :, c0:c1], view[:, :, lo:hi], dkeys, [kx])


def hy_filter(C, l):
    L = C.L
    TT = 512
    NT = L // TT
    with contextlib.ExitStack() as st:
        w1 = C.sb(st, "w1", [33, 64], F32)
        w2 = C.sb(st, "w2", [64, 64], F32)
        w3 = C.sb(st, "w3", [64, 512], F32)
        bb = C.sb(st, "bb", [64, 2], F32)
        fq = C.sb(st, "fq", [64, 2], F32)
        nd = C.sb(st, "nd", [128, 4], F32)
        hbi = C.sb(st, "hbi", [128, 4], F32)
        C.dma(w1[:], C.d["hy_f_w1"][l], [], ["w1"])
        C.dma(w2[:], C.d["hy_f_w2"][l], [], ["w2"])
        C.dma(w3[:], C.d["hy_f_w3"][l], [], ["w3"])
        C.dma(bb[:], C.d["hy_fb"][l], [], ["bb"])
        C.dma(fq[:], C.d["hy_freq"][l], [], ["fq"])
        C.dma(nd[:], C.d["c_ndelta"][:, :], [], ["nd"])
        C.dma(hbi[:], C.d["hy_bias"][l], [], ["hbi"])
        C.tt("dve", bb[:], bb[:], fq[:], ALU.mult, ["bb", "fq"], ["bb"])
        hbuf = C.sb(st, "hbuf", [128, 4, L], F32)
        ft = [C.sb(st, f"ft{i}", [33, TT], F32) for i in range(2)]
        dt_ = [C.sb(st, f"dtl{i}", [128, TT], F32) for i in range(2)]
        u = C.sb(st, "u", [64, TT], F32)
        m = C.sb(st, "m", [64, TT], F32)
        h1 = C.sb(st, "h1", [64, TT], F32)
        wd = [C.sb(st, f"wd{i}", [128, TT], F32) for i in range(2)]
        junk = C.sb(st, "junk", [128, TT], F32)
        sp = C.sb(st, "sp", [128, 4, NT], F32)
        ssum = C.sb(st, "ssum", [128, 4], F32)
        rs = C.sb(st, "rs", [128, 4], F32)
        p12 = C.ps(st, "p12", [128, 512], F32)
        p3 = [C.ps(st, f"p3{i}", [128, 512], F32) for i in range(2)]

        def sin_layer(src_ps, j):
            C.act(u[:], src_ps, AF.Identity, ["p12", "fq", "bb"], ["u"], bias=bb[:, j:j + 1], scale=fq[:, j:j + 1])
            C.ts("dve", m[:], u[:], PI, ALU.is_gt, ["u"], ["m"], s2=-2 * PI, op1=ALU.mult)
            C.tt("dve", u[:], u[:], m[:], ALU.add, ["u", "m"], ["u"])
            C.ts("dve", m[:], u[:], -PI, ALU.is_lt, ["u"], ["m"], s2=2 * PI, op1=ALU.mult)
            C.tt("dve", u[:], u[:], m[:], ALU.add, ["u", "m"], ["u"])
            C.ts("dve", u[:], u[:], 3.14159, ALU.min, ["u"], ["u"], s2=-3.14159, op1=ALU.max)
            C.act(h1[:], u[:], AF.Sin, ["u"], ["h1"])

        for t in range(NT):
            b = t % 2
            sl = slice(t * TT, (t + 1) * TT)
            C.dma(ft[b][:], C.d["c_featsT"][:, sl], [], [("ft", b)])
            C.dma(dt_[b][:], C.d["c_distb"][:, sl], [], [("dtl", b)])
            C.mm(p12[0:64, :], w1[:], ft[b][:], True, True, ["w1", ("ft", b)], ["p12"])
            sin_layer(p12[0:64, :], 0)
            C.mm(p12[0:64, :], w2[:], h1[:], True, True, ["w2", "h1"], ["p12"])
            sin_layer(p12[0:64, :], 1)
            for c in range(4):
                pb_ = c % 2
                C.mm(p3[pb_][:], w3[:, c * 128:(c + 1) * 128], h1[:], True, True, ["w3", "h1"], [("p3", pb_)])
                C.act(wd[pb_][:], dt_[b][:], AF.Exp, [("dtl", b), "nd"], [("wd", pb_)], scale=nd[:, c:c + 1])
                C.tt("dve", hbuf[:, c, sl], p3[pb_][:], wd[pb_][:], ALU.mult, [("p3", pb_), ("wd", pb_)], ["hbuf"])
                C.S.add("act", (lambda c=c, t=t, sl=sl: (lambda e: e.activation(junk[:], hbuf[:, c, sl], AF.Square,
                                                                           accum_out=sp[:, c, t:t + 1])))(), ["hbuf"], ["junk", "sp"])
        C.S.add("dve", lambda e: e.tensor_reduce(ssum[:], sp[:], AX.X, ALU.add), ["sp"], ["ssum"])
        C.ts("dve", ssum[:], ssum[:], 1e-12, ALU.add, ["ssum"], ["ssum"])
        C.act(ssum[:], ssum[:], AF.Sqrt, ["ssum"], ["ssum"])
        C.recip(rs[:], ssum[:], ["ssum"], ["rs"])
        hb = [C.sb(st, f"hb{i}", [128, 2048], BF16) for i in range(2)]
        hv = C.d["hy_hT"].rearrange("(c p) l -> p c l", p=128)
        i = 0
        for c in range(4):
            C.ts("dve", hbuf[:, c, L // 2:L // 2 + 1], hbuf[:, c, L // 2:L // 2 + 1], rs[:, c:c + 1], ALU.mult, ["hbuf", "rs", "hbi"], ["hbuf"],
                 s2=hbi[:, c:c + 1], op1=ALU.add)
            C.recip(junk[:, 0:1], rs[:, c:c + 1], ["rs"], ["junk"])
            C.ts("dve", hbuf[:, c, L // 2:L // 2 + 1], hbuf[:, c, L // 2:L // 2 + 1], junk[:, 0:1], ALU.mult, ["hbuf", "junk"], ["hbuf"])
            for s in range(L // 2048):
                b = i % 2
                i += 1
                C.act(hb[b][:], hbuf[:, c, s * 2048:(s + 1) * 2048], AF.Identity, ["hbuf", "rs"], [("hb", b)], scale=rs[:, c:c + 1])
                C.dma(hv[:, c, s * 2048:(s + 1) * 2048], hb[b][:], [("hb", b)], [("d", "hy_hT", i)], q="pool")
    C.S.barrier()


def hy_front(C, l):
    L = C.L
    TA = 512
    NTA = L // TA
    with contextlib.ExitStack() as st:
        dg = dw_conv_diag(C, st, "hy", C.d["hy_cw"][l], 12, 3)
        cb = C.sb(st, "hycb", [128, 12], F32)
        C.dma(cb[:], C.d["hy_cb"][l], [], ["hycb"])
        xin = [C.sb(st, f"hxin{i}", [128, 12, TA + 2], BF16) for i in range(2)]
        x1 = C.sb(st, "hx1", [128, 4, TA], F32)
        ob = [C.sb(st, f"hob{i}", [128, 8, TA], BF16) for i in range(2)]
        pc = [C.ps(st, f"hpc{i}", [128, 512], F32) for i in range(4)]
        pv = C.d["projB"].rearrange("(c p) l -> p c l", p=128)
        x0v = C.d["hy_x0T"].rearrange("(c p) l -> p c l", p=128)
        zv = C.d["hy_zT"].rearrange("(c p) l -> p c l", p=128)
        for t in range(NTA):
            b = t % 2
            kx = ("hxin", b)
            load_halo(C, xin[b], kx, pv, t, TA, 1, L, [("d", "projB", i) for i in range(max(0, 2 * t - 1), min(L // 256, 2 * t + 3))])
            for c in range(12):
                bank = c % 4
                for k in range(3):
                    C.mm(pc[bank][:], dg[:, c, k, :], xin[b][:, c, k:k + TA], k == 0, k == 2, ["hydg", kx], [("hpc", bank)])
                if c < 4:
                    C.act(ob[b][:, c, :], pc[bank][:], AF.Identity, [("hpc", bank), "hycb"], [("hob", b)], bias=cb[:, c:c + 1])
                elif c < 8:
                    C.act(x1[:, c - 4, :], pc[bank][:], AF.Identity, [("hpc", bank), "hycb"], ["hx1"], bias=cb[:, c:c + 1])
                else:
                    C.stt(ob[b][:, c - 4, :], pc[bank][:], cb[:, c:c + 1], x1[:, c - 8, :], ALU.add, ALU.mult,
                          [("hpc", bank), "hycb", "hx1"], [("hob", b)])
            sl = slice(t * TA, (t + 1) * TA)
            C.dma(x0v[:, :, sl], ob[b][:, 0:4, :], [("hob", b)], [("d", "hy_x0T", t)], q="pool")
            C.dma(zv[:, :, sl], ob[b][:, 4:8, :], [("hob", b)], [("d", "hy_zT", t)], q="pool")
    C.S.barrier()


def hy_fft(C, l, filt):
    L = C.L
    assert L == 8192
    NP = 256
    with contextlib.ExitStack() as st:
        F256 = C.sb(st, "F256", [128, 512], BF16)
        Tw = C.sb(st, "Tw", [128, 768], F32)
        L4 = C.sb(st, "L4", [128, 4, 64], BF16)
        G3 = C.sb(st, "G3", [128, 3, 128], BF16)
        TAB = C.sb(st, "TAB", [128, 2, 2, 128], F32)
        FIN = C.sb(st, "FIN", [128, 2, 2, 128], BF16)
        for nm, t_ in (("F256", F256), ("Tw", Tw), ("L4", L4), ("G3", G3), ("TAB", TAB), ("FIN", FIN)):
            C.dma(t_[:], C.d["c_" + nm], [], [nm])
        zA = C.sb(st, "zA", [128, 512, 64], BF16)
        src = C.d["hy_hT"] if filt else C.d["hy_zT"]
        for g in range(16):
            C.dma(zA[:, g * 32:(g + 1) * 32, :], src[g * 32:(g + 1) * 32, :].rearrange("c (p j) -> p c j", j=64),
                  [], [("zA", g)], q="sp" if g % 2 == 0 else "act")
        if not filt:
            x0A = C.sb(st, "x0A", [128, 512, 64], BF16)
            for g in range(16):
                C.dma(x0A[:, g * 32:(g + 1) * 32, :], C.d["hy_x0T"][g * 32:(g + 1) * 32, :].rearrange("c (p j) -> p c j", j=64),
                      [], [("x0A", g)], q="sp" if g % 2 == 0 else "act")
        NB = 3
        t12 = [C.sb(st, f"t12{i}", [128, 2, 512], BF16) for i in range(NB)]
        pB = [C.ps(st, f"pB{i}", [128, 512], F32) for i in range(2)]
        pZ = [C.ps(st, f"pZ{i}", [128, 512], F32) for i in range(2)]
        if filt:
            Hfb = [C.sb(st, f"Hfb{i}", [128, 768], BF16) for i in range(2)]
        else:
            Hf = [C.sb(st, f"Hf{i}", [128, 768], BF16) for i in range(4)]
            u12 = [C.sb(st, f"u12{i}", [128, 2, 512], BF16) for i in range(NB)]
            w12 = [C.sb(st, f"w12{i}", [128, 2, 4, 128], BF16) for i in range(NB)]
            pQ = [C.ps(st, f"pQ{i}", [128, 512], F32) for i in range(2)]
            pY = [C.ps(st, f"pY{i}", [128, 512], F32) for i in range(2)]
        for i in range(NP):
            b = i % 2
            nb = i % NB
            if not filt:
                hb3 = i % 4
                C.dma(Hf[hb3][:], C.d["hy_Hf"][i], [], [("Hf", hb3)], q="sp")
            for hh in range(2):
                C.mm(pB[b][64 * hh:64 * hh + 64, :], zA[:, 2 * i + hh, :], F256[:], True, True, [("zA", (2 * i) // 32), "F256"], [("pB", b)])
            C.tt("dve", t12[nb][:, 0, :], pB[b][:], Tw[:, 0:512], ALU.mult, [("pB", b), "Tw"], [("t12", nb)])
            C.tt("dve", t12[nb][:, 1, :], pB[b][:], Tw[:, 256:768], ALU.mult, [("pB", b), "Tw"], [("t12", nb)])
            for hh in range(2):
                ps_ = slice(64 * hh, 64 * hh + 64)
                t1a, t1b = t12[nb][ps_, 0, 0:256], t12[nb][ps_, 0, 256:512]
                t2a, t2b = t12[nb][ps_, 1, 0:256], t12[nb][ps_, 1, 256:512]
                cc, nc_, ss, ns = L4[ps_, 0, :], L4[ps_, 1, :], L4[ps_, 2, :], L4[ps_, 3, :]
                zr, zi = pZ[b][ps_, 0:256], pZ[b][ps_, 256:512]
                seq = [(zr, cc, t1a), (zr, nc_, t1b), (zr, ss, t2a), (zr, ss, t2b),
                       (zi, cc, t2a), (zi, cc, t2b), (zi, ns, t1a), (zi, ss, t1b)]
                for j, (o, lt, rh) in enumerate(seq):
                    C.mm(o, lt, rh, j == 0, j == 7, ["L4", ("t12", nb)], [("pZ", b)], skip_group_check=True)
            if filt:
                C.act(Hfb[b][:, 0:512], pZ[b][:], AF.Copy, [("pZ", b)], [("Hfb", b)])
                C.cp("dve", Hfb[b][:, 512:768], pZ[b][:, 0:256], [("pZ", b)], [("Hfb", b)])
                C.dma(C.d["hy_Hf"][i], Hfb[b][:], [("Hfb", b)], [("d", "hy_Hf", i)], q="act")
                continue
            C.tt("dve", u12[nb][:, 0, :], pZ[b][:], Hf[hb3][:, 0:512], ALU.mult, [("pZ", b), ("Hf", hb3)], [("u12", nb)])
            C.tt("dve", u12[nb][:, 1, :], pZ[b][:], Hf[hb3][:, 256:768], ALU.mult, [("pZ", b), ("Hf", hb3)], [("u12", nb)])
            for hh in range(2):
                ps_ = slice(64 * hh, 64 * hh + 64)
                for kh in range(2):
                    o = pQ[b][:, (hh * 2 + kh) * 128:(hh * 2 + kh + 1) * 128]
                    ks = slice(kh * 128, (kh + 1) * 128)
                    ks2 = slice(256 + kh * 128, 256 + (kh + 1) * 128)
                    seq = [(u12[nb][ps_, 0, ks], G3[ps_, 0, :]), (u12[nb][ps_, 0, ks2], G3[ps_, 1, :]),
                           (u12[nb][ps_, 1, ks], G3[ps_, 2, :]), (u12[nb][ps_, 1, ks2], G3[ps_, 2, :])]
                    for j, (lt, rh) in enumerate(seq):
                        C.mm(o, lt, rh, j == 0, j == 3, [("u12", nb), "G3"], [("pQ", b)], skip_group_check=True)
            q4 = pQ[b][:].rearrange("p (g n) -> p g n", g=4)
            for ab in range(2):
                for hh in range(2):
                    C.tt("dve", w12[nb][:, ab, 2 * hh:2 * hh + 2, :], q4[:, 2 * hh:2 * hh + 2, :], TAB[:, ab, :, :], ALU.mult,
                         [("pQ", b), "TAB"], [("w12", nb)])
            yb_ = (i // 4) % 2
            for hh in range(2):
                slot = (2 * i + hh) % 8
                o = pY[yb_][:, slot * 64:(slot + 1) * 64]
                n = 0
                for kh in range(2):
                    for ab in range(2):
                        for half in range(2):
                            C.mm(o, FIN[:, kh, ab, :], w12[nb][:, ab, hh * 2 + kh, half * 64:(half + 1) * 64], n == 0, n == 7,
                                 ["FIN", ("w12", nb)], [("pY", yb_)], skip_group_check=True)
                            n += 1
            if i % 4 == 3:
                c0 = 2 * i - 6
                C.tt("dve", x0A[:, c0:c0 + 8, :], x0A[:, c0:c0 + 8, :], pY[yb_][:].rearrange("p (c n) -> p c n", c=8), ALU.mult,
                     [("x0A", c0 // 32), ("pY", yb_)], [("x0A", c0 // 32)])
                if (c0 + 8) % 32 == 0:
                    g = c0 // 32
                    C.dma(C.d["ybT"][g * 32:(g + 1) * 32, :].rearrange("c (p j) -> p c j", j=64), x0A[:, g * 32:(g + 1) * 32, :],
                          [("x0A", g)], [("d", "ybT", g)], q="sp")
    C.S.barrier()


def stage_hyena(C, l):
    hy_filter(C, l)
    hy_front(C, l)
    hy_fft(C, l, True)
    hy_fft(C, l, False)

RW_C = 0.6065306597126334
RW_GN_EPS = 64e-5


def rw_front(C, l):
    L = C.L
    TA = 512
    NTA = L // TA
    with contextlib.ExitStack() as st:
        mu = C.sb(st, "mu", [128, 2, 15], F32)
        wm = C.sb(st, "wm", [128, 15], F32)
        C.dma(mu[:], C.d["rw_mu"][l], [], ["mu"])
        C.tt("dve", wm[:], mu[:, 0, :], mu[:, 1, :], ALU.add, ["mu"], ["wm"])
        C.ts("dve", wm[:], wm[:], -1.0, ALU.mult, ["wm"], ["wm"], s2=1.0, op1=ALU.add)
        dg = C.sb(st, "rdg", [128, 15, 3, 128], BF16)
        for c in range(15):
            C.ts("pool", dg[:, c, 0, :], C.ident[:], mu[:, 0, c:c + 1], ALU.mult, ["mu", "ident"], ["rdg"])
            C.ts("pool", dg[:, c, 1, :], C.ident[:], wm[:, c:c + 1], ALU.mult, ["wm", "ident"], ["rdg"])
            C.ts("pool", dg[:, c, 2, :], C.ident[:], mu[:, 1, c:c + 1], ALU.mult, ["mu", "ident"], ["rdg"])
        lw = C.sb(st, "lw", [128, 3, 512], BF16)
        with contextlib.ExitStack() as st2:
            stg = C.sb(st2, "lstg", [128, 3, 512], F32)
            C.dma(stg[:, 0, :], C.d["rw_w2"][l], [], ["lstg"])
            C.dma(stg[:, 1, :], C.d["rw_a2"][l], [], ["lstg"])
            C.dma(stg[:, 2, :], C.d["rw_g2"][l], [], ["lstg"])
            C.cp("dve", lw[:], stg[:], ["lstg"], ["lw"])
            C.S.barrier()
        bcs = C.sb(st, "bcs", [128, 8, 512], F32)
        C.dma(bcs[:, 0:7, :], C.d["rw_bc"][l], [], ["bcs"])
        C.dma(bcs[:, 7, :], C.d["rw_bc"][l][:, 6, :], [], ["bcs"])
        C.ts("dve", bcs[:, 6, :], bcs[:, 5, :], -1.0, ALU.mult, ["bcs"], ["bcs"], s2=1.0, op1=ALU.add)
        xin = [C.sb(st, f"rxin{i}", [128, 15, TA + 2], BF16) for i in range(2)]
        sh = C.sb(st, "rsh", [128, 15, TA], BF16)
        rkv = [C.sb(st, f"rkv{i}", [128, 1536], BF16) for i in range(2)]
        T = {n: C.sb(st, "r_" + n, [128, 512], F32) for n in
             ["tmp", "sg0", "sg1", "as0", "as1", "kk", "sq", "kkn", "t", "kd0", "kd1", "rkx"]}
        s8 = C.sb(st, "s8", [128, 8], F32)
        oc = [C.sb(st, f"oc{i}", [128, 5, 512], BF16) for i in range(2)]
        od = [C.sb(st, f"od{i}", [128, 2, 2, 512], BF16) for i in range(2)]
        osg = [C.sb(st, f"osg{i}", [128, 2, 512], F32) for i in range(2)]
        pc = [C.ps(st, f"rpc{i}", [128, 512], F32) for i in range(2)]
        pT = C.ps(st, "rpT", [128, 2048], BF16)
        pl = [C.ps(st, f"rpl{i}", [128, 512], F32) for i in range(4)]
        pv = C.d["projA"].rearrange("(c p) l -> p c l", p=128)
        v3 = lambda ap: ap.rearrange("p (h e) -> p h e", h=8)
        bc8 = lambda ap: ap.unsqueeze(2).broadcast_to([128, 8, 64])
        nblk = 0
        for t in range(NTA):
            b = t % 2
            kx = ("rxin", b)
            load_halo(C, xin[b], kx, pv, t, TA, 1, L, [("d", "projA", i) for i in range(max(0, 2 * t - 1), min(L // 256, 2 * t + 3))])
            for c in range(15):
                bank = c % 2
                for k in range(3):
                    C.mm(pc[bank][:], dg[:, c, k, :], xin[b][:, c, k:k + TA], k == 0, k == 2, ["rdg", kx], [("rpc", bank)])
                fn = AF.Tanh if c == 12 else (AF.Sigmoid if c == 14 else AF.Copy)
                if c % 2 == 0 or c >= 12:
                    C.act(sh[:, c, :], pc[bank][:], fn, [("rpc", bank)], ["rsh"])
                else:
                    C.cp("dve", sh[:, c, :], pc[bank][:], [("rpc", bank)], ["rsh"])
            import os
            RWF = int(os.environ.get("RWF", "9"))
            for blk in range(4):
                if RWF < 2:
                    break
                ob = nblk % 2
                nblk += 1
                bs = slice(blk * 128, (blk + 1) * 128)
                for c in range(12):
                    C.tr(pT[:, c * 128:(c + 1) * 128], sh[:, c, bs], C.identb[:], ["rsh", "identb"], ["rpT"])
                C.cp("dve", rkv[ob][:, 0:768], pT[:, 0:768], ["rpT"], [("rkv", ob)])
                C.act(rkv[ob][:, 768:1536], pT[:, 768:1536], AF.Copy, ["rpT"], [("rkv", ob)])
                r_ = rkv[ob][:, 0:512]
                k_ = rkv[ob][:, 512:1024]
                v_ = rkv[ob][:, 1024:1536]
                krkv = ("rkv", ob)
                if RWF < 3:
                    continue
                for d in range(2):
                    ps_ = slice(64 * d, 64 * d + 64)
                    C.mm(pl[d][:], sh[ps_, 12, bs], lw[ps_, 0, :], True, True, ["rsh", "lw"], [("rpl", d)])
                    C.mm(pl[2 + d][:], sh[ps_, 13, bs], lw[ps_, 1, :], True, True, ["rsh", "lw"], [("rpl", 2 + d)])
                for d in range(2):
                    C.tt("dve", T["tmp"][:], pl[d][:], bcs[:, d, :], ALU.add, [("rpl", d), "bcs"], ["r_tmp"])
                    C.act(osg[ob][:, d, :], T["tmp"][:], AF.Sigmoid, ["r_tmp"], [("osg", ob)])
                    C.tt("dve", T["tmp"][:], pl[2 + d][:], bcs[:, 2 + d, :], ALU.add, [("rpl", 2 + d), "bcs"], ["r_tmp"])
                    C.act(T[f"as{d}"][:], T["tmp"][:], AF.Sigmoid, ["r_tmp"], [f"r_as{d}"])
                C.mm(pl[0][:], sh[:, 14, bs], lw[:, 2, :], True, True, ["rsh", "lw"], [("rpl", 0)])
                C.act(oc[ob][:, 3, :], pl[0][:], AF.Copy, [("rpl", 0)], [("oc", ob)])
                C.cp("pool", oc[ob][:, 0, :], r_, [krkv], [("oc", ob)])
                C.cp("pool", oc[ob][:, 1, :], v_, [krkv], [("oc", ob)])
                if RWF < 4:
                    continue
                C.tt("dve", T["kk"][:], k_, bcs[:, 4, :], ALU.mult, [krkv, "bcs"], ["r_kk"])
                C.tt("pool", T["sq"][:], T["kk"][:], T["kk"][:], ALU.mult, ["r_kk"], ["r_sq"])
                C.S.add("dve", lambda e: e.tensor_reduce(s8[:], v3(T["sq"][:]), AX.X, ALU.add), ["r_sq"], ["s8"])
                C.act(s8[:], s8[:], AF.Sqrt, ["s8"], ["s8"])
                C.ts("dve", s8[:], s8[:], 1e-12, ALU.max, ["s8"], ["s8"])
                C.recip(s8[:], s8[:], ["s8"], ["s8"])
                C.tt("dve", v3(T["kkn"][:]), v3(T["kk"][:]), bc8(s8[:]), ALU.mult, ["r_kk", "s8"], ["r_kkn"])
                C.cp("pool", oc[ob][:, 2, :], T["kkn"][:], ["r_kkn"], [("oc", ob)])
                if RWF < 5:
                    continue
                for d in range(2):
                    a_ = T[f"as{d}"]
                    C.tt("pool", T["t"][:], a_[:], bcs[:, 5, :], ALU.mult, [f"r_as{d}", "bcs"], ["r_t"])
                    C.tt("pool", T["t"][:], T["t"][:], bcs[:, 6, :], ALU.add, ["r_t", "bcs"], ["r_t"])
                    C.tt("dve", T[f"kd{d}"][:], k_, T["t"][:], ALU.mult, [krkv, "r_t"], [f"r_kd{d}"])
                    C.cp("pool", od[ob][:, d, 0, :], T[f"kd{d}"][:], [f"r_kd{d}"], [("od", ob)])
                    C.tt("dve", od[ob][:, d, 1, :], T["kkn"][:], a_[:], ALU.mult, ["r_kkn", f"r_as{d}"], [("od", ob)])
                if RWF == 6:
                    row = t * 4 + blk
                    rs_ = slice(row * 128, (row + 1) * 128)
                    C.dma(C.d["rw_c"][rs_], oc[ob][:], [("oc", ob)], [("d", "rw_c", row)], q="pool")
                    for d in range(2):
                        C.dma(C.d["rw_d"][d, rs_], od[ob][:, d], [("od", ob)], [("d", "rw_d", d, row)], q="pool")
                        C.dma(C.d["rw_sg"][d, rs_], osg[ob][:, d, :], [("osg", ob)], [("d", "rw_sg", d, row)], q="pool")
                if RWF <= 6:
                    continue
                C.tt("pool", T["kd0"][:], T["kd0"][:], T["kd1"][:], ALU.add, ["r_kd0", "r_kd1"], ["r_kd0"])
                C.tt("dve", T["rkx"][:], r_, bcs[:, 7, :], ALU.mult, [krkv, "bcs"], ["r_rkx"])
                C.tt("pool", T["rkx"][:], T["rkx"][:], T["kd0"][:], ALU.mult, ["r_rkx", "r_kd0"], ["r_rkx"])
                C.S.add("dve", lambda e: e.tensor_reduce(s8[:], v3(T["rkx"][:]), AX.X, ALU.add), ["r_rkx"], ["s8"])
                C.tt("dve", v3(oc[ob][:, 4, :]), v3(v_), bc8(s8[:]), ALU.mult, [krkv, "s8"], [("oc", ob)])
                row = t * 4 + blk
                rs_ = slice(row * 128, (row + 1) * 128)
                if os.environ.get("RWD", "1") == "0":
                    continue
                C.dma(C.d["rw_c"][rs_], oc[ob][:], [("oc", ob)], [("d", "rw_c", row)], q="pool")
                for d in range(2):
                    C.dma(C.d["rw_d"][d, rs_], od[ob][:, d], [("od", ob)], [("d", "rw_d", d, row)], q="pool")
                    C.dma(C.d["rw_sg"][d, rs_], osg[ob][:, d, :], [("osg", ob)], [("d", "rw_sg", d, row)], q="pool")
    C.S.barrier()


def rw_scan(C, l):
    L = C.L
    NT = L // 128
    v3 = lambda ap: ap.rearrange("p (h e) -> p h e", h=8)
    bc8 = lambda ap: ap.unsqueeze(2).broadcast_to([128, 8, 64])
    with contextlib.ExitStack() as st:
        rwm = C.sb(st, "rwm", [128, 2, 5, 128], F32)
        C.dma(rwm[:, 0], C.d["c_rwm"][0], [], ["rwm"])
        C.dma(rwm[:, 1], C.d["c_rwm"][1], [], ["rwm"])
        lnx = C.sb(st, "lnx", [128, 2, 512], F32)
        C.dma(lnx[:], C.d["rw_lnx"][l], [], ["lnx"])
        id64 = C.sb(st, "id64", [64, 16, 64], F32)
        C.cp("dve", id64[:], C.ident[0:64, 0:64].unsqueeze(1).broadcast_to([64, 16, 64]), ["ident"], ["id64"])
        ci = [C.sb(st, f"ci{i}", [128, 5, 512], BF16) for i in range(2)]
        di = [C.sb(st, f"di{i}", [128, 2, 512], BF16) for i in range(2)]
        sgi = [C.sb(st, f"sgi{i}", [128, 512], F32) for i in range(2)]
        yfi = [C.sb(st, f"yfi{i}", [128, 512], F32) for i in range(2)]
        E = C.sb(st, "E", [128, 4, 512], F32)
        tk = C.sb(st, "tk", [128, 6, 512], BF16)
        fT = C.sb(st, "fT", [128, 4, 2, 4, 64], BF16)
        NMb = C.sb(st, "NMb", [128, 8, 128], BF16)
        NMk = C.sb(st, "NMk", [128, 8, 128], BF16)
        PP = [C.sb(st, f"PP{i}", [128, 8, 128], BF16) for i in range(2)]
        Xs = C.sb(st, "Xs", [128, 8, 128], BF16)
        gam = C.sb(st, "gam", [64, 8, 2], F32)
        dgam = C.sb(st, "dgam", [64, 16, 64], F32)
        Gs = C.sb(st, "Gs", [64, 16, 64], F32)
        Hs = C.sb(st, "Hs", [64, 16, 64], F32)
        Qs = C.sb(st, "Qs", [64, 16, 64], BF16)
        ST32 = C.sb(st, "ST32", [64, 512], F32)
        STb = C.sb(st, "STb", [64, 512], BF16)
        yo = [C.sb(st, f"yo{i}", [128, 512], F32) for i in range(2)]
        W = {n: C.sb(st, "w_" + n, [128, 512], F32) for n in ["y", "sq", "t"]}
        m8 = C.sb(st, "m8", [128, 8], F32)
        q8 = C.sb(st, "q8", [128, 8], F32)
        y3 = C.sb(st, "ry3", [128, 512], BF16)
        yT = [C.sb(st, f"ryT{i}", [128, 4, 128], BF16) for i in range(2)]
        pT = C.ps(st, "spT", [128, 2048], BF16)
        pA = C.ps(st, "spA", [128, 1024], F32)
        pB = C.ps(st, "spB", [128, 1024], F32)
        pC = C.ps(st, "spC", [128, 1024], F32)
        cv = C.d["rw_c"].rearrange("(q p) s n -> p q s n", p=128)
        yav = C.d["yaT"].rearrange("(c p) l -> p c l", p=128)
        it = 0
        for dr in range(2):
            order = list(range(NT)) if dr == 0 else list(range(NT - 1, -1, -1))
            dv = C.d["rw_d"][dr].rearrange("(q p) s n -> p q s n", p=128)
            sv = C.d["rw_sg"][dr].rearrange("(q p) n -> p q n", p=128)
            yfv = C.d["rw_yf"].rearrange("(q p) n -> p q n", p=128)
            C.memset("dve", ST32[:], 0.0, ["ST32"])
            C.memset("dve", STb[:], 0.0, ["STb"])
            M = rwm[:, dr]
            INC, STR, STRT = M[:, 0, :], M[:, 1, :], M[:, 2, :]
            mask1 = M[:, 3, :]
            maskT = M[:, 4, 0:64]
            ind = M[:, 4, 64:66]
            for q in order:
                b = it % 2
                it += 1
                C.dma(ci[b][:], cv[:, q], [], [("ci", b)])
                C.dma(di[b][:], dv[:, q], [], [("di", b)])
                C.dma(sgi[b][:], sv[:, q], [], [("sgi", b)], q="act")
                if dr == 1:
                    C.dma(yfi[b][:], yfv[:, q], [("d", "rw_yf", q)], [("yfi", b)], q="act")
                r_, v_, kkn_ = ci[b][:, 0, :], ci[b][:, 1, :], ci[b][:, 2, :]
                kd_, bd_ = di[b][:, 0, :], di[b][:, 1, :]
                kci, kdi, ksg = ("ci", b), ("di", b), ("sgi", b)
                kA = [("pA", 0), ("pA", 1)]
                kB = [("pB", 0), ("pB", 1)]
                kX = [("Xs", 0), ("Xs", 1)]
                kP = lambda i: [("PP", i, 0), ("PP", i, 1)]
                C.mm(pA[:, 0:512], INC, sgi[b][:], True, True, ["rwm", ksg], [("pA", 0)])
                C.mm(pA[:, 512:1024], STR, sgi[b][:], True, True, ["rwm", ksg], [("pA", 1)])
                C.mm(pB[:, 0:512], STRT, sgi[b][:], True, True, ["rwm", ksg], [("pB", 0)])
                for h in range(8):
                    C.mm(pB[0:64, 512 + 2 * h:512 + 2 * h + 2], sgi[b][:, h * 64:(h + 1) * 64], ind, h == 0, True, [ksg, "rwm"], [("pB", 1)],
                         skip_group_check=True)
                C.act(E[:, 0, :], pA[:, 512:1024], AF.Exp, [("pA", 1)], ["E"], scale=-RW_C)
                C.act(E[:, 1, :], pA[:, 0:512], AF.Exp, [("pA", 0)], ["E"], scale=-RW_C)
                C.act(E[:, 2, :], pA[:, 0:512], AF.Exp, [("pA", 0)], ["E"], scale=RW_C)
                C.act(E[:, 3, :], pB[:, 0:512], AF.Exp, [("pB", 0)], ["E"], scale=-RW_C)
                C.act(gam[:], pB[0:64, 512:528].rearrange("p (h c) -> p h c", c=2), AF.Exp, [("pB", 1)], ["gam"], scale=-RW_C)
                C.stt(tk[:, 0, :], kkn_, -1.0, E[:, 0, :], ALU.mult, ALU.mult, [kci, "E"], ["tk"])
                C.tt("pool", tk[:, 1, :], r_, E[:, 1, :], ALU.mult, [kci, "E"], ["tk"])
                C.tt("dve", tk[:, 2, :], bd_, E[:, 2, :], ALU.mult, [kdi, "E"], ["tk"])
                C.tt("pool", tk[:, 3, :], kd_, E[:, 2, :], ALU.mult, [kdi, "E"], ["tk"])
                C.tt("dve", tk[:, 4, :], bd_, E[:, 3, :], ALU.mult, [kdi, "E"], ["tk"])
                C.tt("pool", tk[:, 5, :], kd_, E[:, 3, :], ALU.mult, [kdi, "E"], ["tk"])
                for x in range(4):
                    for hp in range(4):
                        o = pT[:, (hp * 4 + x) * 128:(hp * 4 + x + 1) * 128]
                        C.tr(o, tk[:, x, hp * 128:(hp + 1) * 128], C.identb[:], ["tk", "identb"], ["spT"])
                src4 = pT[:].rearrange("p (hp x c t) -> p hp x c t", hp=4, x=4, c=2)
                for c in range(2):
                    if c == 0:
                        C.cp("dve", fT[:, :, c, :, :], src4[:, :, :, c, :], ["spT"], ["fT"])
                    else:
                        C.act(fT[:, :, c, :, :], src4[:, :, :, c, :], AF.Copy, ["spT"], ["fT"])
                for par, c, h in [(par, c, h) for par in range(2) for c in range(2) for h in range(par, 8, 2)]:
                    hp, base = h // 2, 64 * (h % 2)
                    ps_ = slice(base, base + 64)
                    oc_ = slice(64 * c, 64 * c + 64)
                    C.mm(pA[oc_, h * 128:(h + 1) * 128], fT[ps_, hp, c, 2, :], fT[ps_, hp, c, 0:2, :], h % 4 == 0, True, ["fT"], [("pA", h // 4)],
                         skip_group_check=True)
                    C.mm(pB[oc_, h * 128:(h + 1) * 128], fT[ps_, hp, c, 3, :], fT[ps_, hp, c, 0:2, :], h % 4 == 0, True, ["fT"], [("pB", h // 4)],
                         skip_group_check=True)
                    C.mm(pC[oc_, h * 64:(h + 1) * 64], fT[ps_, hp, c, 0, :], fT[ps_, hp, c, 2, :], h == 0, True, ["fT"], ["pC"],
                         skip_group_check=True)
                C.tt("dve", NMb[:], pA[:].rearrange("p (h n) -> p h n", h=8), mask1.unsqueeze(1).broadcast_to([128, 8, 128]), ALU.mult,
                     kA + ["rwm"], ["NMb"])
                C.tt("dve", NMk[:], pB[:].rearrange("p (h n) -> p h n", h=8), mask1.unsqueeze(1).broadcast_to([128, 8, 128]), ALU.mult,
                     kB + ["rwm"], ["NMk"])
                C.cp("pool", PP[0][:, :, 0:64], NMb[:, :, 0:64], ["NMb"], kP(0))
                C.tt("dve", PP[0][:, :, 64:128], pC[:, 0:512].rearrange("p (h n) -> p h n", h=8), maskT.unsqueeze(1).broadcast_to([128, 8, 64]),
                     ALU.mult, ["pC", "rwm"], kP(0))
                for h in range(8):
                    C.mm(pA[:, h * 128:h * 128 + 64], C.identb[:], tk[:, 0, h * 64:(h + 1) * 64], h % 4 == 0, True, ["identb", "tk", "NMb"], [("pA", h // 4)],
                         skip_group_check=True)
                for c in range(2):
                    oc_ = slice(64 * c, 64 * c + 64)
                    for h in range(8):
                        C.mm(pA[oc_, h * 128 + 64:(h + 1) * 128], NMk[oc_, h, 0:64], ci[b][oc_, 1, h * 64:(h + 1) * 64], False, True,
                             ["NMk", kci], [("pA", h // 4)], skip_group_check=True)
                pA3 = pA[:].rearrange("p (h n) -> p h n", h=8)
                pB3 = pB[:].rearrange("p (h n) -> p h n", h=8)
                for g2 in range(2):
                    hs = slice(4 * g2, 4 * g2 + 4)
                    C.act(Xs[:, hs, :], pA3[:, hs, :], AF.Copy, [("pA", g2)], [("Xs", g2)])
                for rd in range(6):
                    pc_, pn_ = PP[rd % 2], PP[(rd + 1) % 2]
                    for g2 in range(2):
                        hs = slice(4 * g2, 4 * g2 + 4)
                        kc, kn = ("PP", rd % 2, g2), ("PP", (rd + 1) % 2, g2)
                        for c in range(2):
                            oc_ = slice(64 * c, 64 * c + 64)
                            for h in range(4 * g2, 4 * g2 + 4):
                                C.mm(pA[oc_, h * 128:(h + 1) * 128], pc_[oc_, h, 0:64], Xs[oc_, h, :], False, True, [kc, ("Xs", g2)], [("pA", g2)],
                                     skip_group_check=True)
                        if rd < 5:
                            for c in range(2):
                                oc_ = slice(64 * c, 64 * c + 64)
                                for h in range(4 * g2, 4 * g2 + 4):
                                    C.mm(pB[oc_, h * 128:h * 128 + 64], pc_[oc_, h, 64:128], pc_[oc_, h, 0:64], h % 4 == 0, True, [kc], [("pB", g2)],
                                         skip_group_check=True)
                                    C.mm(pB[oc_, h * 128 + 64:(h + 1) * 128], pc_[oc_, h, 0:64], pc_[oc_, h, 64:128], False, True, [kc], [("pB", g2)],
                                         skip_group_check=True)
                            C.cp("dve", pn_[:, hs, :], pB3[:, hs, :], [("pB", g2)], [kn])
                        C.act(Xs[:, hs, :], pA3[:, hs, :], AF.Copy, [("pA", g2)], [("Xs", g2)])
                for c in range(2):
                    oc_ = slice(64 * c, 64 * c + 64)
                    for h in range(8):
                        g = c * 8 + h
                        fs = (g % 8 == 0)
                        kx_ = ("Xs", h // 4)
                        C.mm(pC[0:64, g * 64:(g + 1) * 64], Xs[oc_, h, 0:64], tk[oc_, 4, h * 64:(h + 1) * 64], fs, True, [kx_, "tk"], ["pC", "pC2"],
                             skip_group_check=True)
                        C.mm(pB[0:64, g * 64:(g + 1) * 64], tk[oc_, 4, h * 64:(h + 1) * 64], Xs[oc_, h, 64:128], fs, False, [kx_, "tk"], [("pB", c)],
                             skip_group_check=True)
                        C.mm(pB[0:64, g * 64:(g + 1) * 64], tk[oc_, 5, h * 64:(h + 1) * 64], ci[b][oc_, 1, h * 64:(h + 1) * 64], False, True,
                             ["tk", kci], [("pB", c)], skip_group_check=True)
                        C.mm(pA[0:64, g * 64:(g + 1) * 64], Xs[oc_, h, 0:64], NMb[oc_, h, 64:128], fs, False, [kx_, "NMb"], [("pA", c)],
                             skip_group_check=True)
                for par, c, h in [(par, c, h) for par in range(2) for c in range(2) for h in range(par, 8, 2)]:
                    hp, base = h // 2, 64 * (h % 2)
                    ps_ = slice(base, base + 64)
                    g = c * 8 + h
                    C.mm(pA[0:64, g * 64:(g + 1) * 64], C.identb[ps_, base:base + 64], fT[ps_, hp, c, 1, :], False, True, ["identb", "fT"], [("pA", c)],
                         skip_group_check=True)
                for c in range(2):
                    C.tt("pool", dgam[:, c * 8:(c + 1) * 8, :], id64[:, 0:8, :], gam[:, :, c].unsqueeze(2).broadcast_to([64, 8, 64]), ALU.mult,
                         ["id64", "gam"], ["dgam"])
                C.tt("dve", Gs[:], dgam[:], pC[0:64, :].rearrange("p (g n) -> p g n", g=16), ALU.add, ["dgam", "pC", "pC2"], ["Gs"])
                C.act(Hs[:], pB[0:64, :].rearrange("p (g n) -> p g n", g=16), AF.Copy, kB, ["Hs"])
                C.act(Qs[:], pA[0:64, :].rearrange("p (g n) -> p g n", g=16), AF.Copy, kA, ["Qs"])
                for c in ([0, 1] if dr == 0 else [1, 0]):
                    oc_ = slice(64 * c, 64 * c + 64)
                    for h in range(8):
                        o = pC[oc_, h * 64:(h + 1) * 64]
                        C.mm(o, NMb[oc_, h, 64:128], Xs[oc_, h, 64:128], h == 0, False, ["NMb", ("Xs", h // 4), "Gs"], ["pC"], skip_group_check=True)
                        C.mm(o, NMk[oc_, h, 64:128], ci[b][oc_, 1, h * 64:(h + 1) * 64], False, False, ["NMk", kci], ["pC"], skip_group_check=True)
                        C.mm(o, Qs[:, c * 8 + h, :], STb[:, h * 64:(h + 1) * 64], False, True, ["Qs", "STb"], ["pC"], skip_group_check=True)
                    for h in range(8):
                        C.mm(pC[0:64, 512 + h * 64:512 + (h + 1) * 64], Gs[:, c * 8 + h, :], ST32[:, h * 64:(h + 1) * 64], h == 0, True,
                             ["Gs", "ST32"], ["pC2"], skip_group_check=True)
                    C.tt("dve", ST32[:], pC[0:64, 512:1024], Hs[:, c * 8:(c + 1) * 8, :], ALU.add, ["pC2", "Hs"], ["ST32"])
                    C.act(STb[:], ST32[:], AF.Copy, ["ST32"], ["STb"])
                if dr == 0:
                    ob = it % 2
                    C.act(yo[ob][:], pC[:, 0:512], AF.Copy, ["pC"], [("yo", ob)])
                    C.dma(yfv[:, q], yo[ob][:], [("yo", ob)], [("d", "rw_yf", q)], q="pool")
                    continue
                y = W["y"]
                C.tt("dve", y[:], pC[:, 0:512], yfi[b][:], ALU.add, ["pC", ("yfi", b)], ["w_y"])
                C.S.add("dve", lambda e: e.tensor_reduce(m8[:], v3(y[:]), AX.X, ALU.add), ["w_y"], ["m8"])
                C.ts("dve", m8[:], m8[:], 1.0 / 64, ALU.mult, ["m8"], ["m8"])
                C.tt("pool", v3(y[:]), v3(y[:]), bc8(m8[:]), ALU.subtract, ["w_y", "m8"], ["w_y"])
                C.tt("pool", W["sq"][:], y[:], y[:], ALU.mult, ["w_y"], ["w_sq"])
                C.S.add("dve", lambda e: e.tensor_reduce(q8[:], v3(W["sq"][:]), AX.X, ALU.add), ["w_sq"], ["q8"])
                C.ts("dve", q8[:], q8[:], 1.0 / 64, ALU.mult, ["q8"], ["q8"], s2=RW_GN_EPS, op1=ALU.add)
                C.act(q8[:], q8[:], AF.Sqrt, ["q8"], ["q8"])
                C.recip(q8[:], q8[:], ["q8"], ["q8"])
                C.tt("dve", v3(y[:]), v3(y[:]), bc8(q8[:]), ALU.mult, ["w_y", "q8"], ["w_y"])
                C.tt("pool", y[:], y[:], lnx[:, 0, :], ALU.mult, ["w_y", "lnx"], ["w_y"])
                C.tt("pool", y[:], y[:], lnx[:, 1, :], ALU.add, ["w_y", "lnx"], ["w_y"])
                C.tt("dve", y[:], y[:], ci[b][:, 4, :], ALU.add, ["w_y", kci], ["w_y"])
                C.tt("dve", y3[:], y[:], ci[b][:, 3, :], ALU.mult, ["w_y", kci], ["ry3"])
                for c4 in range(4):
                    C.tr(pT[:, c4 * 128:(c4 + 1) * 128], y3[:, c4 * 128:(c4 + 1) * 128], C.identb[:], ["ry3", "identb"], ["spT"])
                C.act(yT[b][:], pT[:, 0:512].rearrange("p (c l) -> p c l", c=4), AF.Copy, ["spT"], [("ryT", b)])
                C.dma(yav[:, :, q * 128:(q + 1) * 128], yT[b][:], [("ryT", b)], [("d", "yaT", q)], q="pool")
            C.S.barrier()
    C.S.barrier()


def stage_rwkv(C, l):
    rw_front(C, l)
    if not getattr(C, "no_rwscan", False):
        rw_scan(C, l)

def stage_out(C, l, src, dst):
    L = C.L
    TT = 256
    NT = L // TT
    with contextlib.ExitStack() as st:
        wg = C.sb(st, "wg", [128, 8, 3 * D], BF16)
        wp = C.sb(st, "wp", [128, 3, 4, D], BF16)
        wo = C.sb(st, "wout", [128, 8, D], BF16)
        with contextlib.ExitStack() as st2:
            load_w_bf16(C, st2, lambda k, a, b: wg[:, k, a:b], C.d["mix_w_in"][l][:, O_G:IN_COLS], 8, 3 * D, "wg", blk=1024)
            for br, nm in enumerate(["proj_a", "proj_b", "proj_c"]):
                load_w_bf16(C, st2, (lambda br: (lambda k, a, b: wp[:, br, k, a:b]))(br), C.d[nm][l], 4, D, "wp", blk=1024)
            load_w_bf16(C, st2, lambda k, a, b: wo[:, k, a:b], C.d["mix_w_out"][l], 8, D, "wout", blk=1024)
            C.S.barrier()
        xt = [C.sb(st, f"xt{i}", [128, 8, TT], F32) for i in range(3)]
        hTb = [C.sb(st, f"hT{i}", [128, 8, TT], BF16) for i in range(2)]
        yin = [C.sb(st, f"yin{i}", [128, 3, 4, TT], BF16) for i in range(2)]
        sg = [C.sb(st, f"sg{i}", [128, TT], F32) for i in range(4)]
        tm = [C.sb(st, f"tm{i}", [128, TT], F32) for i in range(6)]
        mT = C.sb(st, "mT", [128, 8, TT], BF16)
        zsq = C.sb(st, "zsq", [128, 8, TT], F32)
        mean = C.sb(st, "mean", [128, TT], F32)
        msq = C.sb(st, "msq", [128, TT], F32)
        rstd = C.sb(st, "rstd", [128, TT], F32)
        pgp = [C.ps(st, f"pgp{i}", [128, 512], F32) for i in range(4)]
        py = [C.ps(st, f"py{i}", [128, 512], F32) for i in range(4)]
        srcv = src.rearrange("(c p) l -> p c l", p=128)
        dstv = dst.rearrange("(c p) l -> p c l", p=128)
        yv = [C.d[n].rearrange("(c p) l -> p c l", p=128) for n in ["yaT", "ybT", "ycT"]]
        mg = C.modT[l]

        def load(t):
            sl = slice(t * TT, (t + 1) * TT)
            C.dma(xt[t % 3][:], srcv[:, :, sl], [("d", src.name, t)], [("xt", t % 3)])
            for br in range(3):
                C.dma(yin[t % 2][:, br], yv[br][:, :, sl], [], [("yin", t % 2)], q="act" if br == 1 else "sp")

        def prep(t):
            x, kx = xt[t % 3], ("xt", t % 3)
            modulate_tile(C, hTb[t % 2], x, kx, l, 3, TT, hkey=("hT", t % 2))
            C.ts("pool", x[:], x[:], float(ALPHA), ALU.mult, [kx], [kx])

        def tail_pieces(t):
            x, kx = xt[t % 3], ("xt", t % 3)
            ps = layer_norm_pieces(C, x, kx, TT, zsq, mean, msq, rstd, py[0], ("py", 0), py[1], ("py", 1), l, 1)
            ps.append(lambda: C.dma(dstv[:, :, t * TT:(t + 1) * TT], x[:], [kx], [("d", dst.name, t)], q="pool"))
            return ps

        load(0)
        prep(0)
        pend = []
        n = 0
        for t in range(NT):
            b = t % 2
            x, kx = xt[t % 3], ("xt", t % 3)
            hT, kh = hTb[b], ("hT", b)
            if t + 1 < NT:
                load(t + 1)
            for d in range(8):
                for br in range(3):
                    gb = n % 4
                    n += 1
                    tb = (d % 2) * 3 + br
                    for k in range(8):
                        C.mm(pgp[gb][:, 0:TT], wg[:, k, br * D + d * 128:br * D + (d + 1) * 128], hT[:, k, :], k == 0, k == 7,
                             ["wg", kh], [("pgp", gb)])
                    for k in range(4):
                        C.mm(pgp[gb][:, TT:2 * TT], wp[:, br, k, d * 128:(d + 1) * 128], yin[b][:, br, k, :], k == 0, k == 3,
                             ["wp", ("yin", b)], [("pgp", gb)])
                    C.act(sg[gb][:], pgp[gb][:, 0:TT], AF.Sigmoid, [("pgp", gb)], [("sg", gb)])
                    C.tt("dve", tm[tb][:], sg[gb][:], pgp[gb][:, TT:2 * TT], ALU.mult, [("sg", gb), ("pgp", gb)], [("tm", tb)])
                    if (d, br) >= (0, 1) and pend:
                        pend.pop(0)()
                    if (d, br) == (5, 0) and t + 1 < NT:
                        prep(t + 1)
                t0_ = (d % 2) * 3
                C.tt("dve", tm[t0_][:], tm[t0_][:], tm[t0_ + 1][:], ALU.add, [("tm", t0_), ("tm", t0_ + 1)], [("tm", t0_)])
                C.tt("dve", mT[:, d, :], tm[t0_][:], tm[t0_ + 2][:], ALU.add, [("tm", t0_), ("tm", t0_ + 2)], ["mT"])
            while pend:
                pend.pop(0)()
            for d in range(8):
                pyd = py[d // 2][:, (d % 2) * TT:(d % 2 + 1) * TT]
                for k in range(8):
                    C.mm(pyd, wo[:, k, d * 128:(d + 1) * 128], mT[:, k, :], k == 0, k == 7, ["wout", "mT"], [("py", d // 2)])
                C.stt(x[:, d, :], pyd, mg[:, 40 + d:40 + d + 1], x[:, d, :], ALU.mult, ALU.add, [("py", d // 2), kx, ("modT", l)], [kx])
            pend = tail_pieces(t)
        while pend:
            pend.pop(0)()
    C.S.barrier()

STAGES = ["ffn1", "inproj", "ssd", "hyena", "rwkv", "out", "ffn2"]


def build(L, upto="ffn2", depth=DEPTH, debug=()):
    C = Ctx(L)
    C.no_rwscan = 'norwscan' in debug
    dbg = lambda n: n in debug
    C.din("x", [L, D])
    C.din("c", [128, 8])
    C.din("c_ident", [128, 128])
    C.din("c_masks", [128, 6, 128])
    C.din("ada_w", [DEPTH, D, 9 * D])
    C.din("ada_b", [DEPTH, 128, 72])
    C.din("ln_w", [128, DEPTH * 3 * 8])
    C.din("ln_b", [128, DEPTH * 3 * 8])
    C.din("ffn_wi", [DEPTH, 2, D, 2 * D_FF])
    C.din("ffn_wo", [DEPTH, 2, D_FF, D])
    C.din("mix_w_in", [DEPTH, D, IN_COLS])
    C.din("ssd_cw", [DEPTH, 128, 40])
    C.din("ssd_cb", [DEPTH, 128, 8])
    C.din("ssd_dtb", [DEPTH, 128, 16])
    C.din("ssd_alog", [DEPTH, 128, 16])
    C.din("ssd_Dbc", [DEPTH, 128, 512])
    C.din("ssd_nw", [DEPTH, 128, 512])
    C.din("c_featsT", [33, L])
    C.din("c_distb", [128, L])
    C.din("c_ndelta", [128, 4])
    C.din("c_F256", [128, 512], BF16)
    C.din("c_Tw", [128, 768])
    C.din("c_L4", [128, 4, 64], BF16)
    C.din("c_G3", [128, 3, 128], BF16)
    C.din("c_TAB", [128, 2, 2, 128])
    C.din("c_FIN", [128, 2, 2, 128], BF16)
    C.din("hy_f_w1", [DEPTH, 33, 64])
    C.din("hy_f_w2", [DEPTH, 64, 64])
    C.din("hy_f_w3", [DEPTH, 64, 512])
    C.din("hy_fb", [DEPTH, 64, 2])
    C.din("hy_freq", [DEPTH, 64, 2])
    C.din("hy_bias", [DEPTH, 128, 4])
    C.din("hy_cw", [DEPTH, 128, 36])
    C.din("hy_cb", [DEPTH, 128, 12])
    C.din("proj_a", [DEPTH, 512, D])
    C.din("proj_b", [DEPTH, 512, D])
    C.din("proj_c", [DEPTH, 512, D])
    C.din("mix_w_out", [DEPTH, D, D])
    C.din("c_rwm", [2, 128, 5, 128])
    C.din("rw_mu", [DEPTH, 128, 2, 15])
    C.din("rw_w2", [DEPTH, 128, 512])
    C.din("rw_a2", [DEPTH, 128, 512])
    C.din("rw_g2", [DEPTH, 128, 512])
    C.din("rw_bc", [DEPTH, 128, 7, 512])
    C.din("rw_lnx", [DEPTH, 128, 2, 512])
    C.dout("out", [L, D])
    xa = C.dscr("xTa", [D, L], debug=dbg("xTa"))
    xb = C.dscr("xTb", [D, L], debug=dbg("xTb"))
    C.dscr("projA", [A_COLS, L], BF16, debug=dbg("projA"))
    C.dscr("projB", [B_COLS, L], BF16, debug=dbg("projB"))
    C.dscr("projX", [XBC, L], BF16, debug=dbg("projX"))
    C.dscr("z_tok", [L, 512], BF16, debug=dbg("z_tok"))
    C.dscr("dt_tok", [L, 16], F32, debug=dbg("dt_tok"))
    C.dscr("ssd_bcT", [512, L], BF16, debug=dbg("ssd_bcT"))
    C.dscr("ssd_xbtok", [L, 768], BF16, debug=dbg("ssd_xbtok"))
    C.dscr("ycT", [512, L], BF16, debug=dbg("ycT"))
    C.dscr("hy_hT", [512, L], BF16, debug=dbg("hy_hT"))
    C.dscr("hy_x0T", [512, L], BF16, debug=dbg("hy_x0T"))
    C.dscr("hy_zT", [512, L], BF16, debug=dbg("hy_zT"))
    C.dscr("hy_Hf", [256, 128, 768], BF16, debug=dbg("hy_Hf"))
    C.dscr("ybT", [512, L], BF16, debug=dbg("ybT"))
    C.dscr("rw_c", [L, 5, 512], BF16, debug=dbg("rw_c"))
    C.dscr("rw_d", [2, L, 2, 512], BF16, debug=dbg("rw_d"))
    C.dscr("rw_sg", [2, L, 512], F32, debug=dbg("rw_sg"))
    C.dscr("rw_yf", [L, 512], F32, debug=dbg("rw_yf"))
    C.dscr("yaT", [512, L], BF16, debug=dbg("yaT"))
    last = STAGES.index(upto)
    with contextlib.ExitStack() as stp:
        stage_consts(C, stp)
        stage_transpose(C, C.d["x"], xa, True)
        cur, nxt = xa, xb
        for l in range(depth):
            stage_mod(C, l, stp)
            stage_ffn(C, l, 0, cur, nxt, 0, 0)
            cur, nxt = nxt, cur
            if last >= 1:
                stage_inproj(C, l, cur)
            if last >= 2 and "nossd" not in debug:
                stage_ssd(C, l)
            if last >= 3 and "nohy" not in debug:
                stage_hyena(C, l)
            if last >= 4:
                stage_rwkv(C, l)
            if last >= 5:
                stage_out(C, l, cur, nxt)
                cur, nxt = nxt, cur
            if last >= 6:
                stage_ffn(C, l, 1, cur, nxt, 6, 2)
                cur, nxt = nxt, cur
        stage_transpose(C, cur, C.d["out"], False)
        C.S.emit()
    return C


def fm(v, ncol):
    v = np.asarray(v, np.float32)
    sh = v.shape[:-1]
    return np.ascontiguousarray(np.swapaxes(v.reshape(sh + (ncol, 128)), -1, -2))


def bc128(v):
    v = np.asarray(v, np.float32)
    return np.ascontiguousarray(np.broadcast_to(v[:, None, :], (v.shape[0], 128, v.shape[1])))


def const_masks():
    k = np.arange(128)[:, None]
    s = np.arange(128)[None, :]
    m = np.zeros((128, 6, 128), np.float32)
    m[:, 0] = k > s
    m[:, 1] = k <= s
    m[:, 2] = k < s
    m[:, 3] = k >= s
    m[:, 4] = NEG * (s < k)
    m[:, 5] = NEG * (s > k)
    return m


def hyena_consts(L):
    bf = ml_dtypes.bfloat16
    m = {}
    t = np.linspace(0.0, 1.0, L, dtype=np.float32)
    w = (2.0 * np.float32(math.pi) * np.arange(L, dtype=np.float32) / np.float32(L)).astype(np.float32)
    f = np.linspace(1e-4, 15, 16, dtype=np.float32)
    wf = (w[:, None] * f).astype(np.float32)
    feats = np.concatenate([t[:, None], np.cos(wf), np.sin(wf)], -1).astype(np.float32)
    m["c_featsT"] = np.ascontiguousarray(feats.T)
    dist = np.abs(2.0 * t - 1.0).astype(np.float32)
    m["c_distb"] = np.ascontiguousarray(np.broadcast_to(dist[None, :], (128, L)))
    deltas = np.abs(np.linspace(math.log(1e-2) / 1.5, math.log(1e-2) / 0.3, 512, dtype=np.float32))
    m["c_ndelta"] = fm(-deltas, 4)
    N = 16384
    N2 = 64
    N1 = N // N2
    n1 = np.arange(128, dtype=np.float64)[:, None]
    k1 = np.arange(N1, dtype=np.float64)[None, :]
    m["c_F256"] = np.concatenate([np.cos(2 * np.pi * n1 * k1 / N1), -np.sin(2 * np.pi * n1 * k1 / N1)], 1).astype(bf)
    n2 = np.arange(N2, dtype=np.float64)[:, None]
    Tr = np.cos(2 * np.pi * n2 * k1 / N)
    Ti = -np.sin(2 * np.pi * n2 * k1 / N)
    T = np.concatenate([Tr, Ti, Tr], 1)
    m["c_Tw"] = np.concatenate([T, T], 0).astype(np.float32)
    k2 = np.arange(N2, dtype=np.float64)[None, :]
    c64 = np.cos(2 * np.pi * n2 * k2 / N2)
    s64 = np.sin(2 * np.pi * n2 * k2 / N2)
    L4 = np.stack([c64, -c64, s64, -s64], 1)
    m["c_L4"] = np.concatenate([L4, L4], 0).astype(bf)
    G = np.stack([np.concatenate([c64, s64], 1), np.concatenate([-c64, -s64], 1), np.concatenate([-s64, c64], 1)], 1)
    m["c_G3"] = np.concatenate([G, G], 0).astype(bf)
    kk = np.arange(N1, dtype=np.float64)[:, None]
    nn = np.arange(N2, dtype=np.float64)[None, :]
    tc = np.cos(2 * np.pi * kk * nn / N)
    ts = np.sin(2 * np.pi * kk * nn / N)
    TA = np.concatenate([tc, -ts], 1).reshape(2, 128, 128)
    TB = np.concatenate([ts, tc], 1).reshape(2, 128, 128)
    m["c_TAB"] = np.ascontiguousarray(np.transpose(np.stack([TA, TB], 0), (2, 0, 1, 3))).astype(np.float32)
    n1w = np.arange(N1 // 4, N1 // 4 + 128, dtype=np.float64)[None, :]
    cf = (np.cos(2 * np.pi * kk * n1w / N1) / N).reshape(2, 128, 128)
    sf = (-np.sin(2 * np.pi * kk * n1w / N1) / N).reshape(2, 128, 128)
    m["c_FIN"] = np.ascontiguousarray(np.transpose(np.stack([cf, sf], 1), (2, 0, 1, 3))).astype(bf)
    return m


def rwkv_masks():
    m = np.zeros((2, 128, 5, 128), np.float32)
    s = np.arange(64)[:, None]
    t = np.arange(64)[None, :]
    for d in range(2):
        inc = (s <= t) if d == 0 else (s >= t)
        strict = (s < t) if d == 0 else (s > t)
        inc = inc.astype(np.float32)
        strict = strict.astype(np.float32)
        for c in range(2):
            sl = slice(64 * c, 64 * c + 64)
            m[d, sl, 0, sl] = inc
            m[d, sl, 1, sl] = strict
            m[d, sl, 2, sl] = strict.T
            m[d, sl, 3, 0:64] = strict
            m[d, sl, 3, 64:128] = inc
            m[d, sl, 4, 0:64] = strict.T
            m[d, sl, 4, 64 + c] = 1.0
    return m


def make_shared(inp, L):
    m = hyena_consts(L)
    m["c_rwm"] = rwkv_masks()
    for nm in ["proj_a", "proj_b", "proj_c", "mix_w_out"]:
        m[nm] = np.ascontiguousarray(np.asarray(inp[nm], np.float32))
    mu = fm(inp["rw_mu"], 15)
    m["rw_mu"] = np.ascontiguousarray(np.transpose(mu, (0, 2, 1, 3)))
    m["rw_w2"] = np.ascontiguousarray(np.asarray(inp["rw_w2"], np.float32).reshape(DEPTH, 128, 512))
    m["rw_a2"] = np.ascontiguousarray(np.asarray(inp["rw_a2"], np.float32).reshape(DEPTH, 128, 512))
    m["rw_g2"] = np.ascontiguousarray(np.asarray(inp["rw_g2"], np.float32))
    w0 = np.asarray(inp["rw_w0"], np.float32)
    a0 = np.asarray(inp["rw_a0"], np.float32)
    vecs = np.stack([w0[:, 0], w0[:, 1], a0[:, 0], a0[:, 1], np.asarray(inp["rw_kk"], np.float32), np.asarray(inp["rw_ka"], np.float32),
                     np.asarray(inp["rw_rk"], np.float32).reshape(DEPTH, 512)], 1)
    m["rw_bc"] = np.ascontiguousarray(np.broadcast_to(vecs[:, None], (DEPTH, 128, 7, 512)))
    lnx = np.stack([np.asarray(inp["rw_lnx_w"], np.float32), np.asarray(inp["rw_lnx_b"], np.float32)], 1)
    m["rw_lnx"] = np.ascontiguousarray(np.broadcast_to(lnx[:, None], (DEPTH, 128, 2, 512)))
    f32 = lambda a: np.ascontiguousarray(np.asarray(a, dtype=np.float32))
    m["hy_f_w1"] = f32(inp["hy_f_w1"])
    m["hy_f_w2"] = f32(inp["hy_f_w2"])
    m["hy_f_w3"] = f32(inp["hy_f_w3"])
    m["hy_fb"] = f32(np.stack([np.asarray(inp["hy_f_b1"]), np.asarray(inp["hy_f_b2"])], -1))
    m["hy_freq"] = f32(np.swapaxes(np.asarray(inp["hy_freq"]), 1, 2))
    m["hy_bias"] = fm(inp["hy_bias"], 4)
    hcw = fm(inp["hy_conv_w"], 12)
    m["hy_cw"] = np.ascontiguousarray(np.transpose(hcw, (0, 2, 3, 1)).reshape(DEPTH, 128, 36))
    m["hy_cb"] = fm(inp["hy_conv_b"], 12)
    m["c_ident"] = np.eye(128, dtype=np.float32)
    m["c_masks"] = const_masks()
    m["ada_w"] = f32(inp["ada_w"])
    m["ada_b"] = fm(inp["ada_b"], 72)
    lw = fm(np.asarray(inp["ln_w"]).reshape(DEPTH * 3, D), 8)
    lb = fm(np.asarray(inp["ln_b"]).reshape(DEPTH * 3, D), 8)
    m["ln_w"] = np.ascontiguousarray(np.transpose(lw, (1, 0, 2)).reshape(128, -1))
    m["ln_b"] = np.ascontiguousarray(np.transpose(lb, (1, 0, 2)).reshape(128, -1))
    m["ffn_wi"] = f32(inp["ffn_wi"])
    m["ffn_wo"] = f32(inp["ffn_wo"])
    m["mix_w_in"] = f32(inp["mix_w_in"])
    cw = fm(inp["ssd_conv_w"], 8)
    m["ssd_cw"] = np.ascontiguousarray(np.transpose(cw, (0, 2, 3, 1)).reshape(DEPTH, 128, 40))
    m["ssd_cb"] = fm(inp["ssd_conv_b"], 8)
    m["ssd_dtb"] = bc128(np.asarray(inp["ssd_dt_bias"]).reshape(DEPTH, 16))
    m["ssd_alog"] = bc128(np.asarray(inp["ssd_A_log"]).reshape(DEPTH, 16))
    m["ssd_Dbc"] = bc128(np.repeat(np.asarray(inp["ssd_D"]), 64, axis=-1))
    m["ssd_nw"] = bc128(inp["ssd_norm_w"])
    return m


def make_inputs(inp, b, shared=None):
    m = dict(shared if shared is not None else make_shared(inp, inp['x'].shape[1]))
    m["x"] = np.ascontiguousarray(np.asarray(inp["x"][b], dtype=np.float32))
    m["c"] = fm(inp["c"][b], 8)
    return m


def kernel(**inputs):
    inp = {k: np.asarray(v) for k, v in inputs.items()}
    B, L, _ = inp["x"].shape
    C = build(L)
    shared = make_shared(inp, L)
    in_maps = [make_inputs(inp, b, shared) for b in range(B)]
    res = run_bass_kernel_spmd(C.nc, in_maps, core_ids=list(range(B)))
    return np.stack([r["out"] for r in res.results], axis=0).astype(np.float32)
```
